# Optimizing a Trainium2 kernel written in Bass

```python
import math
import jax
import jax.numpy as jnp
from jax import lax
import numpy as np

D_MODEL = 2048
BATCH = 2
SEQ = 16384
DEPTH = 2
DEC_BATCH = 32
DEC_SEQ = 64
PAST_LEN = 4096

CHUNK = 64
N_EVEN = (DEPTH + 1) // 2
N_ODD = DEPTH // 2
D_FF = 5632
ALPHA = (2 * DEPTH) ** 0.25
BETA = (8 * DEPTH) ** -0.25
LN_EPS = 1e-5
RMS_EPS = 1e-6
POOL_WINDOWS = (2, 4, 8, 16)
POOL_GROUPS = len(POOL_WINDOWS)
POOL_CH = 384
POOL_WIDTH = POOL_GROUPS * POOL_CH
POOL_HIST = max(POOL_WINDOWS) - 1
SSM_WIDTH = D_MODEL - POOL_WIDTH
SSM_GROUP_CH = 16
SSM_GROUPS = SSM_WIDTH // SSM_GROUP_CH
SSM_STATE = 64
DT_MIN = 0.001
DT_MAX = 0.1
MLA_HEADS = 16
Q_LORA = 512
KV_LORA = 256
QK_NOPE = 64
QK_ROPE = 32
V_HEAD = 64
MLA_WIDTH = MLA_HEADS * V_HEAD
MLA_IN = Q_LORA + KV_LORA + QK_ROPE
ATTN_SCALE = (QK_NOPE + QK_ROPE) ** -0.5
ROPE_THETA = 10000.0
Q_BLOCK = 128
SG_CHUNK = 128
SG_GROUPS = 8
SG_WIDTH = D_MODEL - MLA_WIDTH
SG_CH = SG_WIDTH // SG_GROUPS
ODD_IN = MLA_IN + 2 * SG_WIDTH
NEG_INF = -1e30

kernel_name = 'hybrid_streaming_encoder_step'


def layer_norm(x, g, b):
    xf = x.astype(jnp.float32)
    mu = jnp.mean(xf, axis=-1, keepdims=True)
    var = jnp.mean(jnp.square(xf - mu), axis=-1, keepdims=True)
    return ((xf - mu) * lax.rsqrt(var + LN_EPS)).astype(x.dtype) * g + b


def rms_norm(x, g):
    xf = x.astype(jnp.float32)
    return (xf * lax.rsqrt(jnp.mean(xf * xf, axis=-1, keepdims=True) + RMS_EPS)).astype(x.dtype) * g


def swiglu_ffn(x, w1, w3, w2):
    return (jax.nn.silu(x @ w1) * (x @ w3)) @ w2


def rotary(x, pos):
    half = QK_ROPE // 2
    inv = ROPE_THETA ** (-jnp.arange(half, dtype=jnp.float32) / half)
    ang = pos.astype(jnp.float32)[:, None] * inv[None, :]
    shape = (1, x.shape[1]) + (1,) * (x.ndim - 3) + (half,)
    cos = jnp.cos(ang).reshape(shape)
    sin = jnp.sin(ang).reshape(shape)
    xf = x.astype(jnp.float32)
    x1, x2 = xf[..., :half], xf[..., half:]
    return jnp.concatenate([x1 * cos - x2 * sin, x2 * cos + x1 * sin], axis=-1).astype(x.dtype)


def pool_mixer(u, hist, pos0, w_pool, pool_scale):
    nb, s, _ = u.shape
    full = jnp.concatenate([hist, u], axis=1)
    cs = jnp.cumsum(full.astype(jnp.float32), axis=1)
    cs = jnp.concatenate([jnp.zeros((nb, 1, POOL_WIDTH), jnp.float32), cs], axis=1)
    t = jnp.arange(s)
    means = []
    for g, w in enumerate(POOL_WINDOWS):
        sl = slice(g * POOL_CH, (g + 1) * POOL_CH)
        hi = cs[:, POOL_HIST + 1:POOL_HIST + 1 + s, sl]
        lo = cs[:, POOL_HIST + 1 - w:POOL_HIST + 1 - w + s, sl]
        cnt = jnp.minimum(pos0 + t + 1, w).astype(jnp.float32)
        means.append((hi - lo) / cnt[None, :, None])
    mean = jnp.stack(means, axis=2).astype(u.dtype)
    d = mean - u.reshape(nb, s, POOL_GROUPS, POOL_CH)
    out = jnp.einsum('bsgc,gcd->bsgd', d, w_pool) * pool_scale
    return out.reshape(nb, s, POOL_WIDTH), full[:, -POOL_HIST:]


def _ssm_combine(e1, e2):
    ar, ai, br, bi = e1
    cr, ci, dr, di = e2
    return (cr * ar - ci * ai, cr * ai + ci * ar,
            cr * br - ci * bi + dr, cr * bi + ci * br + di)


def s5_mixer(u, h_re, h_im, lam_re, lam_im, log_dt, b_re, b_im, c_re, c_im, d_skip, w_glu, b_glu):
    nb, s, _ = u.shape
    f32 = jnp.float32
    uf = u.astype(f32).reshape(nb, s, SSM_GROUPS, SSM_GROUP_CH)
    dt = jnp.exp(log_dt.astype(f32))[:, None]
    lr = lam_re.astype(f32)
    li = lam_im.astype(f32)
    mag = jnp.exp(lr * dt)
    ab_re = mag * jnp.cos(li * dt)
    ab_im = mag * jnp.sin(li * dt)
    den = lr * lr + li * li
    nr = ab_re - 1.0
    co_re = (nr * lr + ab_im * li) / den
    co_im = (ab_im * lr - nr * li) / den
    br = b_re.astype(f32)
    bi = b_im.astype(f32)
    bb_re = co_re[..., None] * br - co_im[..., None] * bi
    bb_im = co_re[..., None] * bi + co_im[..., None] * br
    bu_re = jnp.einsum('bsgc,gpc->bsgp', uf, bb_re)
    bu_im = jnp.einsum('bsgc,gpc->bsgp', uf, bb_im)
    if h_re is not None:
        hr = h_re.astype(f32)
        hi = h_im.astype(f32)
        bu_re = bu_re.at[:, 0].add(ab_re * hr - ab_im * hi)
        bu_im = bu_im.at[:, 0].add(ab_re * hi + ab_im * hr)
    a_re = jnp.broadcast_to(ab_re, bu_re.shape)
    a_im = jnp.broadcast_to(ab_im, bu_im.shape)
    _, _, x_re, x_im = lax.associative_scan(_ssm_combine, (a_re, a_im, bu_re, bu_im), axis=1)
    y = (jnp.einsum('bsgp,gcp->bsgc', x_re, c_re.astype(f32))
         - jnp.einsum('bsgp,gcp->bsgc', x_im, c_im.astype(f32))
         + d_skip.astype(f32).reshape(SSM_GROUPS, SSM_GROUP_CH) * uf)
    y = y.reshape(nb, s, SSM_WIDTH).astype(u.dtype)
    g = jax.nn.gelu(y)
    out = g * jax.nn.sigmoid(g @ w_glu + b_glu)
    return out, x_re[:, -1].astype(u.dtype), x_im[:, -1].astype(u.dtype)


def mla_project(z, pos, g_q, g_kv, w_uq):
    cq = z[..., :Q_LORA]
    ckv = z[..., Q_LORA:Q_LORA + KV_LORA]
    kpe = z[..., Q_LORA + KV_LORA:MLA_IN]
    q = jnp.einsum('bsr,rhd->bshd', rms_norm(cq, g_q), w_uq)
    q_nope = q[..., :QK_NOPE]
    q_pe = rotary(q[..., QK_NOPE:], pos)
    c_kv = rms_norm(ckv, g_kv)
    k_pe = rotary(kpe, pos)
    return q_nope, q_pe, c_kv, k_pe


def mla_attend_prompt(q_nope, q_pe, c_kv, k_pe, w_uk, w_uv):
    nb, s = q_nope.shape[:2]
    k_nope = jnp.einsum('btr,rhd->bthd', c_kv, w_uk)
    v = jnp.einsum('btr,rhd->bthd', c_kv, w_uv)
    key_chunk = jnp.arange(s) // CHUNK
    nblk = s // Q_BLOCK
    qn = q_nope.reshape(nb, nblk, Q_BLOCK, MLA_HEADS, QK_NOPE).transpose(1, 0, 2, 3, 4)
    qp = q_pe.reshape(nb, nblk, Q_BLOCK, MLA_HEADS, QK_ROPE).transpose(1, 0, 2, 3, 4)

    def block(args):
        qn_b, qp_b, i = args
        sc = (jnp.einsum('bqhd,bthd->bhqt', qn_b, k_nope)
              + jnp.einsum('bqhd,btd->bhqt', qp_b, k_pe)).astype(jnp.float32) * ATTN_SCALE
        q_chunk = (i * Q_BLOCK + jnp.arange(Q_BLOCK)) // CHUNK
        mask = key_chunk[None, :] <= q_chunk[:, None]
        p = jax.nn.softmax(jnp.where(mask, sc, NEG_INF), axis=-1).astype(v.dtype)
        return jnp.einsum('bhqt,bthd->bqhd', p, v)

    out = lax.map(block, (qn, qp, jnp.arange(nblk)))
    return out.transpose(1, 0, 2, 3, 4).reshape(nb, s, MLA_WIDTH)


def mla_attend_sample(q_nope, q_pe, c_all, kpe_all, q_pos, k_pos, w_uk, w_uv):
    nb, s = q_nope.shape[:2]
    q_lat = jnp.einsum('bshd,rhd->bshr', q_nope, w_uk)
    sc = (jnp.einsum('bshr,btr->bhst', q_lat, c_all)
          + jnp.einsum('bshd,btd->bhst', q_pe, kpe_all)).astype(jnp.float32) * ATTN_SCALE
    mask = (k_pos // CHUNK)[None, :] <= (q_pos // CHUNK)[:, None]
    p = jax.nn.softmax(jnp.where(mask, sc, NEG_INF), axis=-1).astype(c_all.dtype)
    o_lat = jnp.einsum('bhst,btr->bshr', p, c_all)
    out = jnp.einsum('bshr,rhd->bshd', o_lat, w_uv)
    return out.reshape(nb, s, MLA_WIDTH)


def sgu_mixer(z, g_v, b_v, w_s, b_s):
    nb, s, _ = z.shape
    u = z[..., :SG_WIDTH]
    v = layer_norm(z[..., SG_WIDTH:], g_v, b_v)
    L = min(s, SG_CHUNK)
    vc = v.reshape(nb, s // L, L, SG_GROUPS, SG_CH)
    w = w_s[:, :L, :L] * jnp.tril(jnp.ones((L, L), w_s.dtype))
    mixed = jnp.einsum('gts,bnsgc->bntgc', w, vc) + b_s[:, :L].T[None, None, :, :, None]
    return u * mixed.reshape(nb, s, SG_WIDTH), v


def trunk(x, past_len, pool_hist, ssm_re, ssm_im, ckv_cache, kpe_cache, P):
    nb, s, _ = x.shape
    pos = past_len + jnp.arange(s, dtype=jnp.int32)
    pools, sres, sims, ckvs, kpes, sgvs = [], [], [], [], [], []
    for layer in range(DEPTH):
        ffn1 = swiglu_ffn(x, P['ffn1_w1'][layer], P['ffn1_w3'][layer], P['ffn1_w2'][layer])
        x = layer_norm(ALPHA * x + 0.5 * ffn1, P['ln_g'][layer, 0], P['ln_b'][layer, 0])
        i = layer // 2
        if layer % 2 == 0:
            z = x @ P['w_in_e'][i]
            hist = jnp.zeros((nb, POOL_HIST, POOL_WIDTH), x.dtype) if pool_hist is None else pool_hist[i]
            a_out, new_hist = pool_mixer(z[..., :POOL_WIDTH], hist, past_len, P['pool_w'][i], P['pool_scale'][i])
            h_re = None if ssm_re is None else ssm_re[i]
            h_im = None if ssm_im is None else ssm_im[i]
            b_out, s_re, s_im = s5_mixer(z[..., POOL_WIDTH:], h_re, h_im, P['ssm_lam_re'][i], P['ssm_lam_im'][i],
                                         P['ssm_log_dt'][i], P['ssm_b_re'][i], P['ssm_b_im'][i], P['ssm_c_re'][i],
                                         P['ssm_c_im'][i], P['ssm_d'][i], P['ssm_w_glu'][i], P['ssm_b_glu'][i])
            mix = jnp.concatenate([a_out, b_out], axis=-1) @ P['w_out_e'][i]
            pools.append(new_hist)
            sres.append(s_re)
            sims.append(s_im)
        else:
            z = x @ P['w_in_o'][i]
            q_nope, q_pe, c_kv, k_pe = mla_project(z, pos, P['mla_g_q'][i], P['mla_g_kv'][i], P['mla_w_uq'][i])
            if ckv_cache is None:
                att = mla_attend_prompt(q_nope, q_pe, c_kv, k_pe, P['mla_w_uk'][i], P['mla_w_uv'][i])
            else:
                c_all = jnp.concatenate([ckv_cache[i], c_kv], axis=1)
                kpe_all = jnp.concatenate([kpe_cache[i], k_pe], axis=1)
                k_pos = jnp.arange(c_all.shape[1], dtype=jnp.int32)
                att = mla_attend_sample(q_nope, q_pe, c_all, kpe_all, pos, k_pos, P['mla_w_uk'][i], P['mla_w_uv'][i])
            sg_out, v_rows = sgu_mixer(z[..., MLA_IN:], P['sg_g_v'][i], P['sg_b_v'][i], P['sg_w_s'][i], P['sg_b_s'][i])
            mix = jnp.concatenate([att, sg_out], axis=-1) @ P['w_out_o'][i]
            ckvs.append(c_kv)
            kpes.append(k_pe)
            sgvs.append(v_rows)
        x = layer_norm(ALPHA * x + mix, P['ln_g'][layer, 1], P['ln_b'][layer, 1])
        ffn2 = swiglu_ffn(x, P['ffn2_w1'][layer], P['ffn2_w3'][layer], P['ffn2_w2'][layer])
        x = layer_norm(ALPHA * x + 0.5 * ffn2, P['ln_g'][layer, 2], P['ln_b'][layer, 2])
    return (x, jnp.stack(pools), jnp.stack(sres), jnp.stack(sims),
            jnp.stack(ckvs), jnp.stack(kpes), jnp.stack(sgvs))


def setup_inputs(seed: int = 0) -> dict:
    key = jax.random.key(seed)
    ks = iter(jax.random.split(key, 64))
    f32 = jnp.float32

    def nrm(shape, scale):
        return jax.random.normal(next(ks), shape, f32) * scale

    n_idx = jnp.arange(SSM_STATE, dtype=f32)
    inp = {}
    inp['x_prompt'] = nrm((BATCH, SEQ, D_MODEL), 1.0)
    inp['x_sample'] = nrm((DEC_BATCH, DEC_SEQ, D_MODEL), 1.0)
    inp['cache_pool'] = nrm((N_EVEN, DEC_BATCH, POOL_HIST, POOL_WIDTH), 1.0)
    inp['state_ssm_re'] = nrm((N_EVEN, DEC_BATCH, SSM_GROUPS, SSM_STATE), 0.1)
    inp['state_ssm_im'] = nrm((N_EVEN, DEC_BATCH, SSM_GROUPS, SSM_STATE), 0.1)
    inp['cache_ckv'] = nrm((N_ODD, DEC_BATCH, PAST_LEN, KV_LORA), 1.0)
    inp['cache_kpe'] = nrm((N_ODD, DEC_BATCH, PAST_LEN, QK_ROPE), 1.0)
    inp['ln_g'] = 1.0 + nrm((DEPTH, 3, D_MODEL), 0.02)
    inp['ln_b'] = nrm((DEPTH, 3, D_MODEL), 0.02)
    inp['ffn1_w1'] = nrm((DEPTH, D_MODEL, D_FF), D_MODEL ** -0.5)
    inp['ffn1_w3'] = nrm((DEPTH, D_MODEL, D_FF), D_MODEL ** -0.5)
    inp['ffn1_w2'] = nrm((DEPTH, D_FF, D_MODEL), BETA * D_FF ** -0.5)
    inp['ffn2_w1'] = nrm((DEPTH, D_MODEL, D_FF), D_MODEL ** -0.5)
    inp['ffn2_w3'] = nrm((DEPTH, D_MODEL, D_FF), D_MODEL ** -0.5)
    inp['ffn2_w2'] = nrm((DEPTH, D_FF, D_MODEL), BETA * D_FF ** -0.5)
    inp['w_in_e'] = nrm((N_EVEN, D_MODEL, POOL_WIDTH + SSM_WIDTH), D_MODEL ** -0.5)
    inp['pool_w'] = nrm((N_EVEN, POOL_GROUPS, POOL_CH, POOL_CH), POOL_CH ** -0.5)
    inp['pool_scale'] = 1.0 + nrm((N_EVEN, POOL_GROUPS, POOL_CH), 0.02)
    inp['ssm_lam_re'] = -0.5 + nrm((N_EVEN, SSM_GROUPS, SSM_STATE), 0.01)
    inp['ssm_lam_im'] = math.pi * n_idx + nrm((N_EVEN, SSM_GROUPS, SSM_STATE), 0.01)
    inp['ssm_log_dt'] = jax.random.uniform(next(ks), (N_EVEN, SSM_GROUPS), f32,
                                           minval=math.log(DT_MIN), maxval=math.log(DT_MAX))
    inp['ssm_b_re'] = nrm((N_EVEN, SSM_GROUPS, SSM_STATE, SSM_GROUP_CH), (2 * SSM_GROUP_CH) ** -0.5)
    inp['ssm_b_im'] = nrm((N_EVEN, SSM_GROUPS, SSM_STATE, SSM_GROUP_CH), (2 * SSM_GROUP_CH) ** -0.5)
    inp['ssm_c_re'] = nrm((N_EVEN, SSM_GROUPS, SSM_GROUP_CH, SSM_STATE), (2 * SSM_STATE) ** -0.5)
    inp['ssm_c_im'] = nrm((N_EVEN, SSM_GROUPS, SSM_GROUP_CH, SSM_STATE), (2 * SSM_STATE) ** -0.5)
    inp['ssm_d'] = nrm((N_EVEN, SSM_WIDTH), 1.0)
    inp['ssm_w_glu'] = nrm((N_EVEN, SSM_WIDTH, SSM_WIDTH), SSM_WIDTH ** -0.5)
    inp['ssm_b_glu'] = nrm((N_EVEN, SSM_WIDTH), 0.02)
    inp['w_out_e'] = nrm((N_EVEN, POOL_WIDTH + SSM_WIDTH, D_MODEL), BETA * D_MODEL ** -0.5)
    inp['w_in_o'] = nrm((N_ODD, D_MODEL, ODD_IN), D_MODEL ** -0.5)
    inp['mla_g_q'] = 1.0 + nrm((N_ODD, Q_LORA), 0.02)
    inp['mla_g_kv'] = 1.0 + nrm((N_ODD, KV_LORA), 0.02)
    inp['mla_w_uq'] = nrm((N_ODD, Q_LORA, MLA_HEADS, QK_NOPE + QK_ROPE), Q_LORA ** -0.5)
    inp['mla_w_uk'] = nrm((N_ODD, KV_LORA, MLA_HEADS, QK_NOPE), KV_LORA ** -0.5)
    inp['mla_w_uv'] = nrm((N_ODD, KV_LORA, MLA_HEADS, V_HEAD), BETA * KV_LORA ** -0.5)
    inp['sg_g_v'] = 1.0 + nrm((N_ODD, SG_WIDTH), 0.02)
    inp['sg_b_v'] = nrm((N_ODD, SG_WIDTH), 0.02)
    inp['sg_w_s'] = nrm((N_ODD, SG_GROUPS, SG_CHUNK, SG_CHUNK), 0.5 * SG_CHUNK ** -0.5)
    inp['sg_b_s'] = 1.0 + nrm((N_ODD, SG_GROUPS, SG_CHUNK), 0.02)
    inp['w_out_o'] = nrm((N_ODD, MLA_WIDTH + SG_WIDTH, D_MODEL), BETA * D_MODEL ** -0.5)
    return inp


def reference(x_prompt, x_sample, cache_pool, state_ssm_re, state_ssm_im, cache_ckv, cache_kpe,
              ln_g, ln_b, ffn1_w1, ffn1_w3, ffn1_w2, ffn2_w1, ffn2_w3, ffn2_w2,
              w_in_e, pool_w, pool_scale, ssm_lam_re, ssm_lam_im, ssm_log_dt, ssm_b_re, ssm_b_im,
              ssm_c_re, ssm_c_im, ssm_d, ssm_w_glu, ssm_b_glu, w_out_e,
              w_in_o, mla_g_q, mla_g_kv, mla_w_uq, mla_w_uk, mla_w_uv,
              sg_g_v, sg_b_v, sg_w_s, sg_b_s, w_out_o):
    P = dict(ln_g=ln_g, ln_b=ln_b, ffn1_w1=ffn1_w1, ffn1_w3=ffn1_w3, ffn1_w2=ffn1_w2,
             ffn2_w1=ffn2_w1, ffn2_w3=ffn2_w3, ffn2_w2=ffn2_w2,
             w_in_e=w_in_e, pool_w=pool_w, pool_scale=pool_scale, ssm_lam_re=ssm_lam_re,
             ssm_lam_im=ssm_lam_im, ssm_log_dt=ssm_log_dt, ssm_b_re=ssm_b_re, ssm_b_im=ssm_b_im,
             ssm_c_re=ssm_c_re, ssm_c_im=ssm_c_im, ssm_d=ssm_d, ssm_w_glu=ssm_w_glu,
             ssm_b_glu=ssm_b_glu, w_out_e=w_out_e,
             w_in_o=w_in_o, mla_g_q=mla_g_q, mla_g_kv=mla_g_kv, mla_w_uq=mla_w_uq,
             mla_w_uk=mla_w_uk, mla_w_uv=mla_w_uv, sg_g_v=sg_g_v, sg_b_v=sg_b_v,
             sg_w_s=sg_w_s, sg_b_s=sg_b_s, w_out_o=w_out_o)
    y_prompt, pool_p, sre_p, sim_p, ckv_p, kpe_p, _ = trunk(
        x_prompt, 0, None, None, None, None, None, P)
    past_len = cache_ckv.shape[2]
    y_sample, pool_s, sre_s, sim_s, ckv_s, kpe_s, sgv_s = trunk(
        x_sample, past_len, cache_pool, state_ssm_re, state_ssm_im, cache_ckv, cache_kpe, P)
    return (y_prompt, y_sample, pool_p, sre_p, sim_p, ckv_p, kpe_p,
            pool_s, sre_s, sim_s, ckv_s, kpe_s, sgv_s)
```

```python
import math
import os
import numpy as np
import ml_dtypes
import concourse.bass as bass
import concourse.mybir as mybir
from concourse.bass_utils import run_bass_kernel_spmd

F32 = mybir.dt.float32
BF16 = mybir.dt.bfloat16
AF = mybir.ActivationFunctionType
ALU = mybir.AluOpType

D = 2048
KT = 16
DFF = 5632
FT = 44
DEPTH = 2
ALPHA = (2 * DEPTH) ** 0.25
LN_EPS = 1e-5
RMS_EPS = 1e-6
EPS_LN = LN_EPS / (ALPHA * ALPHA)
ATTN_SCALE = 96 ** -0.5
NEGB = -30000.0
SSM_L = 64
NSEQ = 4
SL = 64


class Buf:
    __slots__ = ("name", "w", "r", "sem", "cnt")

    def __init__(self, name):
        self.name = name
        self.w = None
        self.r = []
        self.sem = None
        self.cnt = 0


class Ctx:
    def __init__(self, nc):
        self.nc = nc
        self.stack = []
        self.semstack = []
        self.eng = ["pe", "act", "dve", "pool", "sp"]
        self.esem = {}
        self.ecnt = {}
        self.seen = {k: {} for k in self.eng}
        self.prog = {k: [] for k in self.eng}
        self.dmabufs = []
        for k in self.eng:
            self.esem[k] = self.enter_sem(nc.semaphore("es_" + k))
            self.ecnt[k] = 0

    def enter(self, cm):
        v = cm.__enter__()
        self.stack.append(cm)
        return v

    def enter_sem(self, cm):
        v = cm.__enter__()
        self.semstack.append(cm)
        return v

    def mark(self):
        return len(self.stack)

    def release(self, m):
        while len(self.stack) > m:
            self.stack.pop().__exit__(None, None, None)

    def uq(self, name):
        self.uid = getattr(self, "uid", 0) + 1
        return f"{name}_{self.uid}"

    def sbuf(self, name, shape, dt):
        return self.enter(self.nc.sbuf_tensor(self.uq(name), shape, dt))

    def psum(self, name, shape, dt=F32):
        return self.enter(self.nc.psum_tensor(name, shape, dt))

    def _deps(self, e, reads, writes):
        toks = []
        for b in reads:
            if b.w is not None:
                toks.append(b.w)
        for b in writes:
            if b.w is not None:
                toks.append(b.w)
            toks.extend(b.r)
        seen = self.seen[e]
        need = {}
        for (s, v) in toks:
            key = id(s)
            if seen.get(key, 0) >= v:
                continue
            if key not in need or need[key][1] < v:
                need[key] = (s, v)
        for key, (s, v) in need.items():
            seen[key] = v
            self.prog[e].append(("wait", s, v))

    def _post(self, tok, reads, writes):
        for b in writes:
            b.w = tok
            b.r = []
        for b in reads:
            if b in writes:
                continue
            b.r.append(tok)
            if len(b.r) > 16:
                best = {}
                for (s, v) in b.r:
                    k = id(s)
                    if k not in best or best[k][1] < v:
                        best[k] = (s, v)
                b.r = list(best.values())

    def op(self, e, fn, reads=(), writes=(), inc=True, wnodep=()):
        self._deps(e, reads, writes)
        writes = list(writes) + list(wnodep)
        if inc:
            if self.ecnt[e] >= 30000:
                self.esem[e] = self.enter_sem(self.nc.semaphore(self.uq("es_" + e)))
                self.ecnt[e] = 0
            self.ecnt[e] += 1
            tok = (self.esem[e], self.ecnt[e])
            self.prog[e].append(("op", fn, self.esem[e], 1))
        else:
            if self.ecnt[e] >= 30000:
                self.esem[e] = self.enter_sem(self.nc.semaphore(self.uq("es_" + e)))
                self.ecnt[e] = 0
            tok = (self.esem[e], self.ecnt[e] + 1)
            self.prog[e].append(("op", fn, None, 0))
        self._post(tok, reads, writes)
        return tok

    def dma(self, e, out, in_, reads=(), writes=(), sembuf=None):
        self._deps(e, reads, writes)
        b = sembuf
        if b.sem is None or b.cnt >= 30000:
            b.sem = self.enter_sem(self.nc.semaphore(self.uq("ds_" + b.name)))
            b.cnt = 0
            if b not in self.dmabufs:
                self.dmabufs.append(b)
        b.cnt += 16
        tok = (b.sem, b.cnt)
        self.prog[e].append(("op", (lambda eng, o=out, i=in_: eng.dma_start(out=o, in_=i)), b.sem, 16))
        self._post(tok, reads, writes)
        return tok

    def collective(self, ins, outs, groups, reads, writes, sembuf):
        e = "pool"
        self._deps(e, reads, writes)
        b = sembuf
        b.sem = self.enter_sem(self.nc.semaphore(self.uq("cs_" + b.name)))
        b.cnt = 1
        tok = (b.sem, 1)
        self.prog[e].append(("op", (lambda eng: eng.collective_compute(
            "AllGather", ALU.bypass, replica_groups=groups, ins=[ins], outs=[outs])), b.sem, 1))
        self._post(tok, reads, writes)

    def barrier(self):
        toks = [(self.esem[k], self.ecnt[k]) for k in self.eng if self.ecnt[k] > 0]
        toks += [(b.sem, b.cnt) for b in self.dmabufs]
        for e in self.eng:
            for (s, v) in toks:
                if self.seen[e].get(id(s), 0) < v:
                    self.seen[e][id(s)] = v
                    self.prog[e].append(("wait", s, v))

    def emit(self):
        nc = self.nc
        with nc.Block() as block:
            def mk(ename):
                def body(eng):
                    for it in self.prog[ename]:
                        if it[0] == "wait":
                            eng.wait_ge(it[1], it[2])
                        else:
                            ins = it[1](eng)
                            if it[2] is not None:
                                ins.then_inc(it[2], it[3])
                return body
            block.tensor(mk("pe"))
            block.scalar(mk("act"))
            block.vector(mk("dve"))
            block.gpsimd(mk("pool"))
            block.sync(mk("sp"))


class K:
    pass


def _dram(nc, name, shape, dt=F32, kind="ExternalInput"):
    if kind is None:
        return nc.dram_tensor(name, list(shape), dt).ap()
    return nc.dram_tensor(name, list(shape), dt, kind=kind).ap()


def build(TP, PAST, NB, phases=(1, 2, 3, 4), noffn=False):
    nc = bass.Bass("TRN2", target_bir_lowering=False)
    c = Ctx(nc)
    g = K()
    g.nc, g.c, g.TP, g.PAST, g.NB = nc, c, TP, PAST, NB
    g.noffn = noffn
    NT = TP + NSEQ * SL
    g.NT = NT
    assert TP % NB == 0 and NSEQ * SL <= NB and NB % SSM_L == 0
    blocks = [(i * NB, NB, "p") for i in range(TP // NB)] + [(TP, NSEQ * SL, "s")]
    g.blocks = blocks

    I = {}

    def inp(name, shape, dt=F32):
        I[name] = _dram(nc, name, shape, dt)
        return I[name]

    inp("xT", [D, NT])
    for l in range(2):
        for f in range(2):
            if not noffn:
                inp(f"w13_{l}{f}", [22, 128, KT * 512])
                inp(f"w2_{l}{f}", [16, 128, FT * 128])
    inp("lng", [128, 6 * KT])
    inp("lnb", [128, 6 * KT])
    inp("w_in_e", [4, 128, KT * 512])
    inp("w_out_e", [4, 128, KT * 512])
    inp("pool_w", [1, 128, 12 * 384])
    inp("pool_scale", [128, 12])
    inp("pool_corr", [128, 12 * 16])
    inp("cache_poolT", [1536, NSEQ, 15])
    inp("ssm_Bre", [128, 16 * 128])
    inp("ssm_Bim", [128, 16 * 128])
    inp("ssm_Cre", [128, 16 * 128])
    inp("ssm_Cim", [128, 16 * 128])
    inp("ssm_lre", [128, 16])
    inp("ssm_lim", [128, 16])
    inp("ssm_ldt", [128, 16])
    inp("ssm_d", [128, 4])
    inp("ssm_h0re", [128, 16 * NSEQ])
    inp("ssm_h0im", [128, 16 * NSEQ])
    inp("w_glu", [1, 128, 4 * 512])
    inp("b_glu", [128, 4])
    inp("meta", [128, 32])
    inp("w_in_o_cq", [1, 128, KT * 512])
    inp("w_in_o_kv", [1, 128, KT * 320])
    inp("w_in_o_u", [2, 128, KT * 512])
    inp("w_in_o_v", [2, 128, KT * 512])
    inp("g_q", [128, 4])
    inp("g_kv", [128, 2])
    inp("w_uq", [2, 128, 4 * 1024])
    inp("w_uk", [1, 128, 2 * 1024])
    inp("w_uv", [1, 128, 2 * 1024])
    inp("rope", [128, 2, NT])
    inp("sg_gv", [128, 1024])
    inp("sg_bv", [128, 1024])
    inp("sg_wT", [128, 8 * 128])
    inp("sg_wT64", [128, 8 * 64])
    inp("sg_bs", [128, 8 * 128])
    inp("sg_bs64", [128, 8 * 128])
    inp("tril", [128, 128])
    inp("dmask", [128, (NB // 128) * NB], BF16)
    inp("w_out_o_a", [4, 64, 16 * 512])
    inp("w_out_o_s", [4, 128, 8 * 512])
    inp("cache_ckvT", [NSEQ, 256, PAST])
    inp("cache_kpeT", [NSEQ, 32, PAST])

    O = {}

    def outp(name, shape):
        O[name] = _dram(nc, name, shape, F32, kind="ExternalOutput")
        return O[name]

    outp("yT", [D, NT])
    outp("poolT_p", [1536, 15])
    outp("poolT_s", [1536, NSEQ, 15])
    outp("ssm_p", [128, 2, 16])
    outp("ssm_s", [128, 2, 16 * NSEQ])
    outp("ckvT", [256, NT])
    outp("kpeT", [32, NT])
    outp("sgv", [NSEQ * SL, 1024])

    S = {}

    def scr(name, shape, dt=F32):
        S[name] = _dram(nc, name, shape, dt, kind=None)
        return S[name]

    scr("x1T", [D, NT])
    scr("uT", [D, NT])
    scr("x4T", [D, NT])
    scr("qT", [16, 96, NT], BF16)
    scr("sgoT", [1024, NT], BF16)
    scr("attT", [16, 64, NT], BF16)
    scr("g1_in", [1, 128 * 32 + 1536 * 15])
    scr("g1_out", [4, 128 * 32 + 1536 * 15])
    scr("skv", [288, NSEQ * SL], BF16)
    assert NB == 256
    g.NCH = TP // 256
    scr("g2_in", [g.NCH, 288, 256], BF16)
    scr("g2_out", [g.NCH, 4 * 288, 256], BF16)
    g.g2B = [Buf(f"g2c{i}") for i in range(g.NCH)]
    g.I, g.O, g.S = I, O, S
    g.DB = {k: Buf("d_" + k) for k in list(O) + list(S)}

    g.banks = [c.psum(f"bank{i}", [128, 512]) for i in range(8)]
    g.bankB = [Buf(f"bank{i}") for i in range(8)]
    g.bi = 0
    g.nrot = 8
    g.NW = 2
    g.wbuf = [c.sbuf(f"wbuf{i}", [128, 8192], BF16) for i in range(g.NW)]
    g.wB = [Buf(f"wbuf{i}") for i in range(g.NW)]
    g.wi = 0
    g.cst = c.sbuf("cst", [128, 6 * KT * 2 + 64], F32)
    g.cstB = Buf("cst")
    g.ones = c.sbuf("ones", [128, 128], BF16)
    g.onesB = Buf("ones")
    g.onesf = c.sbuf("onesf", [128, 128], F32)
    c.dma("sp", g.cst[:, 0:96], I["lng"], writes=[g.cstB], sembuf=g.cstB)
    c.dma("sp", g.cst[:, 96:192], I["lnb"], writes=[g.cstB], sembuf=g.cstB)
    c.dma("sp", g.cst[:, 192:224], I["meta"], writes=[g.cstB], sembuf=g.cstB)
    c.op("dve", lambda e: e.memset(g.onesf[:], 1.0), writes=[g.onesB])
    c.op("dve", lambda e: e.tensor_copy(out=g.ones[:], in_=g.onesf[:]), reads=[g.onesB], writes=[g.onesB])

    m0 = c.mark()
    if 1 in phases:
        phase1(g)
        c.barrier()
        c.release(m0)
    if 2 in phases:
        phase2(g)
        c.barrier()
        c.release(m0)
        phase2b(g)
        c.barrier()
        c.release(m0)
    if 3 in phases:
        phase3a(g)
        c.barrier()
        c.release(m0)
    if 4 in phases:
        phase3b(g)
    outs = [g.DB[k] for k in O]
    toks = []
    for b in outs:
        if b.w is not None:
            toks.append(b.w)
        toks.extend(b.r)
    for (s, v) in toks:
        if c.seen["sp"].get(id(s), 0) < v:
            c.seen["sp"][id(s)] = v
            c.prog["sp"].append(("wait", s, v))
    c.emit()
    c.release(0)
    while c.semstack:
        c.semstack.pop().__exit__(None, None, None)
    return nc


def bank(g):
    i = g.bi
    g.bi = (g.bi + 1) % g.nrot
    return g.banks[i], g.bankB[i]


def loadw(g, dram_ap, rows=128):
    i = g.wi
    g.wi = (g.wi + 1) % g.NW
    X = dram_ap.shape[-1]
    t = g.wbuf[i]
    g.c.dma("pool", t[0:rows, 0:X], dram_ap, writes=[g.wB[i]], sembuf=g.wB[i])
    return t, g.wB[i]


def mm(g, out, lhsT, rhs, first, last, reads, wbuf):
    c = g.c
    fn = lambda e, o=out, l=lhsT, r=rhs, f=first, s=last: e.matmul(o, l, r, start=f, stop=s)
    if first:
        c.op("pe", fn, reads=reads, writes=[wbuf], inc=last)
    else:
        c.op("pe", fn, reads=reads, wnodep=[wbuf], inc=last)


class ActT:
    def __init__(self, g, name, nk, n, dt):
        self.t = g.c.sbuf(name, [128, nk, n], dt)
        self.B = [Buf(f"{name}{k}") for k in range(nk)]
        self.nk = nk


def load_fm(g, a, dram, c0, n, k0=0, nk=None, col_off=0):
    nk = a.nk if nk is None else nk
    src = dram.rearrange("(k p) t -> p k t", p=128)[:, :, c0:c0 + n]
    g.c.dma("sp", a.t[:, k0:k0 + nk, col_off:col_off + n], src, writes=a.B[k0:k0 + nk], sembuf=a.B[k0])


def store_fm(g, a, dname, c0, n, k0=0, nk=None, r0=0):
    nk = a.nk if nk is None else nk
    dram = g.O[dname] if dname in g.O else g.S[dname]
    dst = dram[r0:r0 + nk * 128, :].rearrange("(k p) t -> p k t", p=128)[:, :, c0:c0 + n]
    g.c.dma("sp", dst, a.t[:, k0:k0 + nk, 0:n], reads=a.B[k0:k0 + nk], writes=[g.DB[dname]], sembuf=g.DB[dname])


def linear(g, wname, nch, kt, mc, rhs, rhsB, n, evac, mw=128, krows=128, wrows=128):
    wd = g.I[wname]
    for ch in range(nch):
        wt, wB = loadw(g, wd[ch], rows=wrows)
        wv = wt[:, 0:kt * mc].rearrange("p (k m) -> p k m", k=kt)
        for mi in range(mc // mw):
            bk, bB = bank(g)
            for k in range(kt):
                mm(g, bk[0:mw, 0:n], wv[0:krows, k, mi * mw:(mi + 1) * mw], rhs(k), k == 0, k == kt - 1,
                   [wB, rhsB[k]], bB)
            evac(ch * (mc // mw) + mi, bk, bB)


def cview(g, off, w):
    return g.cst[:, off:off + w]


def ffn(g, l, f, xf, xb, hb, gt, n):
    c = g.c
    if g.noffn:
        return
    gtB = g.gtB
    wd13 = g.I[f"w13_{l}{f}"]
    for ch in range(22):
        wt, wB = loadw(g, wd13[ch])
        wv = wt[:, 0:KT * 512].rearrange("p (k m) -> p k m", k=KT)
        for mi in range(2):
            ba, bAB = bank(g)
            for k in range(KT):
                mm(g, ba[:, 0:n], wv[:, k, mi * 128:(mi + 1) * 128], xb.t[:, k, 0:n], k == 0, k == KT - 1,
                   [wB, xb.B[k]], bAB)
            bb, bBB = bank(g)
            for k in range(KT):
                mm(g, bb[:, 0:n], wv[:, k, 256 + mi * 128:256 + (mi + 1) * 128], xb.t[:, k, 0:n], k == 0,
                   k == KT - 1, [wB, xb.B[k]], bBB)
            gi = g.gti
            g.gti = (g.gti + 1) % 2
            c.op("act", lambda e, o=gt[:, gi, 0:n], i=ba[:, 0:n]: e.activation(out=o, in_=i, func=AF.Silu),
                 reads=[bAB], writes=[gtB[gi]])
            m = ch * 2 + mi
            c.op("dve", lambda e, o=hb.t[:, m, 0:n], a=bb[:, 0:n], b=gt[:, gi, 0:n]: e.tensor_tensor(
                out=o, in0=a, in1=b, op=ALU.mult), reads=[bBB, gtB[gi]], writes=[hb.B[m]])
    wd2 = g.I[f"w2_{l}{f}"]
    for j in range(16):
        wt, wB = loadw(g, wd2[j])
        wv = wt[:, 0:FT * 128].rearrange("p (k m) -> p k m", k=FT)
        bk, bB = bank(g)
        for k in range(FT):
            mm(g, bk[:, 0:n], wv[:, k, :], hb.t[:, k, 0:n], k == 0, k == FT - 1, [wB, hb.B[k]], bB)
        c.op("dve", lambda e, o=xf.t[:, j, 0:n], a=bk[:, 0:n]: e.scalar_tensor_tensor(
            out=o, in0=a, scalar=0.5 / ALPHA, in1=o, op0=ALU.mult, op1=ALU.add), reads=[bB], writes=[xf.B[j]])


def colsum(g, src, srcB, nk, n, krows=128):
    bk, bB = bank(g)
    for k in range(nk):
        mm(g, bk[:, 0:n], g.ones[0:krows, :], src(k), k == 0, k == nk - 1, [g.onesB, srcB[k]], bB)
    return bk, bB


def layernorm(g, xf, xb, sq, st, n, gi):
    c = g.c
    stB = g.stB
    for k in range(KT):
        c.op("act", lambda e, o=xb.t[:, k, 0:n], i=xf.t[:, k, 0:n]: e.activation(out=o, in_=i, func=AF.Copy),
             reads=[xf.B[k]], writes=[xb.B[k]])
        c.op("act", lambda e, o=sq.t[:, k, 0:n], i=xf.t[:, k, 0:n]: e.activation(out=o, in_=i, func=AF.Square),
             reads=[xf.B[k]], writes=[sq.B[k]])
    s1, s1B = colsum(g, lambda k: xb.t[:, k, 0:n], xb.B, KT, n)
    s2, s2B = colsum(g, lambda k: sq.t[:, k, 0:n], sq.B, KT, n)
    mean, msq, var, rstd, nmr = [st[:, i, 0:n] for i in range(5)]
    c.op("act", lambda e: e.activation(out=mean, in_=s1[:, 0:n], func=AF.Copy, scale=1.0 / D), reads=[s1B], writes=[stB[0]])
    c.op("dve", lambda e: e.tensor_tensor(out=msq, in0=mean, in1=mean, op=ALU.mult), reads=[stB[0]], writes=[stB[1]])
    c.op("dve", lambda e: e.scalar_tensor_tensor(out=var, in0=s2[:, 0:n], scalar=1.0 / D, in1=msq, op0=ALU.mult,
                                                 op1=ALU.subtract), reads=[s2B, stB[1]], writes=[stB[2]])
    c.op("dve", lambda e: e.tensor_scalar(out=var, in0=var, scalar1=EPS_LN, scalar2=None, op0=ALU.add),
         reads=[stB[2]], writes=[stB[2]])
    c.op("act", lambda e: e.activation(out=var, in_=var, func=AF.Sqrt), reads=[stB[2]], writes=[stB[2]])
    c.op("dve", lambda e: e.reciprocal(out=rstd, in_=var), reads=[stB[2]], writes=[stB[3]])
    c.op("dve", lambda e: e.scalar_tensor_tensor(out=nmr, in0=mean, scalar=-1.0, in1=rstd, op0=ALU.mult, op1=ALU.mult),
         reads=[stB[0], stB[3]], writes=[stB[4]])
    for k in range(KT):
        c.op("dve", lambda e, o=xf.t[:, k, 0:n]: e.tensor_tensor(out=o, in0=o, in1=rstd, op=ALU.mult),
             reads=[stB[3]], writes=[xf.B[k]])
        c.op("dve", lambda e, o=xf.t[:, k, 0:n]: e.tensor_tensor(out=o, in0=o, in1=nmr, op=ALU.add),
             reads=[stB[4]], writes=[xf.B[k]])
        gs = cview(g, gi * KT + k, 1)
        bs = cview(g, 96 + gi * KT + k, 1)
        c.op("act", lambda e, o=xf.t[:, k, 0:n], s_=gs, b_=bs: e.activation(out=o, in_=o, func=AF.Identity, scale=s_, bias=b_),
             reads=[g.cstB], writes=[xf.B[k]])
        c.op("act", lambda e, o=xb.t[:, k, 0:n], i=xf.t[:, k, 0:n]: e.activation(out=o, in_=i, func=AF.Copy),
             reads=[xf.B[k]], writes=[xb.B[k]])


def alloc_main(g):
    c, NB = g.c, g.NB
    g.xf = ActT(g, "xf", KT, NB, F32)
    g.xb = ActT(g, "xb", KT, NB, BF16)
    g.hb = ActT(g, "hb", FT, NB, BF16)
    g.gt = c.sbuf("gt", [128, 2, NB], F32)
    g.gtB = [Buf("gt0"), Buf("gt1")]
    g.gti = 0
    g.st = c.sbuf("st", [128, 6, NB], F32)
    g.stB = [Buf(f"st{i}") for i in range(6)]


def ssm_setup(g):
    c, I = g.c, g.I
    sB = g.sB = Buf("ssmc")
    L = SSM_L
    g.sv = c.sbuf("ssm_sv", [128, 40, 16], F32)
    sv = g.sv
    V = lambda i: sv[:, i, :]
    LRE, LIM, LDT, DT, LRDT, LIDT, MAG, CTH, STH, ABR, ABI, DEN, NR, COR, COI, T0, T1, T2, PC, PS_, APR, API, NEIL, ERL, EIL, HPI = range(26)
    g.SVI = dict(CTH=CTH, STH=STH, COR=COR, COI=COI, APR=APR, API=API, NEIL=NEIL, ERL=ERL, EIL=EIL, MAG=MAG, T0=T0, T1=T1, T2=T2)
    c.dma("sp", V(LRE), I["ssm_lre"], writes=[sB], sembuf=sB)
    c.dma("sp", V(LIM), I["ssm_lim"], writes=[sB], sembuf=sB)
    c.dma("sp", V(LDT), I["ssm_ldt"], writes=[sB], sembuf=sB)
    d = lambda fn: c.op("dve", fn, reads=[sB], writes=[sB])
    a = lambda fn: c.op("act", fn, reads=[sB], writes=[sB])
    tt = lambda o, x, y, op: d(lambda e: e.tensor_tensor(out=o, in0=x, in1=y, op=op))
    d(lambda e: e.memset(V(HPI), math.pi / 2))
    a(lambda e: e.activation(out=V(DT), in_=V(LDT), func=AF.Exp))
    tt(V(LRDT), V(LRE), V(DT), ALU.mult)
    tt(V(LIDT), V(LIM), V(DT), ALU.mult)
    a(lambda e: e.activation(out=V(MAG), in_=V(LRDT), func=AF.Exp))
    a(lambda e: e.activation(out=V(STH), in_=V(LIDT), func=AF.Sin, scale=1.0 / 16))
    d(lambda e: e.scalar_tensor_tensor(out=V(T0), in0=V(LIDT), scalar=1.0 / 16, in1=V(HPI), op0=ALU.mult, op1=ALU.add))
    a(lambda e: e.activation(out=V(CTH), in_=V(T0), func=AF.Sin))

    def csq(cr, ci):
        tt(V(T0), cr, cr, ALU.mult)
        tt(V(T1), ci, ci, ALU.mult)
        tt(V(T2), cr, ci, ALU.mult)
        tt(cr, V(T0), V(T1), ALU.subtract)
        tt(ci, V(T2), V(T2), ALU.add)
    for _ in range(4):
        csq(V(CTH), V(STH))
    tt(V(ABR), V(MAG), V(CTH), ALU.mult)
    tt(V(ABI), V(MAG), V(STH), ALU.mult)
    tt(V(T0), V(LRE), V(LRE), ALU.mult)
    tt(V(T1), V(LIM), V(LIM), ALU.mult)
    tt(V(DEN), V(T0), V(T1), ALU.add)
    d(lambda e: e.reciprocal(out=V(DEN), in_=V(DEN)))
    d(lambda e: e.tensor_scalar(out=V(NR), in0=V(ABR), scalar1=-1.0, scalar2=None, op0=ALU.add))
    tt(V(T0), V(NR), V(LRE), ALU.mult)
    tt(V(T1), V(ABI), V(LIM), ALU.mult)
    tt(V(T0), V(T0), V(T1), ALU.add)
    tt(V(COR), V(T0), V(DEN), ALU.mult)
    tt(V(T0), V(ABI), V(LRE), ALU.mult)
    tt(V(T1), V(NR), V(LIM), ALU.mult)
    tt(V(T0), V(T0), V(T1), ALU.subtract)
    tt(V(COI), V(T0), V(DEN), ALU.mult)
    d(lambda e: e.tensor_copy(out=V(APR), in_=V(ABR)))
    d(lambda e: e.tensor_copy(out=V(API), in_=V(ABI)))
    for _ in range(int(round(math.log2(g.TP)))):
        csq(V(APR), V(API))
    g.Er = c.sbuf("ssm_Er", [128, 16, L + 1], F32)
    g.Ei = c.sbuf("ssm_Ei", [128, 16, L + 1], F32)
    g.TCr = c.sbuf("ssm_TCr", [128, 16, L], F32)
    g.TC2 = c.sbuf("ssm_TC2", [128, 16, 2, L], F32)
    g.RT = c.sbuf("ssm_RT", [128, 16, L], F32)
    g.tqf = c.sbuf("ssm_tq", [128, 2048], F32)
    g.tq = tq4 = g.tqf[:, 0:2048].rearrange("p (a b w) -> p a b w", a=4, b=16)
    tq2 = g.tqf[:, 0:2048].rearrange("p (a b w) -> p a b w", a=2, b=16)
    Er, Ei = g.Er, g.Ei
    tq = tq4
    d(lambda e: e.memset(Er[:, :, 0:1], 1.0))
    d(lambda e: e.memset(Ei[:, :, 0:1], 0.0))
    d(lambda e: e.tensor_copy(out=V(PC), in_=V(CTH)))
    d(lambda e: e.tensor_copy(out=V(PS_), in_=V(STH)))
    w = 1
    while w <= L:
        ww = min(w, L + 1 - w)
        pcb = V(PC).unsqueeze(2).broadcast_to([128, 16, ww])
        psb = V(PS_).unsqueeze(2).broadcast_to([128, 16, ww])
        tt(tq[:, 0, :, 0:ww], Er[:, :, 0:ww], pcb, ALU.mult)
        tt(tq[:, 1, :, 0:ww], Ei[:, :, 0:ww], psb, ALU.mult)
        tt(tq[:, 2, :, 0:ww], Er[:, :, 0:ww], psb, ALU.mult)
        tt(tq[:, 3, :, 0:ww], Ei[:, :, 0:ww], pcb, ALU.mult)
        tt(Er[:, :, w:w + ww], tq[:, 0, :, 0:ww], tq[:, 1, :, 0:ww], ALU.subtract)
        tt(Ei[:, :, w:w + ww], tq[:, 2, :, 0:ww], tq[:, 3, :, 0:ww], ALU.add)
        csq(V(PC), V(PS_))
        w *= 2
    corb = V(COR).unsqueeze(2).broadcast_to([128, 16, L])
    coib = V(COI).unsqueeze(2).broadcast_to([128, 16, L])
    tt(tq2[:, 0], Er[:, :, 0:L], corb, ALU.mult)
    tt(tq2[:, 1], Ei[:, :, 0:L], coib, ALU.mult)
    tt(g.TCr[:], tq2[:, 0], tq2[:, 1], ALU.add)
    tt(tq2[:, 0], Er[:, :, 0:L], coib, ALU.mult)
    tt(tq2[:, 1], Ei[:, :, 0:L], corb, ALU.mult)
    tt(g.TC2[:, :, 1, :], tq2[:, 0], tq2[:, 1], ALU.subtract)
    d(lambda e, t_=g.TC2: e.tensor_scalar(out=t_[:, :, 0, :], in0=t_[:, :, 1, :], scalar1=-1.0, scalar2=None, op0=ALU.mult))
    d(lambda e, t_=g.RT: e.tensor_copy(out=t_[:], in_=V(MAG).unsqueeze(2).broadcast_to([128, 16, L])))
    d(lambda e: e.tensor_copy(out=V(ERL), in_=Er[:, :, L]))
    d(lambda e: e.tensor_copy(out=V(EIL), in_=Ei[:, :, L]))
    d(lambda e: e.tensor_scalar(out=V(NEIL), in0=Ei[:, :, L], scalar1=-1.0, scalar2=None, op0=ALU.mult))
    g.BR = c.sbuf("ssm_BR", [128, 16, 128], BF16)
    g.BI = c.sbuf("ssm_BI", [128, 16, 128], BF16)
    g.CR = c.sbuf("ssm_CR", [128, 16, 128], BF16)
    g.nCR = c.sbuf("ssm_nCR", [128, 16, 128], BF16)
    g.nCI = c.sbuf("ssm_nCI", [128, 16, 128], BF16)
    stg = g.tqf[:, 0:2048]
    for nm, dst, sc in (("ssm_Bre", [(g.BR, 1.0)], 0), ("ssm_Bim", [(g.BI, 1.0)], 0),
                        ("ssm_Cre", [(g.CR, 1.0), (g.nCR, -1.0)], 0), ("ssm_Cim", [(g.nCI, -1.0)], 0)):
        c.dma("sp", stg, I[nm], reads=[sB], writes=[sB], sembuf=sB)
        for (dt_, s_) in dst:
            d(lambda e, o=dt_, s_=s_: e.tensor_scalar(out=o[:].rearrange("p a b -> p (a b)"), in0=stg, scalar1=s_,
                                                        scalar2=None, op0=ALU.mult))
    g.carry = c.sbuf("ssm_carry", [128, 2, 16], F32)
    g.send = c.sbuf("ssm_send", [128, 2, 16], F32)
    g.sfin = c.sbuf("ssm_fin", [128, 2, 16, NSEQ], F32)
    g.h0s = c.sbuf("ssm_h0s", [128, 2, 16, NSEQ], F32)
    g.swk = [c.sbuf(f"ssm_wk{i}", [128, 8, L], F32) for i in range(3)]
    g.swq = [c.sbuf(f"ssm_wq{i}", [128, 4, L], BF16) for i in range(3)]
    g.swB = [Buf(f"swk{i}") for i in range(3)]
    g.swi = 0
    g.ct = c.sbuf("ssm_ct", [128, 8], F32)


def cmul_sv(g, outr, outi, ar, ai, br, bi, t0, t1):
    c, sB = g.c, g.sB
    tt = lambda o, x, y, op: c.op("dve", lambda e: e.tensor_tensor(out=o, in0=x, in1=y, op=op), reads=[sB], writes=[sB])
    tt(t0, ar, br, ALU.mult)
    tt(t1, ai, bi, ALU.mult)
    tt(t0, t0, t1, ALU.subtract)
    tt(t1, ar, bi, ALU.mult)
    tt(outi, ai, br, ALU.mult)
    tt(outi, outi, t1, ALU.add)
    c.op("dve", lambda e: e.tensor_copy(out=outr, in_=t0), reads=[sB], writes=[sB])


def ssm_block(g, ub, us, yssm, n, kind, project):
    c, sB = g.c, g.sB
    L = SSM_L
    sv = g.sv
    SVI = g.SVI
    if kind == "p":
        pieces = [(p * L, L, None) for p in range(n // L)]
    else:
        pieces = [(s * SL, SL, s) for s in range(NSEQ)]
    for (pc0, ln, seq) in pieces:
        ybk = None
        for i in range(16):
            j = i // 4
            wi_ = g.swi
            g.swi = (g.swi + 1) % 3
            wk, wq, wB = g.swk[wi_], g.swq[wi_], g.swB[wi_]
            bk, bB = bank(g)
            bv = bk[:, 0:512].rearrange("p (a b) -> p a b", a=4)
            rhs = ub.t[:, j, pc0:pc0 + ln]
            for q, tab in enumerate((g.BR, g.BI, g.BI, g.BR)):
                c.op("pe", lambda e, o=bv[:, q, 0:ln], l=tab[:, i, :], r=rhs: e.matmul(o, l, r, start=True, stop=True),
                     reads=[sB, ub.B[j]], writes=[bB] if q == 0 else [], wnodep=[] if q == 0 else [bB], inc=(q == 3))
            tcr = g.TCr[:, i, 0:ln].unsqueeze(1).broadcast_to([128, 2, ln])
            c.op("dve", lambda e, o=wk[:, 0:2, 0:ln], a_=bv[:, 0:2, 0:ln], b_=tcr: e.tensor_tensor(out=o, in0=a_, in1=b_, op=ALU.mult),
                 reads=[bB, sB], writes=[wB])
            c.op("dve", lambda e, o=wk[:, 2:4, 0:ln], a_=bv[:, 2:4, 0:ln], b_=g.TC2[:, i, :, 0:ln]: e.tensor_tensor(out=o, in0=a_, in1=b_, op=ALU.mult),
                 reads=[bB, sB], writes=[wB])
            c.op("pool", lambda e, o=wk[:, 4:6, 0:ln], a_=wk[:, 0:2, 0:ln], b_=wk[:, 2:4, 0:ln]: e.tensor_tensor(out=o, in0=a_, in1=b_, op=ALU.add),
                 reads=[wB], writes=[wB])
            if seq is None:
                inr, ini = g.carry[:, 0, i:i + 1], g.carry[:, 1, i:i + 1]
            else:
                inr, ini = g.h0s[:, 0, i, seq:seq + 1], g.h0s[:, 1, i, seq:seq + 1]
            for q, init in ((0, inr), (1, ini)):
                c.op("dve", lambda e, o=wk[:, 6 + q, 0:ln], d0=g.RT[:, i, 0:ln], d1=wk[:, 4 + q, 0:ln], it=init: e.tensor_tensor_scan(
                    out=o, data0=d0, data1=d1, initial=it, op0=ALU.mult, op1=ALU.add), reads=[wB, sB], writes=[wB])
            wre, wie = wk[:, 6, ln - 1:ln], wk[:, 7, ln - 1:ln]
            if seq is None:
                er, ei, nei = sv[:, SVI["ERL"], i:i + 1], sv[:, SVI["EIL"], i:i + 1], sv[:, SVI["NEIL"], i:i + 1]
                outr, outi = g.carry[:, 0, i:i + 1], g.carry[:, 1, i:i + 1]
                t1, t2 = g.ct[:, 0:1], g.ct[:, 1:2]
                c.op("dve", lambda e, t1=t1, wre=wre, er=er: e.tensor_scalar(out=t1, in0=wre, scalar1=er, scalar2=None, op0=ALU.mult), reads=[wB, sB], writes=[sB])
                c.op("dve", lambda e, t2=t2, wre=wre, ei=ei: e.tensor_scalar(out=t2, in0=wre, scalar1=ei, scalar2=None, op0=ALU.mult), reads=[wB, sB], writes=[sB])
                c.op("dve", lambda e, outr=outr, wie=wie, nei=nei, t1=t1: e.scalar_tensor_tensor(out=outr, in0=wie, scalar=nei, in1=t1, op0=ALU.mult, op1=ALU.add), reads=[wB, sB], writes=[sB])
                c.op("dve", lambda e, outi=outi, wie=wie, er=er, t2=t2: e.scalar_tensor_tensor(out=outi, in0=wie, scalar=er, in1=t2, op0=ALU.mult, op1=ALU.add), reads=[wB, sB], writes=[sB])
            else:
                er, ei = g.Er[:, i, ln - 1:ln], g.Ei[:, i, ln - 1:ln]
                outr, outi = g.sfin[:, 0, i, seq:seq + 1], g.sfin[:, 1, i, seq:seq + 1]
                t1, t2, t3 = g.ct[:, 0:1], g.ct[:, 1:2], g.ct[:, 2:3]
                c.op("dve", lambda e, t1=t1, wre=wre, er=er: e.tensor_scalar(out=t1, in0=wre, scalar1=er, scalar2=None, op0=ALU.mult), reads=[wB, sB], writes=[sB])
                c.op("dve", lambda e, t2=t2, wre=wre, ei=ei: e.tensor_scalar(out=t2, in0=wre, scalar1=ei, scalar2=None, op0=ALU.mult), reads=[wB, sB], writes=[sB])
                c.op("dve", lambda e, t3=t3, wie=wie, ei=ei: e.tensor_scalar(out=t3, in0=wie, scalar1=ei, scalar2=None, op0=ALU.mult), reads=[wB, sB], writes=[sB])
                c.op("dve", lambda e, outr=outr, t1=t1, t3=t3: e.tensor_tensor(out=outr, in0=t1, in1=t3, op=ALU.subtract), reads=[sB], writes=[sB])
                c.op("dve", lambda e, outi=outi, wie=wie, er=er, t2=t2: e.scalar_tensor_tensor(out=outi, in0=wie, scalar=er, in1=t2, op0=ALU.mult, op1=ALU.add), reads=[wB, sB], writes=[sB])
            if not project:
                continue
            erb = g.Er[:, i, 0:ln].unsqueeze(1).broadcast_to([128, 2, ln])
            eib = g.Ei[:, i, 0:ln].unsqueeze(1).broadcast_to([128, 2, ln])
            c.op("dve", lambda e, o=wq[:, 0:2, 0:ln], a_=wk[:, 6:8, 0:ln], b_=erb: e.tensor_tensor(out=o, in0=a_, in1=b_, op=ALU.mult),
                 reads=[wB, sB], writes=[wB])
            c.op("dve", lambda e, o=wq[:, 2:4, 0:ln], a_=wk[:, 6:8, 0:ln], b_=eib: e.tensor_tensor(out=o, in0=a_, in1=b_, op=ALU.mult),
                 reads=[wB, sB], writes=[wB])
            if i % 4 == 0:
                ybk, yB = bank(g)
            yo = ybk[:, 0:ln]
            for q, tab in enumerate((g.CR, g.nCI, g.nCI, g.nCR)):
                first = (i % 4 == 0 and q == 0)
                last = (i % 4 == 3 and q == 3)
                c.op("pe", lambda e, o=yo, l=tab[:, i, :], r=wq[:, q, 0:ln], f=first, s_=last: e.matmul(o, l, r, start=f, stop=s_),
                     reads=[sB, wB], writes=[yB] if first else [], wnodep=[] if first else [yB], inc=(q == 3))
            if i % 4 == 3:
                c.op("dve", lambda e, o=yssm.t[:, j, pc0:pc0 + ln], u_=us.t[:, j, pc0:pc0 + ln], dd=g.ssmd[:, j:j + 1], y_=yo: e.scalar_tensor_tensor(
                    out=o, in0=u_, scalar=dd, in1=y_, op0=ALU.mult, op1=ALU.add), reads=[yB, us.B[j], g.cstB], writes=[yssm.B[j]])


def ssm_true_end(g, outr, outi):
    c, sB, sv, SVI = g.c, g.sB, g.sv, g.SVI
    cth, sth = sv[:, SVI["CTH"], :], sv[:, SVI["STH"], :]
    cr, ci = g.carry[:, 0, :], g.carry[:, 1, :]
    t0, t1 = sv[:, SVI["T0"], :], sv[:, SVI["T1"], :]
    tt = lambda o, x, y, op: c.op("dve", lambda e: e.tensor_tensor(out=o, in0=x, in1=y, op=op), reads=[sB], writes=[sB])
    tt(t0, cth, cr, ALU.mult)
    tt(t1, sth, ci, ALU.mult)
    tt(outr, t0, t1, ALU.add)
    tt(t0, cth, ci, ALU.mult)
    tt(t1, sth, cr, ALU.mult)
    tt(outi, t0, t1, ALU.subtract)


def cast_x(g, n):
    for k in range(KT):
        g.c.op("act", lambda e, o=g.xb.t[:, k, 0:n], i=g.xf.t[:, k, 0:n]: e.activation(out=o, in_=i, func=AF.Copy),
               reads=[g.xf.B[k]], writes=[g.xb.B[k]])


def phase1(g):
    c, I, S = g.c, g.I, g.S
    TP, NB = g.TP, g.NB
    alloc_main(g)
    ssm_setup(g)
    g.ub = ActT(g, "ub", 4, NB, BF16)
    zt = [c.sbuf(f"zt{i}", [128, NB], F32) for i in range(3)]
    ztB = [Buf(f"zt{i}") for i in range(3)]
    zi = [0]
    c.op("dve", lambda e, t_=g.carry: e.memset(t_[:], 0.0), reads=[g.sB], writes=[g.sB])
    nprompt = TP // NB
    for bi, (c0, n, kind) in enumerate(g.blocks):
        load_fm(g, g.xf, I["xT"], c0, n)
        cast_x(g, n)
        ffn(g, 0, 0, g.xf, g.xb, g.hb, g.gt, n)
        layernorm(g, g.xf, g.xb, g.hb, g.st, n, 0)
        store_fm(g, g.xf, "x1T", c0, n)

        def evac(m, bk, bB, c0=c0, n=n):
            z = zi[0]
            zi[0] = (z + 1) % 3
            c.op("act", lambda e, o=zt[z][:, 0:n], i=bk[:, 0:n]: e.activation(out=o, in_=i, func=AF.Copy),
                 reads=[bB], writes=[ztB[z]])
            if m >= 12:
                c.op("dve", lambda e, o=g.ub.t[:, m - 12, 0:n], i=bk[:, 0:n]: e.tensor_copy(out=o, in_=i),
                     reads=[bB], writes=[g.ub.B[m - 12]])
            c.dma("sp", S["uT"][m * 128:(m + 1) * 128, c0:c0 + n], zt[z][:, 0:n], reads=[ztB[z]],
                  writes=[g.DB["uT"]], sembuf=ztB[z])
        linear(g, "w_in_e", 4, KT, 512, lambda k: g.xb.t[:, k, 0:n], g.xb.B, n, evac)
        if kind == "p":
            ssm_block(g, g.ub, None, None, n, "p", project=False)
        if bi == nprompt - 1:
            ssm_true_end(g, g.send[:, 0, :], g.send[:, 1, :])
            c.dma("sp", S["g1_in"][0, 0:4096].rearrange("(p a b) -> p a b", p=128, a=2),
                  g.send[:], reads=[g.sB], writes=[g.DB["g1_in"]], sembuf=g.DB["g1_in"])
            c.dma("sp", S["g1_in"][0, 4096:4096 + 1536 * 15].rearrange("(f t) -> f t", t=15),
                  S["uT"][0:1536, TP - 15:TP], reads=[g.DB["uT"]] + ztB, writes=[g.DB["g1_in"]], sembuf=g.DB["g1_in"])
    c.collective(S["g1_in"], S["g1_out"], [[0, 1, 2, 3], [4, 5, 6, 7]], reads=[g.DB["g1_in"]],
                 writes=[g.DB["g1_out"]], sembuf=g.DB["g1_out"])


def gather_combine(g):
    c, S, sB, sv, SVI = g.c, g.S, g.sB, g.sv, g.SVI
    gs = c.sbuf("g1s", [128, 4, 2, 16], F32)
    g.halo = c.sbuf("halo", [128, 12, 15], F32)
    gh = c.sbuf("g1h", [128, 4, 12, 15], F32)
    for j in range(4):
        c.dma("sp", gs[:, j], S["g1_out"][j, 0:4096].rearrange("(p a b) -> p a b", p=128, a=2),
              reads=[g.DB["g1_out"], sB], writes=[sB], sembuf=sB)
        c.dma("sp", gh[:, j], S["g1_out"][j, 4096:4096 + 1536 * 15].rearrange("(k p t) -> p k t", p=128, t=15),
              reads=[g.DB["g1_out"], sB], writes=[sB], sembuf=sB)
    meta = lambda col: g.cst[:, 192 + col:193 + col]
    d = lambda fn: c.op("dve", fn, reads=[sB, g.cstB], writes=[sB])
    d(lambda e, h_=g.halo: e.tensor_scalar(out=h_[:], in0=gh[:, 0], scalar1=meta(0), scalar2=None, op0=ALU.mult))
    for j in range(1, 4):
        d(lambda e, j=j, h_=g.halo: e.scalar_tensor_tensor(out=h_[:], in0=gh[:, j], scalar=meta(j), in1=h_[:], op0=ALU.mult, op1=ALU.add))
    Td = c.sbuf("g1T", [128, 3, 2, 16], F32)
    for dd in range(3):
        d(lambda e, dd=dd: e.tensor_scalar(out=Td[:, dd], in0=gs[:, 0], scalar1=meta(4 + dd * 4), scalar2=None, op0=ALU.mult))
        for j in range(1, 4):
            d(lambda e, dd=dd, j=j: e.scalar_tensor_tensor(out=Td[:, dd], in0=gs[:, j], scalar=meta(4 + dd * 4 + j), in1=Td[:, dd],
                                                           op0=ALU.mult, op1=ALU.add))
    apr, api = sv[:, SVI["APR"], :], sv[:, SVI["API"], :]
    t0, t1 = sv[:, SVI["T0"], :], sv[:, SVI["T1"], :]
    cmul_sv(g, Td[:, 2, 0], Td[:, 2, 1], Td[:, 2, 0], Td[:, 2, 1], apr, api, t0, t1)
    d(lambda e: e.tensor_tensor(out=Td[:, 1], in0=Td[:, 1], in1=Td[:, 2], op=ALU.add))
    cmul_sv(g, Td[:, 1, 0], Td[:, 1, 1], Td[:, 1, 0], Td[:, 1, 1], apr, api, t0, t1)
    d(lambda e: e.tensor_tensor(out=Td[:, 0], in0=Td[:, 0], in1=Td[:, 1], op=ALU.add))
    cmul_sv(g, g.carry[:, 0, :], g.carry[:, 1, :], Td[:, 0, 0], Td[:, 0, 1], sv[:, SVI["CTH"], :], sv[:, SVI["STH"], :], t0, t1)
    hin = c.sbuf("h0in", [128, 2, 16, NSEQ], F32)
    c.dma("sp", hin[:, 0], g.I["ssm_h0re"].rearrange("p (a b) -> p a b", a=16), reads=[sB], writes=[sB], sembuf=sB)
    c.dma("sp", hin[:, 1], g.I["ssm_h0im"].rearrange("p (a b) -> p a b", a=16), reads=[sB], writes=[sB], sembuf=sB)
    cb = sv[:, SVI["CTH"], :].unsqueeze(2).broadcast_to([128, 16, NSEQ])
    sb_ = sv[:, SVI["STH"], :].unsqueeze(2).broadcast_to([128, 16, NSEQ])
    tq = g.tq
    x0, x1 = tq[:, 0, :, 0:NSEQ], tq[:, 1, :, 0:NSEQ]
    cmul_sv(g, g.h0s[:, 0], g.h0s[:, 1], hin[:, 0], hin[:, 1], cb, sb_, x0, x1)


def pool_mixer(g, c0, n, kind, bi, mixb, last_prompt):
    c, I, S = g.c, g.I, g.S
    nseg, sl = (1, n) if kind == "p" else (NSEQ, SL)
    W = 15 + sl
    wt, wB = loadw(g, I["pool_w"][0])
    wv = wt[:, 0:12 * 384].rearrange("p (k m) -> p k m", k=12)
    for gg in range(4):
        w = 2 ** (gg + 1)
        U, A, Bq = [t[:, :, 0:nseg * W].rearrange("p k (s w) -> p k s w", s=nseg) for t in g.pU]
        UB, AB, BB = g.pUB
        rows = S["uT"][gg * 384:(gg + 1) * 384, :].rearrange("(k p) t -> p k t", p=128)
        if kind == "p":
            lo = 15 if c0 == 0 else 0
            c.dma("sp", U[:, :, 0, lo:W], rows[:, :, c0 - 15 + lo:c0 + n], reads=[g.DB["uT"]], writes=[UB], sembuf=UB)
            if c0 == 0:
                c.op("dve", lambda e, o=U[:, :, 0, 0:15], i=g.halo[:, gg * 3:gg * 3 + 3, :]: e.tensor_copy(out=o, in_=i),
                     reads=[g.sB], writes=[UB])
        else:
            for s_ in range(NSEQ):
                c.dma("sp", U[:, :, s_, 15:W], rows[:, :, c0 + s_ * SL:c0 + (s_ + 1) * SL], reads=[g.DB["uT"]], writes=[UB], sembuf=UB)
            for s_ in range(NSEQ):
                c.dma("sp", U[:, :, s_, 0:15], I["cache_poolT"][gg * 384:(gg + 1) * 384, s_, :].rearrange("(k p) t -> p k t", p=128),
                      writes=[UB], sembuf=UB)
        src, srcB = U, UB
        sh = 1
        dsts = [(A, AB), (Bq, BB), (A, AB), (Bq, BB)]
        for step in range(gg + 1):
            dst, dstB = dsts[step]
            lo = 2 * sh - 1
            c.op("dve", lambda e, o=dst[:, :, :, lo:W], a_=src[:, :, :, lo:W], b_=src[:, :, :, lo - sh:W - sh]: e.tensor_tensor(
                out=o, in0=a_, in1=b_, op=ALU.add), reads=[srcB], writes=[dstB])
            src, srcB = dst, dstB
            sh *= 2
        if kind == "p" and c0 == 0:
            c.op("dve", lambda e, o=src[:, :, 0, 15:31], cr=g.pcorr[:, gg * 3:gg * 3 + 3, :]: e.tensor_tensor(out=o, in0=o, in1=cr, op=ALU.mult),
                 reads=[g.cstB], writes=[srcB])
        dview = g.db.t[:, gg * 3:gg * 3 + 3, 0:n].rearrange("p k (s l) -> p k s l", s=nseg)
        c.op("dve", lambda e, o=dview, a_=src[:, :, :, 15:W], b_=U[:, :, :, 15:W], w=w: e.scalar_tensor_tensor(
            out=o, in0=a_, scalar=1.0 / w, in1=b_, op0=ALU.mult, op1=ALU.subtract), reads=[srcB, UB], writes=g.db.B[gg * 3:gg * 3 + 3])
        if kind == "s":
            for s_ in range(NSEQ):
                c.dma("sp", g.O["poolT_s"][gg * 384:(gg + 1) * 384, s_, :].rearrange("(k p) t -> p k t", p=128), U[:, :, s_, sl:W],
                      reads=[UB], writes=[g.DB["poolT_s"]], sembuf=g.DB["poolT_s"])
        elif last_prompt:
            c.dma("sp", g.O["poolT_p"][gg * 384:(gg + 1) * 384].rearrange("(k p) t -> p k t", p=128), U[:, :, 0, sl:W],
                  reads=[UB], writes=[g.DB["poolT_p"]], sembuf=g.DB["poolT_p"])
        for mi in range(3):
            bk, bB = bank(g)
            for k in range(3):
                mm(g, bk[:, 0:n], wv[:, gg * 3 + k, mi * 128:(mi + 1) * 128], g.db.t[:, gg * 3 + k, 0:n], k == 0, k == 2,
                   [wB, g.db.B[gg * 3 + k]], bB)
            m = gg * 3 + mi
            c.op("act", lambda e, o=mixb.t[:, m, 0:n], i=bk[:, 0:n], s_=g.pscale[:, m:m + 1]: e.activation(out=o, in_=i, func=AF.Copy, scale=s_),
                 reads=[bB, g.cstB], writes=[mixb.B[m]])


def rms_scale(g, srcf, nk, n, dim, stt):
    c = g.c
    for k in range(nk):
        c.op("act", lambda e, o=g.sqs.t[:, k, 0:n], i=srcf.t[:, k, 0:n]: e.activation(out=o, in_=i, func=AF.Square),
             reads=[srcf.B[k]], writes=[g.sqs.B[k]])
    s2, s2B = colsum(g, lambda k: g.sqs.t[:, k, 0:n], g.sqs.B, nk, n)
    r = g.st[:, stt, 0:n]
    c.op("dve", lambda e: e.tensor_scalar(out=r, in0=s2[:, 0:n], scalar1=1.0 / dim, scalar2=RMS_EPS, op0=ALU.mult, op1=ALU.add),
         reads=[s2B], writes=[g.stB[stt]])
    c.op("act", lambda e: e.activation(out=r, in_=r, func=AF.Sqrt), reads=[g.stB[stt]], writes=[g.stB[stt]])
    c.op("dve", lambda e: e.reciprocal(out=r, in_=r), reads=[g.stB[stt]], writes=[g.stB[stt]])
    return r, g.stB[stt]


def odd_proj(g, c0, n, kind):
    c, I, S, O = g.c, g.I, g.S, g.O
    xb = g.xb
    rhs = lambda k: xb.t[:, k, 0:n]
    c.dma("sp", g.rope[64:96, :, 0:n], I["rope"][64:96, :, c0:c0 + n], writes=[g.ropeB], sembuf=g.ropeB)
    def ev_cq(m, bk, bB):
        c.op("act", lambda e, o=g.cqf.t[:, m, 0:n], i=bk[:, 0:n]: e.activation(out=o, in_=i, func=AF.Copy), reads=[bB], writes=[g.cqf.B[m]])
    linear(g, "w_in_o_cq", 1, KT, 512, rhs, xb.B, n, ev_cq)
    if "odd1" in os.environ.get("KDBG", ""):
        return
    rq, rqB = rms_scale(g, g.cqf, 4, n, 512, 0)
    for k in range(4):
        c.op("dve", lambda e, o=g.cqn.t[:, k, 0:n], a_=g.cqf.t[:, k, 0:n], s_=g.gq[:, k:k + 1]: e.scalar_tensor_tensor(
            out=o, in0=a_, scalar=s_, in1=rq, op0=ALU.mult, op1=ALU.mult), reads=[g.cqf.B[k], rqB, g.cstB], writes=[g.cqn.B[k]])
    wt, wB = loadw(g, I["w_in_o_kv"][0])
    wv = wt[:, 0:KT * 320].rearrange("p (k m) -> p k m", k=KT)
    for m in range(2):
        bk, bB = bank(g)
        for k in range(KT):
            mm(g, bk[:, 0:n], wv[:, k, m * 128:(m + 1) * 128], rhs(k), k == 0, k == KT - 1, [wB, xb.B[k]], bB)
        c.op("act", lambda e, o=g.ckf.t[:, m, 0:n], i=bk[:, 0:n]: e.activation(out=o, in_=i, func=AF.Copy), reads=[bB], writes=[g.ckf.B[m]])
    bka, bAB = bank(g)
    for k in range(KT):
        mm(g, bka[64:96, 0:n], wv[:, k, 256:288], rhs(k), k == 0, k == KT - 1, [wB, xb.B[k]], bAB)
    bkb, bBB = bank(g)
    for k in range(KT):
        mm(g, bkb[64:96, 0:n], wv[:, k, 288:320], rhs(k), k == 0, k == KT - 1, [wB, xb.B[k]], bBB)
    kp = g.kpo
    cosr, sinr = g.rope[64:96, 0, 0:n], g.rope[64:96, 1, 0:n]
    c.op("dve", lambda e: e.tensor_tensor(out=kp[64:96, 0, 0:n], in0=bka[64:96, 0:n], in1=cosr, op=ALU.mult), reads=[bAB, g.ropeB], writes=[g.kpoB])
    c.op("dve", lambda e: e.tensor_tensor(out=kp[64:96, 1, 0:n], in0=bkb[64:96, 0:n], in1=sinr, op=ALU.mult), reads=[bBB, g.ropeB], writes=[g.kpoB])
    c.op("dve", lambda e: e.tensor_tensor(out=kp[64:96, 0, 0:n], in0=kp[64:96, 0, 0:n], in1=kp[64:96, 1, 0:n], op=ALU.add), writes=[g.kpoB])
    c.op("dve", lambda e: e.tensor_copy(out=g.kpb[64:96, 0:n], in_=kp[64:96, 0, 0:n]), reads=[g.kpoB], writes=[g.kpbB])
    c.dma("sp", O["kpeT"][:, c0:c0 + n], kp[64:96, 0, 0:n], reads=[g.kpoB], writes=[g.DB["kpeT"]], sembuf=g.DB["kpeT"])
    if "odd2" in os.environ.get("KDBG", ""):
        return
    rk, rkB = rms_scale(g, g.ckf, 2, n, 256, 1)
    for k in range(2):
        c.op("dve", lambda e, o=g.cko.t[:, k, 0:n], a_=g.ckf.t[:, k, 0:n], s_=g.gkv[:, k:k + 1]: e.scalar_tensor_tensor(
            out=o, in0=a_, scalar=s_, in1=rk, op0=ALU.mult, op1=ALU.mult), reads=[g.ckf.B[k], rkB, g.cstB], writes=[g.cko.B[k]])
        c.op("act", lambda e, o=g.ckb.t[:, k, 0:n], i=g.cko.t[:, k, 0:n]: e.activation(out=o, in_=i, func=AF.Copy),
             reads=[g.cko.B[k]], writes=[g.ckb.B[k]])
    store_fm(g, g.cko, "ckvT", c0, n)
    if kind == "p":
        ci = c0 // 256
        c.dma("sp", S["g2_in"][ci, 0:256, :].rearrange("(k p) t -> p k t", p=128), g.ckb.t[:, :, 0:n],
              reads=g.ckb.B, writes=[g.DB["g2_in"]], sembuf=g.DB["g2_in"])
        c.dma("sp", S["g2_in"][ci, 256:288, :], g.kpb[64:96, 0:n], reads=[g.kpbB], writes=[g.DB["g2_in"]], sembuf=g.DB["g2_in"])
    else:
        c.dma("sp", S["skv"][0:256, :].rearrange("(k p) t -> p k t", p=128), g.ckb.t[:, :, 0:n],
              reads=g.ckb.B, writes=[g.DB["skv"]], sembuf=g.DB["skv"])
        c.dma("sp", S["skv"][256:288, :], g.kpb[64:96, 0:n], reads=[g.kpbB], writes=[g.DB["skv"]], sembuf=g.DB["skv"])
    if "odd3" in os.environ.get("KDBG", ""):
        return
    wd = I["w_uq"]
    for ch in range(2):
        wt, wB = loadw(g, wd[ch])
        wv = wt[:, 0:4096].rearrange("p (k m) -> p k m", k=4)
        for h8 in range(8):
            h = ch * 8 + h8
            bk, bB = bank(g)
            for k in range(4):
                mm(g, bk[0:96, 0:n], wv[:, k, h8 * 128:h8 * 128 + 96], g.cqn.t[:, k, 0:n], k == 0, k == 3, [wB, g.cqn.B[k]], bB)
            bk2, bB2 = bank(g)
            for k in range(4):
                mm(g, bk2[64:96, 0:n], wv[:, k, h8 * 128 + 96:h8 * 128 + 128], g.cqn.t[:, k, 0:n], k == 0, k == 3, [wB, g.cqn.B[k]], bB2)
            qi = g.qi
            g.qi = (g.qi + 1) % 2
            qt, qB = g.qt[qi], g.qtB[qi]
            c.op("act", lambda e, o=qt[0:64, 0:n], i=bk[0:64, 0:n]: e.activation(out=o, in_=i, func=AF.Copy), reads=[bB], writes=[qB])
            c.op("dve", lambda e, o=kp[64:96, 2, 0:n], i=bk[64:96, 0:n]: e.tensor_tensor(out=o, in0=i, in1=cosr, op=ALU.mult), reads=[bB, g.ropeB], writes=[g.kpoB])
            c.op("dve", lambda e, o=kp[64:96, 3, 0:n], i=bk2[64:96, 0:n]: e.tensor_tensor(out=o, in0=i, in1=sinr, op=ALU.mult), reads=[bB2, g.ropeB], writes=[g.kpoB])
            c.op("dve", lambda e, o=qt[64:96, 0:n]: e.tensor_tensor(out=o, in0=kp[64:96, 2, 0:n], in1=kp[64:96, 3, 0:n], op=ALU.add), reads=[g.kpoB], writes=[qB])
            c.dma("sp", S["qT"][h, :, c0:c0 + n], qt[0:96, 0:n], reads=[qB], writes=[g.DB["qT"]], sembuf=qB)
    if "odd4" in os.environ.get("KDBG", ""):
        return
    def ev_u(m, bk, bB):
        c.op("act", lambda e, o=g.uf8.t[:, m, 0:n], i=bk[:, 0:n]: e.activation(out=o, in_=i, func=AF.Copy), reads=[bB], writes=[g.uf8.B[m]])
    linear(g, "w_in_o_u", 2, KT, 512, rhs, xb.B, n, ev_u)
    if "odd5" in os.environ.get("KDBG", ""):
        return
    wvs = []
    for ch in range(2):
        wt, wB = loadw(g, I["w_in_o_v"][ch])
        wvs.append((wt[:, 0:KT * 512].rearrange("p (k m) -> p k m", k=KT), wB))
    for ts in range(n // 128):
        vt = g.vtok
        for half in range(2):
            wv, wB = wvs[half]
            bk, bB = bank(g)
            for k in range(KT):
                mm(g, bk[:, 0:512], xb.t[:, k, ts * 128:(ts + 1) * 128], wv[:, k, :], k == 0, k == KT - 1, [wB, xb.B[k]], bB)
            c.op("act", lambda e, o=vt[:, half * 512:(half + 1) * 512], i=bk[:, 0:512]: e.activation(out=o, in_=i, func=AF.Copy),
                 reads=[bB], writes=[g.vtB])
        c.op("dve", lambda e: e.tensor_reduce(out=g.bag[:, 0:1], in_=vt[:], op=ALU.add, axis=mybir.AxisListType.X), reads=[g.vtB], writes=[g.bstB])
        c.op("act", lambda e: e.activation(out=g.vbt[:], in_=vt[:], func=AF.Square, accum_out=g.bag[:, 1:2]), reads=[g.vtB], writes=[g.vbB, g.bstB])
        c.op("dve", lambda e: e.tensor_scalar(out=g.bag[:, 0:2], in0=g.bag[:, 0:2], scalar1=1.0 / 1024, scalar2=None, op0=ALU.mult), reads=[g.bstB], writes=[g.bstB])
        c.op("dve", lambda e: e.tensor_tensor(out=g.bag[:, 2:3], in0=g.bag[:, 0:1], in1=g.bag[:, 0:1], op=ALU.mult), reads=[g.bstB], writes=[g.bstB])
        c.op("dve", lambda e: e.tensor_tensor(out=g.bag[:, 2:3], in0=g.bag[:, 1:2], in1=g.bag[:, 2:3], op=ALU.subtract), reads=[g.bstB], writes=[g.bstB])
        c.op("dve", lambda e: e.tensor_scalar(out=g.bag[:, 2:3], in0=g.bag[:, 2:3], scalar1=LN_EPS, scalar2=None, op0=ALU.add), reads=[g.bstB], writes=[g.bstB])
        c.op("act", lambda e: e.activation(out=g.bag[:, 2:3], in_=g.bag[:, 2:3], func=AF.Sqrt), reads=[g.bstB], writes=[g.bstB])
        c.op("dve", lambda e: e.reciprocal(out=g.bag[:, 3:4], in_=g.bag[:, 2:3]), reads=[g.bstB], writes=[g.bstB])
        c.op("dve", lambda e: e.tensor_scalar(out=vt[:], in0=vt[:], scalar1=g.bag[:, 0:1], scalar2=None, op0=ALU.subtract), reads=[g.bstB], writes=[g.vtB])
        c.op("dve", lambda e: e.tensor_scalar(out=vt[:], in0=vt[:], scalar1=g.bag[:, 3:4], scalar2=None, op0=ALU.mult), reads=[g.bstB], writes=[g.vtB])
        c.op("dve", lambda e: e.tensor_tensor(out=vt[:], in0=vt[:], in1=g.sgg[:], op=ALU.mult), reads=[g.cstB], writes=[g.vtB])
        c.op("dve", lambda e: e.tensor_tensor(out=vt[:], in0=vt[:], in1=g.sgb[:], op=ALU.add), reads=[g.cstB], writes=[g.vtB])
        c.op("act", lambda e: e.activation(out=g.vbt[:], in_=vt[:], func=AF.Copy), reads=[g.vtB], writes=[g.vbB])
        if kind == "s":
            c.dma("sp", O["sgv"][ts * 128:(ts + 1) * 128, :], vt[:], reads=[g.vtB], writes=[g.DB["sgv"]], sembuf=g.DB["sgv"])
        for g4 in range(2):
            tmp = g.sgt
            if kind == "p":
                bk, bB = bank(g)
                bv = bk[:, 0:512].rearrange("p (a b) -> p a b", a=4)
                for gq in range(4):
                    gr = g4 * 4 + gq
                    c.op("pe", lambda e, o=bv[:, gq, :], l=g.vbt[:, gr * 128:(gr + 1) * 128], r=g.swT[:, gr, :]: e.matmul(o, l, r, start=True, stop=True),
                         reads=[g.vbB, g.cstB], writes=[bB] if gq == 0 else [], wnodep=[] if gq == 0 else [bB], inc=(gq == 3))
                bsv = g.sbs[:, g4 * 4:(g4 + 1) * 4, :]
                c.op("dve", lambda e, o=tmp[:], a_=bv, b_=bsv: e.tensor_tensor(out=o, in0=a_, in1=b_, op=ALU.add), reads=[bB, g.cstB], writes=[g.sgtB])
            else:
                for hf in range(2):
                    bk, bB = bank(g)
                    bv = bk[:, 0:512].rearrange("p (a b) -> p a b", a=4)
                    for gq in range(4):
                        gr = g4 * 4 + gq
                        c.op("pe", lambda e, o=bv[:, gq, 0:64], l=g.vbt[hf * 64:(hf + 1) * 64, gr * 128:(gr + 1) * 128],
                             r=g.swT64[hf * 64:(hf + 1) * 64, gr, :]: e.matmul(o, l, r, start=True, stop=True),
                             reads=[g.vbB, g.cstB], writes=[bB] if gq == 0 else [], wnodep=[] if gq == 0 else [bB], inc=(gq == 3))
                    bsv = g.sbs64[:, g4 * 4:(g4 + 1) * 4, hf * 64:(hf + 1) * 64]
                    c.op("dve", lambda e, o=tmp[:, :, hf * 64:(hf + 1) * 64], a_=bv[:, :, 0:64], b_=bsv: e.tensor_tensor(out=o, in0=a_, in1=b_, op=ALU.add),
                         reads=[bB, g.cstB], writes=[g.sgtB])
            c.op("dve", lambda e, o=g.sgo.t[:, g4 * 4:(g4 + 1) * 4, ts * 128:(ts + 1) * 128], a_=tmp[:], b_=g.uf8.t[:, g4 * 4:(g4 + 1) * 4, ts * 128:(ts + 1) * 128]: e.tensor_tensor(
                out=o, in0=a_, in1=b_, op=ALU.mult), reads=[g.sgtB] + g.uf8.B[g4 * 4:(g4 + 1) * 4], writes=g.sgo.B[g4 * 4:(g4 + 1) * 4])
    store_fm(g, g.sgo, "sgoT", c0, n)


def phase2(g):
    c, I, S, O = g.c, g.I, g.S, g.O
    TP, NB = g.TP, g.NB
    alloc_main(g)
    ssm_setup(g)
    gather_combine(g)
    cs = c.sbuf("cst2", [128, 12 + 12 * 16 + 4 + 4 + 4 + 2], F32)
    o = 0
    def ld(name, w):
        nonlocal o
        v = cs[:, o:o + w]
        c.dma("sp", v, I[name], writes=[g.cstB], sembuf=g.cstB)
        o += w
        return v
    g.pscale = ld("pool_scale", 12)
    g.pcorr = ld("pool_corr", 192).rearrange("p (k t) -> p k t", k=12)
    g.ssmd = ld("ssm_d", 4)
    g.bglu = ld("b_glu", 4)
    W = max(15 + NB, NSEQ * (15 + SL))
    g.pU = [c.sbuf(f"pU{i}", [128, 3, W], F32) for i in range(3)]
    g.pUB = [Buf(f"pU{i}") for i in range(3)]
    g.db = ActT(g, "db", 12, NB, BF16)
    g.mixb = ActT(g, "mixb", 16, NB, BF16)
    g.us = ActT(g, "us", 4, NB, F32)
    g.ub = ActT(g, "ub", 4, NB, BF16)
    g.yssm = ActT(g, "yssm", 4, NB, F32)
    g.gf = ActT(g, "gf", 4, NB, F32)
    g.gb = ActT(g, "gb", 4, NB, BF16)
    nprompt = TP // NB
    for bi, (c0, n, kind) in enumerate(g.blocks):
        load_fm(g, g.xf, S["x1T"], c0, n)
        load_fm(g, g.us, S["uT"][1536:2048, :], c0, n)
        for k in range(4):
            c.op("act", lambda e, o=g.ub.t[:, k, 0:n], i=g.us.t[:, k, 0:n]: e.activation(out=o, in_=i, func=AF.Copy),
                 reads=[g.us.B[k]], writes=[g.ub.B[k]])
        if "nopool" not in os.environ.get("KDBG", ""):
            pool_mixer(g, c0, n, kind, bi, g.mixb, bi == nprompt - 1)
        if "nossm" not in os.environ.get("KDBG", ""):
            ssm_block(g, g.ub, g.us, g.yssm, n, kind, project=True)
        if bi == nprompt - 1:
            ssm_true_end(g, g.send[:, 0, :], g.send[:, 1, :])
            c.dma("sp", O["ssm_p"], g.send[:], reads=[g.sB], writes=[g.DB["ssm_p"]], sembuf=g.DB["ssm_p"])
        if kind == "s":
            c.dma("sp", O["ssm_s"], g.sfin[:].rearrange("p a b s -> p a (b s)"), reads=[g.sB], writes=[g.DB["ssm_s"]], sembuf=g.DB["ssm_s"])
        for k in range(4):
            c.op("act", lambda e, o=g.gf.t[:, k, 0:n], i=g.yssm.t[:, k, 0:n]: e.activation(out=o, in_=i, func=AF.Gelu_apprx_tanh),
                 reads=[g.yssm.B[k]], writes=[g.gf.B[k]])
            c.op("dve", lambda e, o=g.gb.t[:, k, 0:n], i=g.gf.t[:, k, 0:n]: e.tensor_copy(out=o, in_=i), reads=[g.gf.B[k]], writes=[g.gb.B[k]])

        def ev_glu(m, bk, bB, n=n):
            c.op("act", lambda e, o=g.gt[:, 0, 0:n], i=bk[:, 0:n], b_=g.bglu[:, m:m + 1]: e.activation(out=o, in_=i, func=AF.Sigmoid, bias=b_),
                 reads=[bB, g.cstB], writes=[g.gtB[0]])
            c.op("dve", lambda e, o=g.mixb.t[:, 12 + m, 0:n], a_=g.gf.t[:, m, 0:n], b_=g.gt[:, 0, 0:n]: e.tensor_tensor(out=o, in0=a_, in1=b_, op=ALU.mult),
                 reads=[g.gtB[0], g.gf.B[m]], writes=[g.mixb.B[12 + m]])
        linear(g, "w_glu", 1, 4, 512, lambda k: g.gb.t[:, k, 0:n], g.gb.B, n, ev_glu)

        def ev_mix(m, bk, bB, n=n):
            c.op("dve", lambda e, o=g.xf.t[:, m, 0:n], a_=bk[:, 0:n]: e.scalar_tensor_tensor(out=o, in0=a_, scalar=1.0 / ALPHA, in1=o, op0=ALU.mult, op1=ALU.add),
                 reads=[bB], writes=[g.xf.B[m]])
        linear(g, "w_out_e", 4, KT, 512, lambda k: g.mixb.t[:, k, 0:n], g.mixb.B, n, ev_mix)
        layernorm(g, g.xf, g.xb, g.hb, g.st, n, 1)
        ffn(g, 0, 1, g.xf, g.xb, g.hb, g.gt, n)
        layernorm(g, g.xf, g.xb, g.hb, g.st, n, 2)
        ffn(g, 1, 0, g.xf, g.xb, g.hb, g.gt, n)
        layernorm(g, g.xf, g.xb, g.hb, g.st, n, 3)
        store_fm(g, g.xf, "x4T", c0, n)


def phase2b(g):
    c, I, S, O = g.c, g.I, g.S, g.O
    TP, NB = g.TP, g.NB
    g.xf = ActT(g, "xf", KT, NB, F32)
    g.xb = ActT(g, "xb", KT, NB, BF16)
    g.st = c.sbuf("st", [128, 6, NB], F32)
    g.stB = [Buf(f"st{i}") for i in range(6)]
    cs = c.sbuf("cst3", [128, 8], F32)
    g.gq = cs[:, 0:4]
    g.gkv = cs[:, 4:6]
    c.dma("sp", g.gq, I["g_q"], writes=[g.cstB], sembuf=g.cstB)
    c.dma("sp", g.gkv, I["g_kv"], writes=[g.cstB], sembuf=g.cstB)
    g.sgg = c.sbuf("sgg", [128, 1024], F32)
    g.sgb = c.sbuf("sgb", [128, 1024], F32)
    c.dma("sp", g.sgg[:], I["sg_gv"], writes=[g.cstB], sembuf=g.cstB)
    c.dma("sp", g.sgb[:], I["sg_bv"], writes=[g.cstB], sembuf=g.cstB)
    g.sbs = c.sbuf("sbs", [128, 8, 128], F32)
    g.sbs64 = c.sbuf("sbs64", [128, 8, 128], F32)
    c.dma("sp", g.sbs[:], I["sg_bs"].rearrange("p (a b) -> p a b", a=8), writes=[g.cstB], sembuf=g.cstB)
    c.dma("sp", g.sbs64[:], I["sg_bs64"].rearrange("p (a b) -> p a b", a=8), writes=[g.cstB], sembuf=g.cstB)
    g.swT = c.sbuf("swT", [128, 8, 128], BF16)
    g.swT64 = c.sbuf("swT64", [128, 8, 64], BF16)
    trl = c.sbuf("tril", [128, 128], F32)
    stgt = c.sbuf("stg", [128, 1536], F32)
    stg = stgt[:, 0:1536]
    c.dma("sp", trl[:], I["tril"], writes=[g.cstB], sembuf=g.cstB)
    c.dma("sp", stg[:, 0:1024], I["sg_wT"], writes=[g.cstB], sembuf=g.cstB)
    c.dma("sp", stg[:, 1024:1536], I["sg_wT64"], writes=[g.cstB], sembuf=g.cstB)
    c.op("dve", lambda e: e.tensor_tensor(out=g.swT[:], in0=stg[:, 0:1024].rearrange("p (a b) -> p a b", a=8),
                                          in1=trl[:].unsqueeze(1).broadcast_to([128, 8, 128]), op=ALU.mult), reads=[g.cstB], writes=[g.cstB])
    for hf in range(2):
        c.op("dve", lambda e, hf=hf: e.tensor_tensor(out=g.swT64[hf * 64:(hf + 1) * 64], in0=stg[hf * 64:(hf + 1) * 64, 1024:1536].rearrange("p (a b) -> p a b", a=8),
                                                     in1=trl[hf * 64:(hf + 1) * 64, hf * 64:(hf + 1) * 64].unsqueeze(1).broadcast_to([64, 8, 64]), op=ALU.mult),
             reads=[g.cstB], writes=[g.cstB])
    g.cqf = ActT(g, "cqf", 4, NB, F32)
    g.cqn = ActT(g, "cqn", 4, NB, BF16)
    g.sqs = ActT(g, "sqs", 4, NB, BF16)
    g.ckf = ActT(g, "ckf", 2, NB, F32)
    g.cko = ActT(g, "cko", 2, NB, F32)
    g.ckb = ActT(g, "ckb", 2, NB, BF16)
    g.kpo = c.sbuf("kpo", [128, 4, NB], F32)
    g.kpoB = Buf("kpo")
    g.kpb = c.sbuf("kpb", [128, NB], BF16)
    g.kpbB = Buf("kpb")
    g.rope = c.sbuf("rope", [128, 2, NB], F32)
    g.ropeB = Buf("rope")
    g.qt = [c.sbuf(f"qt{i}", [128, NB], BF16) for i in range(2)]
    g.qtB = [Buf(f"qt{i}") for i in range(2)]
    g.qi = 0
    g.uf8 = ActT(g, "uf8", 8, NB, F32)
    g.sgo = ActT(g, "sgo", 8, NB, BF16)
    g.vtok = c.sbuf("vtok", [128, 1024], F32)
    g.vtB = Buf("vtok")
    g.vbt = c.sbuf("vbt", [128, 1024], BF16)
    g.vbB = Buf("vbt")
    g.bst = c.sbuf("bst", [128, 2, 6], F32)
    g.bag = c.sbuf("bag", [128, 4], F32)
    g.bstB = Buf("bst")
    g.sgt = c.sbuf("sgt", [128, 4, 128], F32)
    g.sgtB = Buf("sgt")
    for bi, (c0, n, kind) in enumerate(g.blocks):
        if "noodd" in os.environ.get("KDBG", ""):
            break
        load_fm(g, g.xf, S["x4T"], c0, n)
        cast_x(g, n)
        odd_proj(g, c0, n, kind)
    for ci in range(g.NCH):
        c.collective(S["g2_in"][ci], S["g2_out"][ci], [[0, 1, 2, 3], [4, 5, 6, 7]], reads=[g.DB["g2_in"]],
                     writes=[g.g2B[ci]], sembuf=g.g2B[ci])


def attend(g, A, qap, nq, tiles, out_ap):
    c = g.c
    ob, oB = g.banks[7], g.bankB[7]
    nt = len(tiles)
    for idx, (kc, kn, bias, mask) in enumerate(tiles):
        sb, sB_ = bank(g)
        c.op("pe", lambda e, o=sb[0:kn, 0:nq], l=A.KT[0:96, kc:kc + kn], r=qap: e.matmul(o, l, r, start=True, stop=True),
             reads=[A.KTB, A.QB], writes=[sB_])
        pi = A.pi
        A.pi = (A.pi + 1) % 2
        pt, pB = A.PT[pi], A.PTB[pi]
        if bias is None:
            c.op("act", lambda e, o=pt[0:kn, 0:nq], i=sb[0:kn, 0:nq]: e.activation(out=o, in_=i, func=AF.Exp, scale=ATTN_SCALE),
                 reads=[sB_], writes=[pB])
        else:
            c.op("act", lambda e, o=pt[0:kn, 0:nq], i=sb[0:kn, 0:nq], b_=bias: e.activation(out=o, in_=i, func=AF.Exp, scale=ATTN_SCALE, bias=b_),
                 reads=[sB_, g.cstB], writes=[pB])
        if mask is not None:
            c.op("dve", lambda e, o=pt[0:kn, 0:nq], m_=mask: e.tensor_tensor(out=o, in0=o, in1=m_, op=ALU.mult), reads=[A.dmB], writes=[pB])
        c.op("pe", lambda e, o=ob[0:65, 0:nq], l=A.VT[0:kn, kc // 128, 0:65], r=pt[0:kn, 0:nq], f=(idx == 0), s_=(idx == nt - 1): e.matmul(o, l, r, start=f, stop=s_),
             reads=[A.VTB, pB], writes=[oB] if idx == 0 else [], wnodep=[] if idx == 0 else [oB], inc=True)
    c.op("dve", lambda e, o=A.rd[64:65, 0:nq], i=ob[64:65, 0:nq]: e.reciprocal(out=o, in_=i), reads=[oB], writes=[A.rdB])
    b2, b2B = g.banks[6], g.bankB[6]
    c.op("pe", lambda e, o=b2[0:64, 0:nq], l=g.onesf[64:65, 0:64], r=A.rd[64:65, 0:nq]: e.matmul(o, l, r, start=True, stop=True),
         reads=[A.rdB, g.onesB], writes=[b2B])
    c.op("act", lambda e, o=A.to[0:64, 0:nq], i=ob[0:64, 0:nq]: e.activation(out=o, in_=i, func=AF.Copy), reads=[oB], writes=[A.toB])
    ai = A.ai
    A.ai = (A.ai + 1) % 2
    at, aB = A.att[ai], A.attB[ai]
    c.op("dve", lambda e, o=at[0:64, 0:nq], a_=A.to[0:64, 0:nq], b_=b2[0:64, 0:nq]: e.tensor_tensor(out=o, in0=a_, in1=b_, op=ALU.mult),
         reads=[A.toB, b2B], writes=[aB])
    c.dma("sp", out_ap, at[0:64, 0:nq], reads=[aB], writes=[g.DB["attT"]], sembuf=aB)


def kv_produce(g, A, h, nk):
    c = g.c
    col = 0
    flip = 0
    while col < nk:
        w = min(512, nk - col)
        bk, bB = bank(g)
        for kt in range(2):
            mm(g, bk[0:64, 0:w], A.wuk[:, kt, h * 64:(h + 1) * 64], A.CK[:, kt, col:col + w], kt == 0, kt == 1, [A.wB, A.CKB], bB)
        if flip:
            c.op("act", lambda e, o=A.KT[0:64, col:col + w], i=bk[0:64, 0:w]: e.activation(out=o, in_=i, func=AF.Copy), reads=[bB], writes=[A.KTB])
        else:
            c.op("dve", lambda e, o=A.KT[0:64, col:col + w], i=bk[0:64, 0:w]: e.tensor_copy(out=o, in_=i), reads=[bB], writes=[A.KTB])
        flip ^= 1
        col += w
    ntile = (nk + 127) // 128
    t = 0
    while t < ntile:
        gsz = min(8, ntile - t)
        bk, bB = bank(g)
        bv = bk[:, 0:512].rearrange("p (a b) -> p a b", a=8)
        full = 0
        for q in range(gsz):
            kn = min(128, nk - (t + q) * 128)
            if kn == 128:
                full += 1
            for kt in range(2):
                first = (q == 0 and kt == 0)
                c.op("pe", lambda e, o=bv[0:kn, q, :], l=A.CK[:, kt, (t + q) * 128:(t + q) * 128 + kn], r=A.wuv[:, kt, h * 64:(h + 1) * 64], f=(kt == 0), s_=(kt == 1): e.matmul(o, l, r, start=f, stop=s_),
                     reads=[A.wB, A.CKB], writes=[bB] if first else [], wnodep=[] if first else [bB], inc=(q == gsz - 1 and kt == 1))
        if full:
            c.op("dve", lambda e, o=A.VT[:, t:t + full, 0:64], i=bv[:, 0:full, :]: e.tensor_copy(out=o, in_=i), reads=[bB], writes=[A.VTB])
        if full < gsz:
            kn = nk - (t + full) * 128
            c.op("dve", lambda e, o=A.VT[0:kn, t + full, 0:64], i=bv[0:kn, full, :]: e.tensor_copy(out=o, in_=i), reads=[bB], writes=[A.VTB])
        t += gsz


def phase3a(g):
    c, I, S = g.c, g.I, g.S
    TP, NB, PAST, NT = g.TP, g.NB, g.PAST, g.NT
    A = K()
    g.nrot = 6
    g.bi = 0
    NKP = 5 * TP
    NKS = PAST + SL
    KW = max(NKP, NKS)
    A.CK = c.sbuf("aCK", [128, 2, KW], BF16); A.CKB = Buf("aCK")
    A.KT = c.sbuf("aKT", [128, KW], BF16); A.KTB = Buf("aKT")
    A.VT = c.sbuf("aVT", [128, (KW + 127) // 128, 65], BF16); A.VTB = Buf("aVT")
    A.Q = c.sbuf("aQ", [128, max(NT, 16 * SL)], BF16); A.QB = Buf("aQ")
    A.PT = [c.sbuf(f"aPT{i}", [128, NB], BF16) for i in range(2)]; A.PTB = [Buf(f"aPT{i}") for i in range(2)]; A.pi = 0
    A.att = [c.sbuf(f"aat{i}", [128, NB], BF16) for i in range(2)]; A.attB = [Buf(f"aat{i}") for i in range(2)]; A.ai = 0
    A.rd = c.sbuf("ard", [128, NB], F32); A.rdB = Buf("ard")
    A.to = c.sbuf("ato", [128, NB], F32); A.toB = Buf("ato")
    A.wuk = c.sbuf("awuk", [128, 2, 1024], BF16)
    A.wuv = c.sbuf("awuv", [128, 2, 1024], BF16)
    A.wB = Buf("awu")
    ND = NB // 128
    A.dm = c.sbuf("adm", [128, ND, NB], BF16); A.dmB = Buf("adm")
    c.dma("pool", A.wuk[:].rearrange("p a b -> p (a b)"), I["w_uk"][0], writes=[A.wB], sembuf=A.wB)
    c.dma("pool", A.wuv[:].rearrange("p a b -> p (a b)"), I["w_uv"][0], writes=[A.wB], sembuf=A.wB)
    c.dma("sp", A.dm[:].rearrange("p a b -> p (a b)"), I["dmask"], writes=[A.dmB], sembuf=A.dmB)
    c.op("dve", lambda e: e.memset(A.VT[:, :, 64:65], 1.0), writes=[A.VTB])
    for j in range(5):
        for ci in range(g.NCH):
            src = S["g2_out"][ci, j * 288:(j + 1) * 288, :] if j < 4 else S["g2_in"][ci]
            rd_ = [g.g2B[ci]] if j < 4 else [g.DB["g2_in"]]
            cs_ = j * TP + ci * 256
            c.dma("sp", A.CK[:, :, cs_:cs_ + 256], src[0:256, :].rearrange("(k p) t -> p k t", p=128), reads=rd_, writes=[A.CKB], sembuf=A.CKB)
            c.dma("sp", A.KT[64:96, cs_:cs_ + 256], src[256:288, :], reads=rd_, writes=[A.KTB], sembuf=A.KTB)
    for h in range(16):
        kv_produce(g, A, h, NKP)
        c.dma("sp", A.Q[0:96, 0:NT], S["qT"][h], reads=[g.DB["qT"]], writes=[A.QB], sembuf=A.QB)
        for qb in range(TP // NB):
            c0 = qb * NB
            tiles = []
            for j in range(4):
                for t in range(TP // 128):
                    tiles.append((j * TP + t * 128, 128, g.cst[:, 192 + 16 + j:192 + 17 + j], None))
            for t in range((c0 + NB) // 128):
                if t * 128 < c0:
                    tiles.append((4 * TP + t * 128, 128, None, None))
                else:
                    tiles.append((4 * TP + t * 128, 128, None, A.dm[:, (t * 128 - c0) // 128, :]))
            attend(g, A, A.Q[0:96, c0:c0 + NB], NB, tiles, S["attT"][h, :, c0:c0 + NB])
    for s_ in range(NSEQ):
        c.dma("pool", A.CK[:, :, 0:PAST], I["cache_ckvT"][s_].rearrange("(k p) t -> p k t", p=128), writes=[A.CKB], sembuf=A.CKB)
        c.dma("sp", A.CK[:, :, PAST:PAST + SL], S["skv"][0:256, s_ * SL:(s_ + 1) * SL].rearrange("(k p) t -> p k t", p=128),
              reads=[g.DB["skv"]], writes=[A.CKB], sembuf=A.CKB)
        c.dma("pool", A.KT[64:96, 0:PAST], I["cache_kpeT"][s_], writes=[A.KTB], sembuf=A.KTB)
        c.dma("sp", A.KT[64:96, PAST:PAST + SL], S["skv"][256:288, s_ * SL:(s_ + 1) * SL], reads=[g.DB["skv"]], writes=[A.KTB], sembuf=A.KTB)
        qv = A.Q[0:96, 0:16 * SL].rearrange("p (h t) -> p h t", h=16)
        c.dma("sp", qv, S["qT"][:, :, TP + s_ * SL:TP + (s_ + 1) * SL].rearrange("h d t -> d h t"), reads=[g.DB["qT"]], writes=[A.QB], sembuf=A.QB)
        for h in range(16):
            kv_produce(g, A, h, NKS)
            tiles = [(t * 128, min(128, NKS - t * 128), None, None) for t in range((NKS + 127) // 128)]
            attend(g, A, qv[:, h, :], SL, tiles, S["attT"][h, :, TP + s_ * SL:TP + (s_ + 1) * SL])


def phase3b(g):
    c, I, S = g.c, g.I, g.S
    TP, NB = g.TP, g.NB
    g.nrot = 8
    alloc_main(g)
    atb = c.sbuf("atb", [128, 16, NB], BF16)
    atB = Buf("atb")
    sgb = ActT(g, "sgb3", 8, NB, BF16)
    for bi, (c0, n, kind) in enumerate(g.blocks):
        load_fm(g, g.xf, S["x4T"], c0, n)
        c.dma("sp", atb[0:64, :, 0:n], S["attT"][:, :, c0:c0 + n].rearrange("h d t -> d h t"), reads=[g.DB["attT"]], writes=[atB], sembuf=atB)
        load_fm(g, sgb, S["sgoT"], c0, n)
        for ch in range(4):
            wa, waB = loadw(g, I["w_out_o_a"][ch], rows=64)
            ws, wsB = loadw(g, I["w_out_o_s"][ch])
            wav = wa[:, 0:16 * 512].rearrange("p (k m) -> p k m", k=16)
            wsv = ws[:, 0:8 * 512].rearrange("p (k m) -> p k m", k=8)
            for mi in range(4):
                bk, bB = bank(g)
                for h in range(16):
                    mm(g, bk[:, 0:n], wav[0:64, h, mi * 128:(mi + 1) * 128], atb[0:64, h, 0:n], h == 0, False, [waB, atB], bB)
                for k in range(8):
                    mm(g, bk[:, 0:n], wsv[:, k, mi * 128:(mi + 1) * 128], sgb.t[:, k, 0:n], False, k == 7, [wsB, sgb.B[k]], bB)
                m = ch * 4 + mi
                c.op("dve", lambda e, o=g.xf.t[:, m, 0:n], a_=bk[:, 0:n]: e.scalar_tensor_tensor(out=o, in0=a_, scalar=1.0 / ALPHA, in1=o, op0=ALU.mult, op1=ALU.add),
                     reads=[bB], writes=[g.xf.B[m]])
        layernorm(g, g.xf, g.xb, g.hb, g.st, n, 4)
        ffn(g, 1, 1, g.xf, g.xb, g.hb, g.gt, n)
        layernorm(g, g.xf, g.xb, g.hb, g.st, n, 5)
        store_fm(g, g.xf, "yT", c0, n)


def _chunked(W, kt, mc):
    Kd, Md = W.shape
    nch = Md // mc
    return np.ascontiguousarray(W.reshape(kt, 128, nch, mc).transpose(2, 1, 0, 3).reshape(nch, 128, kt * mc))


def _sm(v):
    return np.ascontiguousarray(np.asarray(v, np.float32).reshape(16, 128).T)


def prep_shared(inp, TP, PAST, NB):
    f = lambda a: np.asarray(a, np.float32)
    P = {}
    for l in range(2):
        for fi, nm in enumerate(["ffn1", "ffn2"]):
            w1 = f(inp[nm + "_w1"][l]).reshape(D, 22, 256)
            w3 = f(inp[nm + "_w3"][l]).reshape(D, 22, 256)
            P[f"w13_{l}{fi}"] = _chunked(np.concatenate([w1, w3], axis=2).reshape(D, 22 * 512), KT, 512)
            P[f"w2_{l}{fi}"] = _chunked(f(inp[nm + "_w2"][l]), FT, 128)
    P["lng"] = np.ascontiguousarray(f(inp["ln_g"]).reshape(6, KT, 128).transpose(2, 0, 1).reshape(128, 96))
    P["lnb"] = np.ascontiguousarray(f(inp["ln_b"]).reshape(6, KT, 128).transpose(2, 0, 1).reshape(128, 96))
    P["w_in_e"] = _chunked(f(inp["w_in_e"][0]), KT, 512)
    P["w_out_e"] = _chunked(f(inp["w_out_e"][0]), KT, 512)
    pw = f(inp["pool_w"][0])
    P["pool_w"] = np.ascontiguousarray(pw.reshape(4, 3, 128, 384).transpose(2, 0, 1, 3).reshape(1, 128, 12 * 384))
    P["pool_scale"] = np.ascontiguousarray(f(inp["pool_scale"][0]).reshape(12, 128).T)
    bre, bim = f(inp["ssm_b_re"][0]), f(inp["ssm_b_im"][0])
    cre, cim = f(inp["ssm_c_re"][0]), f(inp["ssm_c_im"][0])
    Bre = np.zeros((128, 16, 128), np.float32); Bim = np.zeros_like(Bre)
    Cre = np.zeros((128, 16, 128), np.float32); Cim = np.zeros_like(Cre)
    for gg in range(32):
        i = gg // 2
        r0 = (gg % 8) * 16
        c0 = (gg % 2) * 64
        Bre[r0:r0 + 16, i, c0:c0 + 64] = bre[gg].T
        Bim[r0:r0 + 16, i, c0:c0 + 64] = bim[gg].T
        Cre[c0:c0 + 64, i, r0:r0 + 16] = cre[gg].T
        Cim[c0:c0 + 64, i, r0:r0 + 16] = cim[gg].T
    P["ssm_Bre"], P["ssm_Bim"] = Bre.reshape(128, -1), Bim.reshape(128, -1)
    P["ssm_Cre"], P["ssm_Cim"] = Cre.reshape(128, -1), Cim.reshape(128, -1)
    P["ssm_lre"] = _sm(f(inp["ssm_lam_re"][0]).reshape(-1))
    P["ssm_lim"] = _sm(f(inp["ssm_lam_im"][0]).reshape(-1))
    P["ssm_ldt"] = _sm(np.repeat(f(inp["ssm_log_dt"][0]), 64))
    P["ssm_d"] = np.ascontiguousarray(f(inp["ssm_d"][0]).reshape(4, 128).T)
    P["w_glu"] = _chunked(f(inp["ssm_w_glu"][0]), 4, 512)
    P["b_glu"] = np.ascontiguousarray(f(inp["ssm_b_glu"][0]).reshape(4, 128).T)
    wo = f(inp["w_in_o"][0])
    P["w_in_o_cq"] = _chunked(wo[:, 0:512], KT, 512)
    kpe = wo[:, 768:800]
    sw = kpe[:, (np.arange(32) + 16) % 32]
    P["w_in_o_kv"] = _chunked(np.concatenate([wo[:, 512:768], kpe, sw], axis=1), KT, 320)
    P["w_in_o_u"] = _chunked(wo[:, 800:1824], KT, 512)
    P["w_in_o_v"] = _chunked(wo[:, 1824:2848], KT, 512)
    P["g_q"] = np.ascontiguousarray(f(inp["mla_g_q"][0]).reshape(4, 128).T)
    P["g_kv"] = np.ascontiguousarray(f(inp["mla_g_kv"][0]).reshape(2, 128).T)
    uq = f(inp["mla_w_uq"][0])
    uqs = uq[:, :, 64 + (np.arange(32) + 16) % 32]
    P["w_uq"] = _chunked(np.concatenate([uq, uqs], axis=2).reshape(512, 16 * 128), 4, 1024)
    P["w_uk"] = _chunked(f(inp["mla_w_uk"][0]).reshape(256, 1024), 2, 1024)
    P["w_uv"] = _chunked(f(inp["mla_w_uv"][0]).reshape(256, 1024), 2, 1024)
    P["sg_gv"] = np.ascontiguousarray(np.broadcast_to(f(inp["sg_g_v"][0])[None, :], (128, 1024)))
    P["sg_bv"] = np.ascontiguousarray(np.broadcast_to(f(inp["sg_b_v"][0])[None, :], (128, 1024)))
    ws = f(inp["sg_w_s"][0])
    P["sg_wT"] = np.ascontiguousarray(ws.transpose(2, 0, 1).reshape(128, 8 * 128))
    w64 = ws[:, 0:64, 0:64].transpose(2, 0, 1)
    P["sg_wT64"] = np.ascontiguousarray(np.concatenate([w64, w64], axis=0).reshape(128, 8 * 64))
    bs = f(inp["sg_b_s"][0])
    P["sg_bs"] = np.ascontiguousarray(np.broadcast_to(bs.reshape(1, 8 * 128), (128, 8 * 128)))
    bs64 = np.concatenate([bs[:, 0:64], bs[:, 0:64]], axis=1)
    P["sg_bs64"] = np.ascontiguousarray(np.broadcast_to(bs64.reshape(1, 8 * 128), (128, 8 * 128)))
    ss, tt = np.meshgrid(np.arange(128), np.arange(128), indexing="ij")
    P["tril"] = (ss <= tt).astype(np.float32)
    nd = NB // 128
    t_ = np.arange(128)[:, None, None]
    d_ = np.arange(nd)[None, :, None]
    q_ = np.arange(NB)[None, None, :]
    P["dmask"] = (((d_ * 128 + t_) // 64) <= (q_ // 64)).astype(np.float32).reshape(128, nd * NB).astype(ml_dtypes.bfloat16)
    woo = f(inp["w_out_o"][0])
    P["w_out_o_a"] = np.ascontiguousarray(woo[0:1024].reshape(16, 64, 4, 512).transpose(2, 1, 0, 3).reshape(4, 64, 16 * 512))
    P["w_out_o_s"] = _chunked(woo[1024:2048], 8, 512)
    return P


def prep_core(inp, P, cid, TP, PAST, NB):
    f = lambda a: np.asarray(a, np.float32)
    b, r = cid // 4, cid % 4
    NT = TP + NSEQ * SL
    m = dict(P)
    xp = f(inp["x_prompt"][b, r * TP:(r + 1) * TP])
    xs = f(inp["x_sample"][NSEQ * cid:NSEQ * (cid + 1)]).reshape(NSEQ * SL, D)
    m["xT"] = np.ascontiguousarray(np.concatenate([xp, xs], axis=0).T)
    corr = np.ones((128, 12, 16), np.float32)
    if r == 0:
        for t12 in range(12):
            w = 2 ** (t12 // 3 + 1)
            corr[:, t12, :] = w / np.minimum(np.arange(16) + 1, w)
    m["pool_corr"] = corr.reshape(128, -1)
    m["cache_poolT"] = np.ascontiguousarray(f(inp["cache_pool"][0, NSEQ * cid:NSEQ * (cid + 1)]).transpose(2, 0, 1))
    hre = f(inp["state_ssm_re"][0, NSEQ * cid:NSEQ * (cid + 1)]).reshape(NSEQ, 16, 128)
    him = f(inp["state_ssm_im"][0, NSEQ * cid:NSEQ * (cid + 1)]).reshape(NSEQ, 16, 128)
    m["ssm_h0re"] = np.ascontiguousarray(hre.transpose(2, 1, 0).reshape(128, 16 * NSEQ))
    m["ssm_h0im"] = np.ascontiguousarray(him.transpose(2, 1, 0).reshape(128, 16 * NSEQ))
    meta = np.zeros((128, 32), np.float32)
    for j in range(4):
        meta[:, j] = 1.0 if j == r - 1 else 0.0
        for d in range(1, 4):
            meta[:, 4 + (d - 1) * 4 + j] = 1.0 if j == r - d else 0.0
        meta[:, 16 + j] = 0.0 if j < r else NEGB
    m["meta"] = meta
    pos = np.concatenate([r * TP + np.arange(TP), np.tile(PAST + np.arange(SL), NSEQ)]).astype(np.float32)
    inv = (10000.0 ** (-np.arange(16, dtype=np.float32) / 16)).astype(np.float32)
    ang = pos[None, :] * np.concatenate([inv, inv])[:, None]
    rope = np.zeros((128, 2, NT), np.float32)
    rope[64:96, 0] = np.cos(ang)
    sn = np.sin(ang)
    sn[0:16] *= -1.0
    rope[64:96, 1] = sn
    m["rope"] = rope
    m["cache_ckvT"] = np.ascontiguousarray(f(inp["cache_ckv"][0, NSEQ * cid:NSEQ * (cid + 1)]).transpose(0, 2, 1))
    m["cache_kpeT"] = np.ascontiguousarray(f(inp["cache_kpe"][0, NSEQ * cid:NSEQ * (cid + 1)]).transpose(0, 2, 1))
    return m


def run(inp, TP, PAST, NB, phases=(1, 2, 3, 4), noffn=False):
    nc = build(TP, PAST, NB, phases, noffn)
    P = prep_shared(inp, TP, PAST, NB)
    if noffn:
        P = {k: v for k, v in P.items() if not (k.startswith("w13_") or k.startswith("w2_"))}
    maps = [prep_core(inp, P, cid, TP, PAST, NB) for cid in range(8)]
    res = run_bass_kernel_spmd(nc, maps, core_ids=list(range(8)))
    return res.results


def assemble(R, TP, PAST):
    NT = TP + NSEQ * SL
    S = 4 * TP
    yp = np.zeros((2, S, D), np.float32); ys = np.zeros((32, SL, D), np.float32)
    ckp = np.zeros((1, 2, S, 256), np.float32); kpp = np.zeros((1, 2, S, 32), np.float32)
    cks = np.zeros((1, 32, SL, 256), np.float32); kps = np.zeros((1, 32, SL, 32), np.float32)
    pp = np.zeros((1, 2, 15, 1536), np.float32); ps = np.zeros((1, 32, 15, 1536), np.float32)
    srp = np.zeros((1, 2, 32, 64), np.float32); sip = np.zeros_like(srp)
    srs = np.zeros((1, 32, 32, 64), np.float32); sis = np.zeros_like(srs)
    sgv = np.zeros((1, 32, SL, 1024), np.float32)
    for cid in range(8):
        b, r = cid // 4, cid % 4
        o = R[cid]
        yT = o["yT"]
        yp[b, r * TP:(r + 1) * TP] = yT[:, :TP].T
        ys[NSEQ * cid:NSEQ * (cid + 1)] = yT[:, TP:].T.reshape(NSEQ, SL, D)
        ckp[0, b, r * TP:(r + 1) * TP] = o["ckvT"][:, :TP].T
        kpp[0, b, r * TP:(r + 1) * TP] = o["kpeT"][:, :TP].T
        cks[0, NSEQ * cid:NSEQ * (cid + 1)] = o["ckvT"][:, TP:].T.reshape(NSEQ, SL, 256)
        kps[0, NSEQ * cid:NSEQ * (cid + 1)] = o["kpeT"][:, TP:].T.reshape(NSEQ, SL, 32)
        ps[0, NSEQ * cid:NSEQ * (cid + 1)] = o["poolT_s"].transpose(1, 2, 0)
        ss = o["ssm_s"].reshape(128, 2, 16, NSEQ)
        srs[0, NSEQ * cid:NSEQ * (cid + 1)] = ss[:, 0].transpose(2, 1, 0).reshape(NSEQ, 32, 64)
        sis[0, NSEQ * cid:NSEQ * (cid + 1)] = ss[:, 1].transpose(2, 1, 0).reshape(NSEQ, 32, 64)
        sgv[0, NSEQ * cid:NSEQ * (cid + 1)] = o["sgv"].reshape(NSEQ, SL, 1024)
        if r == 3:
            pp[0, b] = o["poolT_p"].T
            sp_ = o["ssm_p"]
            srp[0, b] = sp_[:, 0].T.reshape(32, 64)
            sip[0, b] = sp_[:, 1].T.reshape(32, 64)
    return (yp, ys, pp, srp, sip, ckp, kpp, ps, srs, sis, cks, kps, sgv)


def kernel(**inputs):
    R = run(inputs, 4096, 4096, 256)
    return assemble(R, 4096, 4096)
```

```python
import math
import os
import numpy as np
import ml_dtypes
import concourse.bass as bass
import concourse.mybir as mybir
from concourse.bass_utils import run_bass_kernel_spmd

F32 = mybir.dt.float32
BF16 = mybir.dt.bfloat16
AF = mybir.ActivationFunctionType
ALU = mybir.AluOpType

D = 2048
KT = 16
DFF = 5632
FT = 44
DEPTH = 2
ALPHA = (2 * DEPTH) ** 0.25
LN_EPS = 1e-5
RMS_EPS = 1e-6
EPS_LN = LN_EPS / (ALPHA * ALPHA)
ATTN_SCALE = 96 ** -0.5
NEGB = -30000.0
SSM_L = 64
NSEQ = 4
SL = 64


class Buf:
    __slots__ = ("name", "w", "r", "sem", "cnt")

    def __init__(self, name):
        self.name = name
        self.w = None
        self.r = []
        self.sem = None
        self.cnt = 0


class Ctx:
    def __init__(self, nc):
        self.nc = nc
        self.stack = []
        self.semstack = []
        self.eng = ["pe", "act", "dve", "pool", "sp"]
        self.esem = {}
        self.ecnt = {}
        self.seen = {k: {} for k in self.eng}
        self.prog = {k: [] for k in self.eng}
        self.dmabufs = []
        for k in self.eng:
            self.esem[k] = self.enter_sem(nc.semaphore("es_" + k))
            self.ecnt[k] = 0

    def enter(self, cm):
        v = cm.__enter__()
        self.stack.append(cm)
        return v

    def enter_sem(self, cm):
        v = cm.__enter__()
        self.semstack.append(cm)
        return v

    def mark(self):
        return len(self.stack)

    def release(self, m):
        while len(self.stack) > m:
            self.stack.pop().__exit__(None, None, None)

    def uq(self, name):
        self.uid = getattr(self, "uid", 0) + 1
        return f"{name}_{self.uid}"

    def sbuf(self, name, shape, dt):
        return self.enter(self.nc.sbuf_tensor(self.uq(name), shape, dt))

    def psum(self, name, shape, dt=F32):
        return self.enter(self.nc.psum_tensor(name, shape, dt))

    def _deps(self, e, reads, writes):
        toks = []
        for b in reads:
            if b.w is not None:
                toks.append(b.w)
        for b in writes:
            if b.w is not None:
                toks.append(b.w)
            toks.extend(b.r)
        seen = self.seen[e]
        need = {}
        for (s, v) in toks:
            key = id(s)
            if seen.get(key, 0) >= v:
                continue
            if key not in need or need[key][1] < v:
                need[key] = (s, v)
        for key, (s, v) in need.items():
            seen[key] = v
            self.prog[e].append(("wait", s, v))

    def _post(self, tok, reads, writes):
        for b in writes:
            b.w = tok
            b.r = []
        for b in reads:
            if b in writes:
                continue
            b.r.append(tok)
            if len(b.r) > 16:
                best = {}
                for (s, v) in b.r:
                    k = id(s)
                    if k not in best or best[k][1] < v:
                        best[k] = (s, v)
                b.r = list(best.values())

    def op(self, e, fn, reads=(), writes=(), inc=True, wnodep=()):
        self._deps(e, reads, writes)
        writes = list(writes) + list(wnodep)
        if inc:
            if self.ecnt[e] >= 30000:
                self.esem[e] = self.enter_sem(self.nc.semaphore(self.uq("es_" + e)))
                self.ecnt[e] = 0
            self.ecnt[e] += 1
            tok = (self.esem[e], self.ecnt[e])
            self.prog[e].append(("op", fn, self.esem[e], 1))
        else:
            if self.ecnt[e] >= 30000:
                self.esem[e] = self.enter_sem(self.nc.semaphore(self.uq("es_" + e)))
                self.ecnt[e] = 0
            tok = (self.esem[e], self.ecnt[e] + 1)
            self.prog[e].append(("op", fn, None, 0))
        self._post(tok, reads, writes)
        return tok

    def dma(self, e, out, in_, reads=(), writes=(), sembuf=None):
        self._deps(e, reads, writes)
        b = sembuf
        if b.sem is None or b.cnt >= 30000:
            b.sem = self.enter_sem(self.nc.semaphore(self.uq("ds_" + b.name)))
            b.cnt = 0
            if b not in self.dmabufs:
                self.dmabufs.append(b)
        b.cnt += 16
        tok = (b.sem, b.cnt)
        self.prog[e].append(("op", (lambda eng, o=out, i=in_: eng.dma_start(out=o, in_=i)), b.sem, 16))
        self._post(tok, reads, writes)
        return tok

    def collective(self, ins, outs, groups, reads, writes, sembuf):
        e = "pool"
        self._deps(e, reads, writes)
        b = sembuf
        b.sem = self.enter_sem(self.nc.semaphore(self.uq("cs_" + b.name)))
        b.cnt = 1
        tok = (b.sem, 1)
        self.prog[e].append(("op", (lambda eng: eng.collective_compute(
            "AllGather", ALU.bypass, replica_groups=groups, ins=[ins], outs=[outs])), b.sem, 1))
        self._post(tok, reads, writes)

    def barrier(self):
        toks = [(self.esem[k], self.ecnt[k]) for k in self.eng if self.ecnt[k] > 0]
        toks += [(b.sem, b.cnt) for b in self.dmabufs]
        for e in self.eng:
            for (s, v) in toks:
                if self.seen[e].get(id(s), 0) < v:
                    self.seen[e][id(s)] = v
                    self.prog[e].append(("wait", s, v))

    def emit(self):
        nc = self.nc
        with nc.Block() as block:
            def mk(ename):
                def body(eng):
                    for it in self.prog[ename]:
                        if it[0] == "wait":
                            eng.wait_ge(it[1], it[2])
                        else:
                            ins = it[1](eng)
                            if it[2] is not None:
                                ins.then_inc(it[2], it[3])
                return body
            block.tensor(mk("pe"))
            block.scalar(mk("act"))
            block.vector(mk("dve"))
            block.gpsimd(mk("pool"))
            block.sync(mk("sp"))


class K:
    pass


def _dram(nc, name, shape, dt=F32, kind="ExternalInput"):
    if kind is None:
        return nc.dram_tensor(name, list(shape), dt).ap()
    return nc.dram_tensor(name, list(shape), dt, kind=kind).ap()


def build(TP, PAST, NB, phases=(1, 2, 3, 4), noffn=False):
    nc = bass.Bass("TRN2", target_bir_lowering=False)
    c = Ctx(nc)
    g = K()
    g.nc, g.c, g.TP, g.PAST, g.NB = nc, c, TP, PAST, NB
    g.noffn = noffn
    NT = TP + NSEQ * SL
    g.NT = NT
    assert TP % NB == 0 and NSEQ * SL <= NB and NB % SSM_L == 0
    blocks = [(i * NB, NB, "p") for i in range(TP // NB)] + [(TP, NSEQ * SL, "s")]
    g.blocks = blocks

    I = {}

    def inp(name, shape, dt=F32):
        I[name] = _dram(nc, name, shape, dt)
        return I[name]

    inp("xT", [D, NT])
    for l in range(2):
        for f in range(2):
            if not noffn:
                inp(f"w13_{l}{f}", [22, 128, KT * 512])
                inp(f"w2_{l}{f}", [16, 128, FT * 128])
    inp("lng", [128, 6 * KT])
    inp("lnb", [128, 6 * KT])
    inp("w_in_e", [4, 128, KT * 512])
    inp("w_out_e", [4, 128, KT * 512])
    inp("pool_w", [1, 128, 12 * 384])
    inp("pool_scale", [128, 12])
    inp("pool_corr", [128, 12 * 16])
    inp("cache_poolT", [1536, NSEQ, 15])
    inp("ssm_Bre", [128, 16 * 128])
    inp("ssm_Bim", [128, 16 * 128])
    inp("ssm_Cre", [128, 16 * 128])
    inp("ssm_Cim", [128, 16 * 128])
    inp("ssm_lre", [128, 16])
    inp("ssm_lim", [128, 16])
    inp("ssm_ldt", [128, 16])
    inp("ssm_d", [128, 4])
    inp("ssm_h0re", [128, 16 * NSEQ])
    inp("ssm_h0im", [128, 16 * NSEQ])
    inp("w_glu", [1, 128, 4 * 512])
    inp("b_glu", [128, 4])
    inp("meta", [128, 32])
    inp("w_in_o_cq", [1, 128, KT * 512])
    inp("w_in_o_kv", [1, 128, KT * 320])
    inp("w_in_o_u", [2, 128, KT * 512])
    inp("w_in_o_v", [2, 128, KT * 512])
    inp("g_q", [128, 4])
    inp("g_kv", [128, 2])
    inp("w_uq", [2, 128, 4 * 1024])
    inp("w_uk", [1, 128, 2 * 1024])
    inp("w_uv", [1, 128, 2 * 1024])
    inp("rope", [128, 2, NT])
    inp("sg_gv", [128, 1024])
    inp("sg_bv", [128, 1024])
    inp("sg_wT", [128, 8 * 128])
    inp("sg_wT64", [128, 8 * 64])
    inp("sg_bs", [128, 8 * 128])
    inp("sg_bs64", [128, 8 * 128])
    inp("tril", [128, 128])
    inp("dmask", [128, (NB // 128) * NB], BF16)
    inp("w_out_o_a", [4, 64, 16 * 512])
    inp("w_out_o_s", [4, 128, 8 * 512])
    inp("cache_ckvT", [NSEQ, 256, PAST])
    inp("cache_kpeT", [NSEQ, 32, PAST])

    O = {}

    def outp(name, shape):
        O[name] = _dram(nc, name, shape, F32, kind="ExternalOutput")
        return O[name]

    outp("yT", [D, NT])
    outp("poolT_p", [1536, 15])
    outp("poolT_s", [1536, NSEQ, 15])
    outp("ssm_p", [128, 2, 16])
    outp("ssm_s", [128, 2, 16 * NSEQ])
    outp("ckvT", [256, NT])
    outp("kpeT", [32, NT])
    outp("sgv", [NSEQ * SL, 1024])

    S = {}

    def scr(name, shape, dt=F32):
        S[name] = _dram(nc, name, shape, dt, kind=None)
        return S[name]

    scr("x1T", [D, NT])
    scr("uT", [D, NT])
    scr("x4T", [D, NT])
    scr("qT", [16, 96, NT], BF16)
    scr("sgoT", [1024, NT], BF16)
    scr("attT", [16, 64, NT], BF16)
    scr("g1_in", [1, 128 * 32 + 1536 * 15])
    scr("g1_out", [4, 128 * 32 + 1536 * 15])
    scr("skv", [288, NSEQ * SL], BF16)
    assert NB == 256
    g.NCH = TP // 256
    scr("g2_in", [g.NCH, 288, 256], BF16)
    scr("g2_out", [g.NCH, 4 * 288, 256], BF16)
    g.g2B = [Buf(f"g2c{i}") for i in range(g.NCH)]
    g.I, g.O, g.S = I, O, S
    g.DB = {k: Buf("d_" + k) for k in list(O) + list(S)}

    g.banks = [c.psum(f"bank{i}", [128, 512]) for i in range(8)]
    g.bankB = [Buf(f"bank{i}") for i in range(8)]
    g.bi = 0
    g.nrot = 8
    g.NW = 2
    g.wbuf = [c.sbuf(f"wbuf{i}", [128, 8192], BF16) for i in range(g.NW)]
    g.wB = [Buf(f"wbuf{i}") for i in range(g.NW)]
    g.wi = 0
    g.cst = c.sbuf("cst", [128, 6 * KT * 2 + 64], F32)
    g.cstB = Buf("cst")
    g.ones = c.sbuf("ones", [128, 128], BF16)
    g.onesB = Buf("ones")
    g.onesf = c.sbuf("onesf", [128, 128], F32)
    c.dma("sp", g.cst[:, 0:96], I["lng"], writes=[g.cstB], sembuf=g.cstB)
    c.dma("sp", g.cst[:, 96:192], I["lnb"], writes=[g.cstB], sembuf=g.cstB)
    c.dma("sp", g.cst[:, 192:224], I["meta"], writes=[g.cstB], sembuf=g.cstB)
    c.op("dve", lambda e: e.memset(g.onesf[:], 1.0), writes=[g.onesB])
    c.op("dve", lambda e: e.tensor_copy(out=g.ones[:], in_=g.onesf[:]), reads=[g.onesB], writes=[g.onesB])

    m0 = c.mark()
    if 1 in phases:
        phase1(g)
        c.barrier()
        c.release(m0)
    if 2 in phases:
        phase2(g)
        c.barrier()
        c.release(m0)
        phase2b(g)
        c.barrier()
        c.release(m0)
    if 3 in phases:
        phase3a(g)
        c.barrier()
        c.release(m0)
    if 4 in phases:
        phase3b(g)
    outs = [g.DB[k] for k in O]
    toks = []
    for b in outs:
        if b.w is not None:
            toks.append(b.w)
        toks.extend(b.r)
    for (s, v) in toks:
        if c.seen["sp"].get(id(s), 0) < v:
            c.seen["sp"][id(s)] = v
            c.prog["sp"].append(("wait", s, v))
    c.emit()
    c.release(0)
    while c.semstack:
        c.semstack.pop().__exit__(None, None, None)
    return nc


def bank(g):
    i = g.bi
    g.bi = (g.bi + 1) % g.nrot
    return g.banks[i], g.bankB[i]


def loadw(g, dram_ap, rows=128):
    i = g.wi
    g.wi = (g.wi + 1) % g.NW
    X = dram_ap.shape[-1]
    t = g.wbuf[i]
    g.c.dma("pool", t[0:rows, 0:X], dram_ap, writes=[g.wB[i]], sembuf=g.wB[i])
    return t, g.wB[i]


def mm(g, out, lhsT, rhs, first, last, reads, wbuf):
    c = g.c
    fn = lambda e, o=out, l=lhsT, r=rhs, f=first, s=last: e.matmul(o, l, r, start=f, stop=s)
    if first:
        c.op("pe", fn, reads=reads, writes=[wbuf], inc=last)
    else:
        c.op("pe", fn, reads=reads, wnodep=[wbuf], inc=last)


class ActT:
    def __init__(self, g, name, nk, n, dt):
        self.t = g.c.sbuf(name, [128, nk, n], dt)
        self.B = [Buf(f"{name}{k}") for k in range(nk)]
        self.nk = nk


def load_fm(g, a, dram, c0, n, k0=0, nk=None, col_off=0):
    nk = a.nk if nk is None else nk
    src = dram.rearrange("(k p) t -> p k t", p=128)[:, :, c0:c0 + n]
    g.c.dma("sp", a.t[:, k0:k0 + nk, col_off:col_off + n], src, writes=a.B[k0:k0 + nk], sembuf=a.B[k0])


def store_fm(g, a, dname, c0, n, k0=0, nk=None, r0=0):
    nk = a.nk if nk is None else nk
    dram = g.O[dname] if dname in g.O else g.S[dname]
    dst = dram[r0:r0 + nk * 128, :].rearrange("(k p) t -> p k t", p=128)[:, :, c0:c0 + n]
    g.c.dma("sp", dst, a.t[:, k0:k0 + nk, 0:n], reads=a.B[k0:k0 + nk], writes=[g.DB[dname]], sembuf=g.DB[dname])


def linear(g, wname, nch, kt, mc, rhs, rhsB, n, evac, mw=128, krows=128, wrows=128):
    wd = g.I[wname]
    for ch in range(nch):
        wt, wB = loadw(g, wd[ch], rows=wrows)
        wv = wt[:, 0:kt * mc].rearrange("p (k m) -> p k m", k=kt)
        for mi in range(mc // mw):
            bk, bB = bank(g)
            for k in range(kt):
                mm(g, bk[0:mw, 0:n], wv[0:krows, k, mi * mw:(mi + 1) * mw], rhs(k), k == 0, k == kt - 1,
                   [wB, rhsB[k]], bB)
            evac(ch * (mc // mw) + mi, bk, bB)


def cview(g, off, w):
    return g.cst[:, off:off + w]


def ffn(g, l, f, xf, xb, hb, gt, n):
    c = g.c
    if g.noffn:
        return
    gtB = g.gtB
    wd13 = g.I[f"w13_{l}{f}"]
    for ch in range(22):
        wt, wB = loadw(g, wd13[ch])
        wv = wt[:, 0:KT * 512].rearrange("p (k m) -> p k m", k=KT)
        for mi in range(2):
            ba, bAB = bank(g)
            for k in range(KT):
                mm(g, ba[:, 0:n], wv[:, k, mi * 128:(mi + 1) * 128], xb.t[:, k, 0:n], k == 0, k == KT - 1,
                   [wB, xb.B[k]], bAB)
            bb, bBB = bank(g)
            for k in range(KT):
                mm(g, bb[:, 0:n], wv[:, k, 256 + mi * 128:256 + (mi + 1) * 128], xb.t[:, k, 0:n], k == 0,
                   k == KT - 1, [wB, xb.B[k]], bBB)
            gi = g.gti
            g.gti = (g.gti + 1) % 2
            c.op("act", lambda e, o=gt[:, gi, 0:n], i=ba[:, 0:n]: e.activation(out=o, in_=i, func=AF.Silu),
                 reads=[bAB], writes=[gtB[gi]])
            m = ch * 2 + mi
            c.op("dve", lambda e, o=hb.t[:, m, 0:n], a=bb[:, 0:n], b=gt[:, gi, 0:n]: e.tensor_tensor(
                out=o, in0=a, in1=b, op=ALU.mult), reads=[bBB, gtB[gi]], writes=[hb.B[m]])
    wd2 = g.I[f"w2_{l}{f}"]
    for j in range(16):
        wt, wB = loadw(g, wd2[j])
        wv = wt[:, 0:FT * 128].rearrange("p (k m) -> p k m", k=FT)
        bk, bB = bank(g)
        for k in range(FT):
            mm(g, bk[:, 0:n], wv[:, k, :], hb.t[:, k, 0:n], k == 0, k == FT - 1, [wB, hb.B[k]], bB)
        c.op("dve", lambda e, o=xf.t[:, j, 0:n], a=bk[:, 0:n]: e.scalar_tensor_tensor(
            out=o, in0=a, scalar=0.5 / ALPHA, in1=o, op0=ALU.mult, op1=ALU.add), reads=[bB], writes=[xf.B[j]])


def colsum(g, src, srcB, nk, n, krows=128):
    bk, bB = bank(g)
    for k in range(nk):
        mm(g, bk[:, 0:n], g.ones[0:krows, :], src(k), k == 0, k == nk - 1, [g.onesB, srcB[k]], bB)
    return bk, bB


def layernorm(g, xf, xb, sq, st, n, gi):
    c = g.c
    stB = g.stB
    for k in range(KT):
        c.op("act", lambda e, o=xb.t[:, k, 0:n], i=xf.t[:, k, 0:n]: e.activation(out=o, in_=i, func=AF.Copy),
             reads=[xf.B[k]], writes=[xb.B[k]])
        c.op("act", lambda e, o=sq.t[:, k, 0:n], i=xf.t[:, k, 0:n]: e.activation(out=o, in_=i, func=AF.Square),
             reads=[xf.B[k]], writes=[sq.B[k]])
    s1, s1B = colsum(g, lambda k: xb.t[:, k, 0:n], xb.B, KT, n)
    s2, s2B = colsum(g, lambda k: sq.t[:, k, 0:n], sq.B, KT, n)
    mean, msq, var, rstd, nmr = [st[:, i, 0:n] for i in range(5)]
    c.op("act", lambda e: e.activation(out=mean, in_=s1[:, 0:n], func=AF.Copy, scale=1.0 / D), reads=[s1B], writes=[stB[0]])
    c.op("dve", lambda e: e.tensor_tensor(out=msq, in0=mean, in1=mean, op=ALU.mult), reads=[stB[0]], writes=[stB[1]])
    c.op("dve", lambda e: e.scalar_tensor_tensor(out=var, in0=s2[:, 0:n], scalar=1.0 / D, in1=msq, op0=ALU.mult,
                                                 op1=ALU.subtract), reads=[s2B, stB[1]], writes=[stB[2]])
    c.op("dve", lambda e: e.tensor_scalar(out=var, in0=var, scalar1=EPS_LN, scalar2=None, op0=ALU.add),
         reads=[stB[2]], writes=[stB[2]])
    c.op("act", lambda e: e.activation(out=var, in_=var, func=AF.Sqrt), reads=[stB[2]], writes=[stB[2]])
    c.op("dve", lambda e: e.reciprocal(out=rstd, in_=var), reads=[stB[2]], writes=[stB[3]])
    c.op("dve", lambda e: e.scalar_tensor_tensor(out=nmr, in0=mean, scalar=-1.0, in1=rstd, op0=ALU.mult, op1=ALU.mult),
         reads=[stB[0], stB[3]], writes=[stB[4]])
    for k in range(KT):
        c.op("dve", lambda e, o=xf.t[:, k, 0:n]: e.tensor_tensor(out=o, in0=o, in1=rstd, op=ALU.mult),
             reads=[stB[3]], writes=[xf.B[k]])
        c.op("dve", lambda e, o=xf.t[:, k, 0:n]: e.tensor_tensor(out=o, in0=o, in1=nmr, op=ALU.add),
             reads=[stB[4]], writes=[xf.B[k]])
        gs = cview(g, gi * KT + k, 1)
        bs = cview(g, 96 + gi * KT + k, 1)
        c.op("act", lambda e, o=xf.t[:, k, 0:n], s_=gs, b_=bs: e.activation(out=o, in_=o, func=AF.Identity, scale=s_, bias=b_),
             reads=[g.cstB], writes=[xf.B[k]])
        c.op("act", lambda e, o=xb.t[:, k, 0:n], i=xf.t[:, k, 0:n]: e.activation(out=o, in_=i, func=AF.Copy),
             reads=[xf.B[k]], writes=[xb.B[k]])


def alloc_main(g):
    c, NB = g.c, g.NB
    g.xf = ActT(g, "xf", KT, NB, F32)
    g.xb = ActT(g, "xb", KT, NB, BF16)
    g.hb = ActT(g, "hb", FT, NB, BF16)
    g.gt = c.sbuf("gt", [128, 2, NB], F32)
    g.gtB = [Buf("gt0"), Buf("gt1")]
    g.gti = 0
    g.st = c.sbuf("st", [128, 6, NB], F32)
    g.stB = [Buf(f"st{i}") for i in range(6)]


def ssm_setup(g):
    c, I = g.c, g.I
    sB = g.sB = Buf("ssmc")
    L = SSM_L
    g.sv = c.sbuf("ssm_sv", [128, 40, 16], F32)
    sv = g.sv
    V = lambda i: sv[:, i, :]
    LRE, LIM, LDT, DT, LRDT, LIDT, MAG, CTH, STH, ABR, ABI, DEN, NR, COR, COI, T0, T1, T2, PC, PS_, APR, API, NEIL, ERL, EIL, HPI = range(26)
    g.SVI = dict(CTH=CTH, STH=STH, COR=COR, COI=COI, APR=APR, API=API, NEIL=NEIL, ERL=ERL, EIL=EIL, MAG=MAG, T0=T0, T1=T1, T2=T2)
    c.dma("sp", V(LRE), I["ssm_lre"], writes=[sB], sembuf=sB)
    c.dma("sp", V(LIM), I["ssm_lim"], writes=[sB], sembuf=sB)
    c.dma("sp", V(LDT), I["ssm_ldt"], writes=[sB], sembuf=sB)
    d = lambda fn: c.op("dve", fn, reads=[sB], writes=[sB])
    a = lambda fn: c.op("act", fn, reads=[sB], writes=[sB])
    tt = lambda o, x, y, op: d(lambda e: e.tensor_tensor(out=o, in0=x, in1=y, op=op))
    d(lambda e: e.memset(V(HPI), math.pi / 2))
    a(lambda e: e.activation(out=V(DT), in_=V(LDT), func=AF.Exp))
    tt(V(LRDT), V(LRE), V(DT), ALU.mult)
    tt(V(LIDT), V(LIM), V(DT), ALU.mult)
    a(lambda e: e.activation(out=V(MAG), in_=V(LRDT), func=AF.Exp))
    a(lambda e: e.activation(out=V(STH), in_=V(LIDT), func=AF.Sin, scale=1.0 / 16))
    d(lambda e: e.scalar_tensor_tensor(out=V(T0), in0=V(LIDT), scalar=1.0 / 16, in1=V(HPI), op0=ALU.mult, op1=ALU.add))
    a(lambda e: e.activation(out=V(CTH), in_=V(T0), func=AF.Sin))

    def csq(cr, ci):
        tt(V(T0), cr, cr, ALU.mult)
        tt(V(T1), ci, ci, ALU.mult)
        tt(V(T2), cr, ci, ALU.mult)
        tt(cr, V(T0), V(T1), ALU.subtract)
        tt(ci, V(T2), V(T2), ALU.add)
    for _ in range(4):
        csq(V(CTH), V(STH))
    tt(V(ABR), V(MAG), V(CTH), ALU.mult)
    tt(V(ABI), V(MAG), V(STH), ALU.mult)
    tt(V(T0), V(LRE), V(LRE), ALU.mult)
    tt(V(T1), V(LIM), V(LIM), ALU.mult)
    tt(V(DEN), V(T0), V(T1), ALU.add)
    d(lambda e: e.reciprocal(out=V(DEN), in_=V(DEN)))
    d(lambda e: e.tensor_scalar(out=V(NR), in0=V(ABR), scalar1=-1.0, scalar2=None, op0=ALU.add))
    tt(V(T0), V(NR), V(LRE), ALU.mult)
    tt(V(T1), V(ABI), V(LIM), ALU.mult)
    tt(V(T0), V(T0), V(T1), ALU.add)
    tt(V(COR), V(T0), V(DEN), ALU.mult)
    tt(V(T0), V(ABI), V(LRE), ALU.mult)
    tt(V(T1), V(NR), V(LIM), ALU.mult)
    tt(V(T0), V(T0), V(T1), ALU.subtract)
    tt(V(COI), V(T0), V(DEN), ALU.mult)
    d(lambda e: e.tensor_copy(out=V(APR), in_=V(ABR)))
    d(lambda e: e.tensor_copy(out=V(API), in_=V(ABI)))
    for _ in range(int(round(math.log2(g.TP)))):
        csq(V(APR), V(API))
    g.Er = c.sbuf("ssm_Er", [128, 16, L + 1], F32)
    g.Ei = c.sbuf("ssm_Ei", [128, 16, L + 1], F32)
    g.TCr = c.sbuf("ssm_TCr", [128, 16, L], F32)
    g.TC2 = c.sbuf("ssm_TC2", [128, 16, 2, L], F32)
    g.RT = c.sbuf("ssm_RT", [128, 16, L], F32)
    g.tqf = c.sbuf("ssm_tq", [128, 2048], F32)
    g.tq = tq4 = g.tqf[:, 0:2048].rearrange("p (a b w) -> p a b w", a=4, b=16)
    tq2 = g.tqf[:, 0:2048].rearrange("p (a b w) -> p a b w", a=2, b=16)
    Er, Ei = g.Er, g.Ei
    tq = tq4
    d(lambda e: e.memset(Er[:, :, 0:1], 1.0))
    d(lambda e: e.memset(Ei[:, :, 0:1], 0.0))
    d(lambda e: e.tensor_copy(out=V(PC), in_=V(CTH)))
    d(lambda e: e.tensor_copy(out=V(PS_), in_=V(STH)))
    w = 1
    while w <= L:
        ww = min(w, L + 1 - w)
        pcb = V(PC).unsqueeze(2).broadcast_to([128, 16, ww])
        psb = V(PS_).unsqueeze(2).broadcast_to([128, 16, ww])
        tt(tq[:, 0, :, 0:ww], Er[:, :, 0:ww], pcb, ALU.mult)
        tt(tq[:, 1, :, 0:ww], Ei[:, :, 0:ww], psb, ALU.mult)
        tt(tq[:, 2, :, 0:ww], Er[:, :, 0:ww], psb, ALU.mult)
        tt(tq[:, 3, :, 0:ww], Ei[:, :, 0:ww], pcb, ALU.mult)
        tt(Er[:, :, w:w + ww], tq[:, 0, :, 0:ww], tq[:, 1, :, 0:ww], ALU.subtract)
        tt(Ei[:, :, w:w + ww], tq[:, 2, :, 0:ww], tq[:, 3, :, 0:ww], ALU.add)
        csq(V(PC), V(PS_))
        w *= 2
    corb = V(COR).unsqueeze(2).broadcast_to([128, 16, L])
    coib = V(COI).unsqueeze(2).broadcast_to([128, 16, L])
    tt(tq2[:, 0], Er[:, :, 0:L], corb, ALU.mult)
    tt(tq2[:, 1], Ei[:, :, 0:L], coib, ALU.mult)
    tt(g.TCr[:], tq2[:, 0], tq2[:, 1], ALU.add)
    tt(tq2[:, 0], Er[:, :, 0:L], coib, ALU.mult)
    tt(tq2[:, 1], Ei[:, :, 0:L], corb, ALU.mult)
    tt(g.TC2[:, :, 1, :], tq2[:, 0], tq2[:, 1], ALU.subtract)
    d(lambda e, t_=g.TC2: e.tensor_scalar(out=t_[:, :, 0, :], in0=t_[:, :, 1, :], scalar1=-1.0, scalar2=None, op0=ALU.mult))
    d(lambda e, t_=g.RT: e.tensor_copy(out=t_[:], in_=V(MAG).unsqueeze(2).broadcast_to([128, 16, L])))
    d(lambda e: e.tensor_copy(out=V(ERL), in_=Er[:, :, L]))
    d(lambda e: e.tensor_copy(out=V(EIL), in_=Ei[:, :, L]))
    d(lambda e: e.tensor_scalar(out=V(NEIL), in0=Ei[:, :, L], scalar1=-1.0, scalar2=None, op0=ALU.mult))
    g.BR = c.sbuf("ssm_BR", [128, 16, 128], BF16)
    g.BI = c.sbuf("ssm_BI", [128, 16, 128], BF16)
    g.CR = c.sbuf("ssm_CR", [128, 16, 128], BF16)
    g.nCR = c.sbuf("ssm_nCR", [128, 16, 128], BF16)
    g.nCI = c.sbuf("ssm_nCI", [128, 16, 128], BF16)
    stg = g.tqf[:, 0:2048]
    for nm, dst, sc in (("ssm_Bre", [(g.BR, 1.0)], 0), ("ssm_Bim", [(g.BI, 1.0)], 0),
                        ("ssm_Cre", [(g.CR, 1.0), (g.nCR, -1.0)], 0), ("ssm_Cim", [(g.nCI, -1.0)], 0)):
        c.dma("sp", stg, I[nm], reads=[sB], writes=[sB], sembuf=sB)
        for (dt_, s_) in dst:
            d(lambda e, o=dt_, s_=s_: e.tensor_scalar(out=o[:].rearrange("p a b -> p (a b)"), in0=stg, scalar1=s_,
                                                        scalar2=None, op0=ALU.mult))
    g.carry = c.sbuf("ssm_carry", [128, 2, 16], F32)
    g.send = c.sbuf("ssm_send", [128, 2, 16], F32)
    g.sfin = c.sbuf("ssm_fin", [128, 2, 16, NSEQ], F32)
    g.h0s = c.sbuf("ssm_h0s", [128, 2, 16, NSEQ], F32)
    g.swk = [c.sbuf(f"ssm_wk{i}", [128, 8, L], F32) for i in range(3)]
    g.swq = [c.sbuf(f"ssm_wq{i}", [128, 4, L], BF16) for i in range(3)]
    g.swB = [Buf(f"swk{i}") for i in range(3)]
    g.swi = 0
    g.ct = c.sbuf("ssm_ct", [128, 8], F32)


def cmul_sv(g, outr, outi, ar, ai, br, bi, t0, t1):
    c, sB = g.c, g.sB
    tt = lambda o, x, y, op: c.op("dve", lambda e: e.tensor_tensor(out=o, in0=x, in1=y, op=op), reads=[sB], writes=[sB])
    tt(t0, ar, br, ALU.mult)
    tt(t1, ai, bi, ALU.mult)
    tt(t0, t0, t1, ALU.subtract)
    tt(t1, ar, bi, ALU.mult)
    tt(outi, ai, br, ALU.mult)
    tt(outi, outi, t1, ALU.add)
    c.op("dve", lambda e: e.tensor_copy(out=outr, in_=t0), reads=[sB], writes=[sB])


def ssm_block(g, ub, us, yssm, n, kind, project):
    c, sB = g.c, g.sB
    L = SSM_L
    sv = g.sv
    SVI = g.SVI
    if kind == "p":
        pieces = [(p * L, L, None) for p in range(n // L)]
    else:
        pieces = [(s * SL, SL, s) for s in range(NSEQ)]
    for (pc0, ln, seq) in pieces:
        ybk = None
        for i in range(16):
            j = i // 4
            wi_ = g.swi
            g.swi = (g.swi + 1) % 3
            wk, wq, wB = g.swk[wi_], g.swq[wi_], g.swB[wi_]
            bk, bB = bank(g)
            bv = bk[:, 0:512].rearrange("p (a b) -> p a b", a=4)
            rhs = ub.t[:, j, pc0:pc0 + ln]
            for q, tab in enumerate((g.BR, g.BI, g.BI, g.BR)):
                c.op("pe", lambda e, o=bv[:, q, 0:ln], l=tab[:, i, :], r=rhs: e.matmul(o, l, r, start=True, stop=True),
                     reads=[sB, ub.B[j]], writes=[bB] if q == 0 else [], wnodep=[] if q == 0 else [bB], inc=(q == 3))
            tcr = g.TCr[:, i, 0:ln].unsqueeze(1).broadcast_to([128, 2, ln])
            c.op("dve", lambda e, o=wk[:, 0:2, 0:ln], a_=bv[:, 0:2, 0:ln], b_=tcr: e.tensor_tensor(out=o, in0=a_, in1=b_, op=ALU.mult),
                 reads=[bB, sB], writes=[wB])
            c.op("dve", lambda e, o=wk[:, 2:4, 0:ln], a_=bv[:, 2:4, 0:ln], b_=g.TC2[:, i, :, 0:ln]: e.tensor_tensor(out=o, in0=a_, in1=b_, op=ALU.mult),
                 reads=[bB, sB], writes=[wB])
            c.op("pool", lambda e, o=wk[:, 4:6, 0:ln], a_=wk[:, 0:2, 0:ln], b_=wk[:, 2:4, 0:ln]: e.tensor_tensor(out=o, in0=a_, in1=b_, op=ALU.add),
                 reads=[wB], writes=[wB])
            if seq is None:
                inr, ini = g.carry[:, 0, i:i + 1], g.carry[:, 1, i:i + 1]
            else:
                inr, ini = g.h0s[:, 0, i, seq:seq + 1], g.h0s[:, 1, i, seq:seq + 1]
            for q, init in ((0, inr), (1, ini)):
                c.op("dve", lambda e, o=wk[:, 6 + q, 0:ln], d0=g.RT[:, i, 0:ln], d1=wk[:, 4 + q, 0:ln], it=init: e.tensor_tensor_scan(
                    out=o, data0=d0, data1=d1, initial=it, op0=ALU.mult, op1=ALU.add), reads=[wB, sB], writes=[wB])
            wre, wie = wk[:, 6, ln - 1:ln], wk[:, 7, ln - 1:ln]
            if seq is None:
                er, ei, nei = sv[:, SVI["ERL"], i:i + 1], sv[:, SVI["EIL"], i:i + 1], sv[:, SVI["NEIL"], i:i + 1]
                outr, outi = g.carry[:, 0, i:i + 1], g.carry[:, 1, i:i + 1]
                t1, t2 = g.ct[:, 0:1], g.ct[:, 1:2]
                c.op("dve", lambda e, t1=t1, wre=wre, er=er: e.tensor_scalar(out=t1, in0=wre, scalar1=er, scalar2=None, op0=ALU.mult), reads=[wB, sB], writes=[sB])
                c.op("dve", lambda e, t2=t2, wre=wre, ei=ei: e.tensor_scalar(out=t2, in0=wre, scalar1=ei, scalar2=None, op0=ALU.mult), reads=[wB, sB], writes=[sB])
                c.op("dve", lambda e, outr=outr, wie=wie, nei=nei, t1=t1: e.scalar_tensor_tensor(out=outr, in0=wie, scalar=nei, in1=t1, op0=ALU.mult, op1=ALU.add), reads=[wB, sB], writes=[sB])
                c.op("dve", lambda e, outi=outi, wie=wie, er=er, t2=t2: e.scalar_tensor_tensor(out=outi, in0=wie, scalar=er, in1=t2, op0=ALU.mult, op1=ALU.add), reads=[wB, sB], writes=[sB])
            else:
                er, ei = g.Er[:, i, ln - 1:ln], g.Ei[:, i, ln - 1:ln]
                outr, outi = g.sfin[:, 0, i, seq:seq + 1], g.sfin[:, 1, i, seq:seq + 1]
                t1, t2, t3 = g.ct[:, 0:1], g.ct[:, 1:2], g.ct[:, 2:3]
                c.op("dve", lambda e, t1=t1, wre=wre, er=er: e.tensor_scalar(out=t1, in0=wre, scalar1=er, scalar2=None, op0=ALU.mult), reads=[wB, sB], writes=[sB])
                c.op("dve", lambda e, t2=t2, wre=wre, ei=ei: e.tensor_scalar(out=t2, in0=wre, scalar1=ei, scalar2=None, op0=ALU.mult), reads=[wB, sB], writes=[sB])
                c.op("dve", lambda e, t3=t3, wie=wie, ei=ei: e.tensor_scalar(out=t3, in0=wie, scalar1=ei, scalar2=None, op0=ALU.mult), reads=[wB, sB], writes=[sB])
                c.op("dve", lambda e, outr=outr, t1=t1, t3=t3: e.tensor_tensor(out=outr, in0=t1, in1=t3, op=ALU.subtract), reads=[sB], writes=[sB])
                c.op("dve", lambda e, outi=outi, wie=wie, er=er, t2=t2: e.scalar_tensor_tensor(out=outi, in0=wie, scalar=er, in1=t2, op0=ALU.mult, op1=ALU.add), reads=[wB, sB], writes=[sB])
            if not project:
                continue
            erb = g.Er[:, i, 0:ln].unsqueeze(1).broadcast_to([128, 2, ln])
            eib = g.Ei[:, i, 0:ln].unsqueeze(1).broadcast_to([128, 2, ln])
            c.op("dve", lambda e, o=wq[:, 0:2, 0:ln], a_=wk[:, 6:8, 0:ln], b_=erb: e.tensor_tensor(out=o, in0=a_, in1=b_, op=ALU.mult),
                 reads=[wB, sB], writes=[wB])
            c.op("dve", lambda e, o=wq[:, 2:4, 0:ln], a_=wk[:, 6:8, 0:ln], b_=eib: e.tensor_tensor(out=o, in0=a_, in1=b_, op=ALU.mult),
                 reads=[wB, sB], writes=[wB])
            if i % 4 == 0:
                ybk, yB = bank(g)
            yo = ybk[:, 0:ln]
            for q, tab in enumerate((g.CR, g.nCI, g.nCI, g.nCR)):
                first = (i % 4 == 0 and q == 0)
                last = (i % 4 == 3 and q == 3)
                c.op("pe", lambda e, o=yo, l=tab[:, i, :], r=wq[:, q, 0:ln], f=first, s_=last: e.matmul(o, l, r, start=f, stop=s_),
                     reads=[sB, wB], writes=[yB] if first else [], wnodep=[] if first else [yB], inc=(q == 3))
            if i % 4 == 3:
                c.op("dve", lambda e, o=yssm.t[:, j, pc0:pc0 + ln], u_=us.t[:, j, pc0:pc0 + ln], dd=g.ssmd[:, j:j + 1], y_=yo: e.scalar_tensor_tensor(
                    out=o, in0=u_, scalar=dd, in1=y_, op0=ALU.mult, op1=ALU.add), reads=[yB, us.B[j], g.cstB], writes=[yssm.B[j]])


def ssm_true_end(g, outr, outi):
    c, sB, sv, SVI = g.c, g.sB, g.sv, g.SVI
    cth, sth = sv[:, SVI["CTH"], :], sv[:, SVI["STH"], :]
    cr, ci = g.carry[:, 0, :], g.carry[:, 1, :]
    t0, t1 = sv[:, SVI["T0"], :], sv[:, SVI["T1"], :]
    tt = lambda o, x, y, op: c.op("dve", lambda e: e.tensor_tensor(out=o, in0=x, in1=y, op=op), reads=[sB], writes=[sB])
    tt(t0, cth, cr, ALU.mult)
    tt(t1, sth, ci, ALU.mult)
    tt(outr, t0, t1, ALU.add)
    tt(t0, cth, ci, ALU.mult)
    tt(t1, sth, cr, ALU.mult)
    tt(outi, t0, t1, ALU.subtract)


def cast_x(g, n):
    for k in range(KT):
        g.c.op("act", lambda e, o=g.xb.t[:, k, 0:n], i=g.xf.t[:, k, 0:n]: e.activation(out=o, in_=i, func=AF.Copy),
               reads=[g.xf.B[k]], writes=[g.xb.B[k]])


def phase1(g):
    c, I, S = g.c, g.I, g.S
    TP, NB = g.TP, g.NB
    alloc_main(g)
    ssm_setup(g)
    g.ub = ActT(g, "ub", 4, NB, BF16)
    zt = [c.sbuf(f"zt{i}", [128, NB], F32) for i in range(3)]
    ztB = [Buf(f"zt{i}") for i in range(3)]
    zi = [0]
    c.op("dve", lambda e, t_=g.carry: e.memset(t_[:], 0.0), reads=[g.sB], writes=[g.sB])
    nprompt = TP // NB
    for bi, (c0, n, kind) in enumerate(g.blocks):
        load_fm(g, g.xf, I["xT"], c0, n)
        cast_x(g, n)
        ffn(g, 0, 0, g.xf, g.xb, g.hb, g.gt, n)
        layernorm(g, g.xf, g.xb, g.hb, g.st, n, 0)
        store_fm(g, g.xf, "x1T", c0, n)

        def evac(m, bk, bB, c0=c0, n=n):
            z = zi[0]
            zi[0] = (z + 1) % 3
            c.op("act", lambda e, o=zt[z][:, 0:n], i=bk[:, 0:n]: e.activation(out=o, in_=i, func=AF.Copy),
                 reads=[bB], writes=[ztB[z]])
            if m >= 12:
                c.op("dve", lambda e, o=g.ub.t[:, m - 12, 0:n], i=bk[:, 0:n]: e.tensor_copy(out=o, in_=i),
                     reads=[bB], writes=[g.ub.B[m - 12]])
            c.dma("sp", S["uT"][m * 128:(m + 1) * 128, c0:c0 + n], zt[z][:, 0:n], reads=[ztB[z]],
                  writes=[g.DB["uT"]], sembuf=ztB[z])
        linear(g, "w_in_e", 4, KT, 512, lambda k: g.xb.t[:, k, 0:n], g.xb.B, n, evac)
        if kind == "p":
            ssm_block(g, g.ub, None, None, n, "p", project=False)
        if bi == nprompt - 1:
            ssm_true_end(g, g.send[:, 0, :], g.send[:, 1, :])
            c.dma("sp", S["g1_in"][0, 0:4096].rearrange("(p a b) -> p a b", p=128, a=2),
                  g.send[:], reads=[g.sB], writes=[g.DB["g1_in"]], sembuf=g.DB["g1_in"])
            c.dma("sp", S["g1_in"][0, 4096:4096 + 1536 * 15].rearrange("(f t) -> f t", t=15),
                  S["uT"][0:1536, TP - 15:TP], reads=[g.DB["uT"]] + ztB, writes=[g.DB["g1_in"]], sembuf=g.DB["g1_in"])
    c.collective(S["g1_in"], S["g1_out"], [[0, 1, 2, 3], [4, 5, 6, 7]], reads=[g.DB["g1_in"]],
                 writes=[g.DB["g1_out"]], sembuf=g.DB["g1_out"])


def gather_combine(g):
    c, S, sB, sv, SVI = g.c, g.S, g.sB, g.sv, g.SVI
    gs = c.sbuf("g1s", [128, 4, 2, 16], F32)
    g.halo = c.sbuf("halo", [128, 12, 15], F32)
    gh = c.sbuf("g1h", [128, 4, 12, 15], F32)
    for j in range(4):
        c.dma("sp", gs[:, j], S["g1_out"][j, 0:4096].rearrange("(p a b) -> p a b", p=128, a=2),
              reads=[g.DB["g1_out"], sB], writes=[sB], sembuf=sB)
        c.dma("sp", gh[:, j], S["g1_out"][j, 4096:4096 + 1536 * 15].rearrange("(k p t) -> p k t", p=128, t=15),
              reads=[g.DB["g1_out"], sB], writes=[sB], sembuf=sB)
    meta = lambda col: g.cst[:, 192 + col:193 + col]
    d = lambda fn: c.op("dve", fn, reads=[sB, g.cstB], writes=[sB])
    d(lambda e, h_=g.halo: e.tensor_scalar(out=h_[:], in0=gh[:, 0], scalar1=meta(0), scalar2=None, op0=ALU.mult))
    for j in range(1, 4):
        d(lambda e, j=j, h_=g.halo: e.scalar_tensor_tensor(out=h_[:], in0=gh[:, j], scalar=meta(j), in1=h_[:], op0=ALU.mult, op1=ALU.add))
    Td = c.sbuf("g1T", [128, 3, 2, 16], F32)
    for dd in range(3):
        d(lambda e, dd=dd: e.tensor_scalar(out=Td[:, dd], in0=gs[:, 0], scalar1=meta(4 + dd * 4), scalar2=None, op0=ALU.mult))
        for j in range(1, 4):
            d(lambda e, dd=dd, j=j: e.scalar_tensor_tensor(out=Td[:, dd], in0=gs[:, j], scalar=meta(4 + dd * 4 + j), in1=Td[:, dd],
                                                           op0=ALU.mult, op1=ALU.add))
    apr, api = sv[:, SVI["APR"], :], sv[:, SVI["API"], :]
    t0, t1 = sv[:, SVI["T0"], :], sv[:, SVI["T1"], :]
    cmul_sv(g, Td[:, 2, 0], Td[:, 2, 1], Td[:, 2, 0], Td[:, 2, 1], apr, api, t0, t1)
    d(lambda e: e.tensor_tensor(out=Td[:, 1], in0=Td[:, 1], in1=Td[:, 2], op=ALU.add))
    cmul_sv(g, Td[:, 1, 0], Td[:, 1, 1], Td[:, 1, 0], Td[:, 1, 1], apr, api, t0, t1)
    d(lambda e: e.tensor_tensor(out=Td[:, 0], in0=Td[:, 0], in1=Td[:, 1], op=ALU.add))
    cmul_sv(g, g.carry[:, 0, :], g.carry[:, 1, :], Td[:, 0, 0], Td[:, 0, 1], sv[:, SVI["CTH"], :], sv[:, SVI["STH"], :], t0, t1)
    hin = c.sbuf("h0in", [128, 2, 16, NSEQ], F32)
    c.dma("sp", hin[:, 0], g.I["ssm_h0re"].rearrange("p (a b) -> p a b", a=16), reads=[sB], writes=[sB], sembuf=sB)
    c.dma("sp", hin[:, 1], g.I["ssm_h0im"].rearrange("p (a b) -> p a b", a=16), reads=[sB], writes=[sB], sembuf=sB)
    cb = sv[:, SVI["CTH"], :].unsqueeze(2).broadcast_to([128, 16, NSEQ])
    sb_ = sv[:, SVI["STH"], :].unsqueeze(2).broadcast_to([128, 16, NSEQ])
    tq = g.tq
    x0, x1 = tq[:, 0, :, 0:NSEQ], tq[:, 1, :, 0:NSEQ]
    cmul_sv(g, g.h0s[:, 0], g.h0s[:, 1], hin[:, 0], hin[:, 1], cb, sb_, x0, x1)


def pool_mixer(g, c0, n, kind, bi, mixb, last_prompt):
    c, I, S = g.c, g.I, g.S
    nseg, sl = (1, n) if kind == "p" else (NSEQ, SL)
    W = 15 + sl
    wt, wB = loadw(g, I["pool_w"][0])
    wv = wt[:, 0:12 * 384].rearrange("p (k m) -> p k m", k=12)
    for gg in range(4):
        w = 2 ** (gg + 1)
        U, A, Bq = [t[:, :, 0:nseg * W].rearrange("p k (s w) -> p k s w", s=nseg) for t in g.pU]
        UB, AB, BB = g.pUB
        rows = S["uT"][gg * 384:(gg + 1) * 384, :].rearrange("(k p) t -> p k t", p=128)
        if kind == "p":
            lo = 15 if c0 == 0 else 0
            c.dma("sp", U[:, :, 0, lo:W], rows[:, :, c0 - 15 + lo:c0 + n], reads=[g.DB["uT"]], writes=[UB], sembuf=UB)
            if c0 == 0:
                c.op("dve", lambda e, o=U[:, :, 0, 0:15], i=g.halo[:, gg * 3:gg * 3 + 3, :]: e.tensor_copy(out=o, in_=i),
                     reads=[g.sB], writes=[UB])
        else:
            for s_ in range(NSEQ):
                c.dma("sp", U[:, :, s_, 15:W], rows[:, :, c0 + s_ * SL:c0 + (s_ + 1) * SL], reads=[g.DB["uT"]], writes=[UB], sembuf=UB)
            for s_ in range(NSEQ):
                c.dma("sp", U[:, :, s_, 0:15], I["cache_poolT"][gg * 384:(gg + 1) * 384, s_, :].rearrange("(k p) t -> p k t", p=128),
                      writes=[UB], sembuf=UB)
        src, srcB = U, UB
        sh = 1
        dsts = [(A, AB), (Bq, BB), (A, AB), (Bq, BB)]
        for step in range(gg + 1):
            dst, dstB = dsts[step]
            lo = 2 * sh - 1
            c.op("dve", lambda e, o=dst[:, :, :, lo:W], a_=src[:, :, :, lo:W], b_=src[:, :, :, lo - sh:W - sh]: e.tensor_tensor(
                out=o, in0=a_, in1=b_, op=ALU.add), reads=[srcB], writes=[dstB])
            src, srcB = dst, dstB
            sh *= 2
        if kind == "p" and c0 == 0:
            c.op("dve", lambda e, o=src[:, :, 0, 15:31], cr=g.pcorr[:, gg * 3:gg * 3 + 3, :]: e.tensor_tensor(out=o, in0=o, in1=cr, op=ALU.mult),
                 reads=[g.cstB], writes=[srcB])
        dview = g.db.t[:, gg * 3:gg * 3 + 3, 0:n].rearrange("p k (s l) -> p k s l", s=nseg)
        c.op("dve", lambda e, o=dview, a_=src[:, :, :, 15:W], b_=U[:, :, :, 15:W], w=w: e.scalar_tensor_tensor(
            out=o, in0=a_, scalar=1.0 / w, in1=b_, op0=ALU.mult, op1=ALU.subtract), reads=[srcB, UB], writes=g.db.B[gg * 3:gg * 3 + 3])
        if kind == "s":
            for s_ in range(NSEQ):
                c.dma("sp", g.O["poolT_s"][gg * 384:(gg + 1) * 384, s_, :].rearrange("(k p) t -> p k t", p=128), U[:, :, s_, sl:W],
                      reads=[UB], writes=[g.DB["poolT_s"]], sembuf=g.DB["poolT_s"])
        elif last_prompt:
            c.dma("sp", g.O["poolT_p"][gg * 384:(gg + 1) * 384].rearrange("(k p) t -> p k t", p=128), U[:, :, 0, sl:W],
                  reads=[UB], writes=[g.DB["poolT_p"]], sembuf=g.DB["poolT_p"])
        for mi in range(3):
            bk, bB = bank(g)
            for k in range(3):
                mm(g, bk[:, 0:n], wv[:, gg * 3 + k, mi * 128:(mi + 1) * 128], g.db.t[:, gg * 3 + k, 0:n], k == 0, k == 2,
                   [wB, g.db.B[gg * 3 + k]], bB)
            m = gg * 3 + mi
            c.op("act", lambda e, o=mixb.t[:, m, 0:n], i=bk[:, 0:n], s_=g.pscale[:, m:m + 1]: e.activation(out=o, in_=i, func=AF.Copy, scale=s_),
                 reads=[bB, g.cstB], writes=[mixb.B[m]])


def rms_scale(g, srcf, nk, n, dim, stt):
    c = g.c
    for k in range(nk):
        c.op("act", lambda e, o=g.sqs.t[:, k, 0:n], i=srcf.t[:, k, 0:n]: e.activation(out=o, in_=i, func=AF.Square),
             reads=[srcf.B[k]], writes=[g.sqs.B[k]])
    s2, s2B = colsum(g, lambda k: g.sqs.t[:, k, 0:n], g.sqs.B, nk, n)
    r = g.st[:, stt, 0:n]
    c.op("dve", lambda e: e.tensor_scalar(out=r, in0=s2[:, 0:n], scalar1=1.0 / dim, scalar2=RMS_EPS, op0=ALU.mult, op1=ALU.add),
         reads=[s2B], writes=[g.stB[stt]])
    c.op("act", lambda e: e.activation(out=r, in_=r, func=AF.Sqrt), reads=[g.stB[stt]], writes=[g.stB[stt]])
    c.op("dve", lambda e: e.reciprocal(out=r, in_=r), reads=[g.stB[stt]], writes=[g.stB[stt]])
    return r, g.stB[stt]


def odd_proj(g, c0, n, kind):
    c, I, S, O = g.c, g.I, g.S, g.O
    xb = g.xb
    rhs = lambda k: xb.t[:, k, 0:n]
    c.dma("sp", g.rope[64:96, :, 0:n], I["rope"][64:96, :, c0:c0 + n], writes=[g.ropeB], sembuf=g.ropeB)
    def ev_cq(m, bk, bB):
        c.op("act", lambda e, o=g.cqf.t[:, m, 0:n], i=bk[:, 0:n]: e.activation(out=o, in_=i, func=AF.Copy), reads=[bB], writes=[g.cqf.B[m]])
    linear(g, "w_in_o_cq", 1, KT, 512, rhs, xb.B, n, ev_cq)
    if "odd1" in os.environ.get("KDBG", ""):
        return
    rq, rqB = rms_scale(g, g.cqf, 4, n, 512, 0)
    for k in range(4):
        c.op("dve", lambda e, o=g.cqn.t[:, k, 0:n], a_=g.cqf.t[:, k, 0:n], s_=g.gq[:, k:k + 1]: e.scalar_tensor_tensor(
            out=o, in0=a_, scalar=s_, in1=rq, op0=ALU.mult, op1=ALU.mult), reads=[g.cqf.B[k], rqB, g.cstB], writes=[g.cqn.B[k]])
    wt, wB = loadw(g, I["w_in_o_kv"][0])
    wv = wt[:, 0:KT * 320].rearrange("p (k m) -> p k m", k=KT)
    for m in range(2):
        bk, bB = bank(g)
        for k in range(KT):
            mm(g, bk[:, 0:n], wv[:, k, m * 128:(m + 1) * 128], rhs(k), k == 0, k == KT - 1, [wB, xb.B[k]], bB)
        c.op("act", lambda e, o=g.ckf.t[:, m, 0:n], i=bk[:, 0:n]: e.activation(out=o, in_=i, func=AF.Copy), reads=[bB], writes=[g.ckf.B[m]])
    bka, bAB = bank(g)
    for k in range(KT):
        mm(g, bka[64:96, 0:n], wv[:, k, 256:288], rhs(k), k == 0, k == KT - 1, [wB, xb.B[k]], bAB)
    bkb, bBB = bank(g)
    for k in range(KT):
        mm(g, bkb[64:96, 0:n], wv[:, k, 288:320], rhs(k), k == 0, k == KT - 1, [wB, xb.B[k]], bBB)
    kp = g.kpo
    cosr, sinr = g.rope[64:96, 0, 0:n], g.rope[64:96, 1, 0:n]
    c.op("dve", lambda e: e.tensor_tensor(out=kp[64:96, 0, 0:n], in0=bka[64:96, 0:n], in1=cosr, op=ALU.mult), reads=[bAB, g.ropeB], writes=[g.kpoB])
    c.op("dve", lambda e: e.tensor_tensor(out=kp[64:96, 1, 0:n], in0=bkb[64:96, 0:n], in1=sinr, op=ALU.mult), reads=[bBB, g.ropeB], writes=[g.kpoB])
    c.op("dve", lambda e: e.tensor_tensor(out=kp[64:96, 0, 0:n], in0=kp[64:96, 0, 0:n], in1=kp[64:96, 1, 0:n], op=ALU.add), writes=[g.kpoB])
    c.op("dve", lambda e: e.tensor_copy(out=g.kpb[64:96, 0:n], in_=kp[64:96, 0, 0:n]), reads=[g.kpoB], writes=[g.kpbB])
    c.dma("sp", O["kpeT"][:, c0:c0 + n], kp[64:96, 0, 0:n], reads=[g.kpoB], writes=[g.DB["kpeT"]], sembuf=g.DB["kpeT"])
    if "odd2" in os.environ.get("KDBG", ""):
        return
    rk, rkB = rms_scale(g, g.ckf, 2, n, 256, 1)
    for k in range(2):
        c.op("dve", lambda e, o=g.cko.t[:, k, 0:n], a_=g.ckf.t[:, k, 0:n], s_=g.gkv[:, k:k + 1]: e.scalar_tensor_tensor(
            out=o, in0=a_, scalar=s_, in1=rk, op0=ALU.mult, op1=ALU.mult), reads=[g.ckf.B[k], rkB, g.cstB], writes=[g.cko.B[k]])
        c.op("act", lambda e, o=g.ckb.t[:, k, 0:n], i=g.cko.t[:, k, 0:n]: e.activation(out=o, in_=i, func=AF.Copy),
             reads=[g.cko.B[k]], writes=[g.ckb.B[k]])
    store_fm(g, g.cko, "ckvT", c0, n)
    if kind == "p":
        ci = c0 // 256
        c.dma("sp", S["g2_in"][ci, 0:256, :].rearrange("(k p) t -> p k t", p=128), g.ckb.t[:, :, 0:n],
              reads=g.ckb.B, writes=[g.DB["g2_in"]], sembuf=g.DB["g2_in"])
        c.dma("sp", S["g2_in"][ci, 256:288, :], g.kpb[64:96, 0:n], reads=[g.kpbB], writes=[g.DB["g2_in"]], sembuf=g.DB["g2_in"])
    else:
        c.dma("sp", S["skv"][0:256, :].rearrange("(k p) t -> p k t", p=128), g.ckb.t[:, :, 0:n],
              reads=g.ckb.B, writes=[g.DB["skv"]], sembuf=g.DB["skv"])
        c.dma("sp", S["skv"][256:288, :], g.kpb[64:96, 0:n], reads=[g.kpbB], writes=[g.DB["skv"]], sembuf=g.DB["skv"])
    if "odd3" in os.environ.get("KDBG", ""):
        return
    wd = I["w_uq"]
    for ch in range(2):
        wt, wB = loadw(g, wd[ch])
        wv = wt[:, 0:4096].rearrange("p (k m) -> p k m", k=4)
        for h8 in range(8):
            h = ch * 8 + h8
            bk, bB = bank(g)
            for k in range(4):
                mm(g, bk[0:96, 0:n], wv[:, k, h8 * 128:h8 * 128 + 96], g.cqn.t[:, k, 0:n], k == 0, k == 3, [wB, g.cqn.B[k]], bB)
            bk2, bB2 = bank(g)
            for k in range(4):
                mm(g, bk2[64:96, 0:n], wv[:, k, h8 * 128 + 96:h8 * 128 + 128], g.cqn.t[:, k, 0:n], k == 0, k == 3, [wB, g.cqn.B[k]], bB2)
            qi = g.qi
            g.qi = (g.qi + 1) % 2
            qt, qB = g.qt[qi], g.qtB[qi]
            c.op("act", lambda e, o=qt[0:64, 0:n], i=bk[0:64, 0:n]: e.activation(out=o, in_=i, func=AF.Copy), reads=[bB], writes=[qB])
            c.op("dve", lambda e, o=kp[64:96, 2, 0:n], i=bk[64:96, 0:n]: e.tensor_tensor(out=o, in0=i, in1=cosr, op=ALU.mult), reads=[bB, g.ropeB], writes=[g.kpoB])
            c.op("dve", lambda e, o=kp[64:96, 3, 0:n], i=bk2[64:96, 0:n]: e.tensor_tensor(out=o, in0=i, in1=sinr, op=ALU.mult), reads=[bB2, g.ropeB], writes=[g.kpoB])
            c.op("dve", lambda e, o=qt[64:96, 0:n]: e.tensor_tensor(out=o, in0=kp[64:96, 2, 0:n], in1=kp[64:96, 3, 0:n], op=ALU.add), reads=[g.kpoB], writes=[qB])
            c.dma("sp", S["qT"][h, :, c0:c0 + n], qt[0:96, 0:n], reads=[qB], writes=[g.DB["qT"]], sembuf=qB)
    if "odd4" in os.environ.get("KDBG", ""):
        return
    def ev_u(m, bk, bB):
        c.op("act", lambda e, o=g.uf8.t[:, m, 0:n], i=bk[:, 0:n]: e.activation(out=o, in_=i, func=AF.Copy), reads=[bB], writes=[g.uf8.B[m]])
    linear(g, "w_in_o_u", 2, KT, 512, rhs, xb.B, n, ev_u)
    if "odd5" in os.environ.get("KDBG", ""):
        return
    wvs = []
    for ch in range(2):
        wt, wB = loadw(g, I["w_in_o_v"][ch])
        wvs.append((wt[:, 0:KT * 512].rearrange("p (k m) -> p k m", k=KT), wB))
    for ts in range(n // 128):
        vt = g.vtok
        for half in range(2):
            wv, wB = wvs[half]
            bk, bB = bank(g)
            for k in range(KT):
                mm(g, bk[:, 0:512], xb.t[:, k, ts * 128:(ts + 1) * 128], wv[:, k, :], k == 0, k == KT - 1, [wB, xb.B[k]], bB)
            c.op("act", lambda e, o=vt[:, half * 512:(half + 1) * 512], i=bk[:, 0:512]: e.activation(out=o, in_=i, func=AF.Copy),
                 reads=[bB], writes=[g.vtB])
        c.op("dve", lambda e: e.tensor_reduce(out=g.bag[:, 0:1], in_=vt[:], op=ALU.add, axis=mybir.AxisListType.X), reads=[g.vtB], writes=[g.bstB])
        c.op("act", lambda e: e.activation(out=g.vbt[:], in_=vt[:], func=AF.Square, accum_out=g.bag[:, 1:2]), reads=[g.vtB], writes=[g.vbB, g.bstB])
        c.op("dve", lambda e: e.tensor_scalar(out=g.bag[:, 0:2], in0=g.bag[:, 0:2], scalar1=1.0 / 1024, scalar2=None, op0=ALU.mult), reads=[g.bstB], writes=[g.bstB])
        c.op("dve", lambda e: e.tensor_tensor(out=g.bag[:, 2:3], in0=g.bag[:, 0:1], in1=g.bag[:, 0:1], op=ALU.mult), reads=[g.bstB], writes=[g.bstB])
        c.op("dve", lambda e: e.tensor_tensor(out=g.bag[:, 2:3], in0=g.bag[:, 1:2], in1=g.bag[:, 2:3], op=ALU.subtract), reads=[g.bstB], writes=[g.bstB])
        c.op("dve", lambda e: e.tensor_scalar(out=g.bag[:, 2:3], in0=g.bag[:, 2:3], scalar1=LN_EPS, scalar2=None, op0=ALU.add), reads=[g.bstB], writes=[g.bstB])
        c.op("act", lambda e: e.activation(out=g.bag[:, 2:3], in_=g.bag[:, 2:3], func=AF.Sqrt), reads=[g.bstB], writes=[g.bstB])
        c.op("dve", lambda e: e.reciprocal(out=g.bag[:, 3:4], in_=g.bag[:, 2:3]), reads=[g.bstB], writes=[g.bstB])
        c.op("dve", lambda e: e.tensor_scalar(out=vt[:], in0=vt[:], scalar1=g.bag[:, 0:1], scalar2=None, op0=ALU.subtract), reads=[g.bstB], writes=[g.vtB])
        c.op("dve", lambda e: e.tensor_scalar(out=vt[:], in0=vt[:], scalar1=g.bag[:, 3:4], scalar2=None, op0=ALU.mult), reads=[g.bstB], writes=[g.vtB])
        c.op("dve", lambda e: e.tensor_tensor(out=vt[:], in0=vt[:], in1=g.sgg[:], op=ALU.mult), reads=[g.cstB], writes=[g.vtB])
        c.op("dve", lambda e: e.tensor_tensor(out=vt[:], in0=vt[:], in1=g.sgb[:], op=ALU.add), reads=[g.cstB], writes=[g.vtB])
        c.op("act", lambda e: e.activation(out=g.vbt[:], in_=vt[:], func=AF.Copy), reads=[g.vtB], writes=[g.vbB])
        if kind == "s":
            c.dma("sp", O["sgv"][ts * 128:(ts + 1) * 128, :], vt[:], reads=[g.vtB], writes=[g.DB["sgv"]], sembuf=g.DB["sgv"])
        for g4 in range(2):
            tmp = g.sgt
            if kind == "p":
                bk, bB = bank(g)
                bv = bk[:, 0:512].rearrange("p (a b) -> p a b", a=4)
                for gq in range(4):
                    gr = g4 * 4 + gq
                    c.op("pe", lambda e, o=bv[:, gq, :], l=g.vbt[:, gr * 128:(gr + 1) * 128], r=g.swT[:, gr, :]: e.matmul(o, l, r, start=True, stop=True),
                         reads=[g.vbB, g.cstB], writes=[bB] if gq == 0 else [], wnodep=[] if gq == 0 else [bB], inc=(gq == 3))
                bsv = g.sbs[:, g4 * 4:(g4 + 1) * 4, :]
                c.op("dve", lambda e, o=tmp[:], a_=bv, b_=bsv: e.tensor_tensor(out=o, in0=a_, in1=b_, op=ALU.add), reads=[bB, g.cstB], writes=[g.sgtB])
            else:
                for hf in range(2):
                    bk, bB = bank(g)
                    bv = bk[:, 0:512].rearrange("p (a b) -> p a b", a=4)
                    for gq in range(4):
                        gr = g4 * 4 + gq
                        c.op("pe", lambda e, o=bv[:, gq, 0:64], l=g.vbt[hf * 64:(hf + 1) * 64, gr * 128:(gr + 1) * 128],
                             r=g.swT64[hf * 64:(hf + 1) * 64, gr, :]: e.matmul(o, l, r, start=True, stop=True),
                             reads=[g.vbB, g.cstB], writes=[bB] if gq == 0 else [], wnodep=[] if gq == 0 else [bB], inc=(gq == 3))
                    bsv = g.sbs64[:, g4 * 4:(g4 + 1) * 4, hf * 64:(hf + 1) * 64]
                    c.op("dve", lambda e, o=tmp[:, :, hf * 64:(hf + 1) * 64], a_=bv[:, :, 0:64], b_=bsv: e.tensor_tensor(out=o, in0=a_, in1=b_, op=ALU.add),
                         reads=[bB, g.cstB], writes=[g.sgtB])
            c.op("dve", lambda e, o=g.sgo.t[:, g4 * 4:(g4 + 1) * 4, ts * 128:(ts + 1) * 128], a_=tmp[:], b_=g.uf8.t[:, g4 * 4:(g4 + 1) * 4, ts * 128:(ts + 1) * 128]: e.tensor_tensor(
                out=o, in0=a_, in1=b_, op=ALU.mult), reads=[g.sgtB] + g.uf8.B[g4 * 4:(g4 + 1) * 4], writes=g.sgo.B[g4 * 4:(g4 + 1) * 4])
    store_fm(g, g.sgo, "sgoT", c0, n)


def phase2(g):
    c, I, S, O = g.c, g.I, g.S, g.O
    TP, NB = g.TP, g.NB
    alloc_main(g)
    ssm_setup(g)
    gather_combine(g)
    cs = c.sbuf("cst2", [128, 12 + 12 * 16 + 4 + 4 + 4 + 2], F32)
    o = 0
    def ld(name, w):
        nonlocal o
        v = cs[:, o:o + w]
        c.dma("sp", v, I[name], writes=[g.cstB], sembuf=g.cstB)
        o += w
        return v
    g.pscale = ld("pool_scale", 12)
    g.pcorr = ld("pool_corr", 192).rearrange("p (k t) -> p k t", k=12)
    g.ssmd = ld("ssm_d", 4)
    g.bglu = ld("b_glu", 4)
    W = max(15 + NB, NSEQ * (15 + SL))
    g.pU = [c.sbuf(f"pU{i}", [128, 3, W], F32) for i in range(3)]
    g.pUB = [Buf(f"pU{i}") for i in range(3)]
    g.db = ActT(g, "db", 12, NB, BF16)
    g.mixb = ActT(g, "mixb", 16, NB, BF16)
    g.us = ActT(g, "us", 4, NB, F32)
    g.ub = ActT(g, "ub", 4, NB, BF16)
    g.yssm = ActT(g, "yssm", 4, NB, F32)
    g.gf = ActT(g, "gf", 4, NB, F32)
    g.gb = ActT(g, "gb", 4, NB, BF16)
    nprompt = TP // NB
    for bi, (c0, n, kind) in enumerate(g.blocks):
        load_fm(g, g.xf, S["x1T"], c0, n)
        load_fm(g, g.us, S["uT"][1536:2048, :], c0, n)
        for k in range(4):
            c.op("act", lambda e, o=g.ub.t[:, k, 0:n], i=g.us.t[:, k, 0:n]: e.activation(out=o, in_=i, func=AF.Copy),
                 reads=[g.us.B[k]], writes=[g.ub.B[k]])
        if "nopool" not in os.environ.get("KDBG", ""):
            pool_mixer(g, c0, n, kind, bi, g.mixb, bi == nprompt - 1)
        if "nossm" not in os.environ.get("KDBG", ""):
            ssm_block(g, g.ub, g.us, g.yssm, n, kind, project=True)
        if bi == nprompt - 1:
            ssm_true_end(g, g.send[:, 0, :], g.send[:, 1, :])
            c.dma("sp", O["ssm_p"], g.send[:], reads=[g.sB], writes=[g.DB["ssm_p"]], sembuf=g.DB["ssm_p"])
        if kind == "s":
            c.dma("sp", O["ssm_s"], g.sfin[:].rearrange("p a b s -> p a (b s)"), reads=[g.sB], writes=[g.DB["ssm_s"]], sembuf=g.DB["ssm_s"])
        for k in range(4):
            c.op("act", lambda e, o=g.gf.t[:, k, 0:n], i=g.yssm.t[:, k, 0:n]: e.activation(out=o, in_=i, func=AF.Gelu_apprx_tanh),
                 reads=[g.yssm.B[k]], writes=[g.gf.B[k]])
            c.op("dve", lambda e, o=g.gb.t[:, k, 0:n], i=g.gf.t[:, k, 0:n]: e.tensor_copy(out=o, in_=i), reads=[g.gf.B[k]], writes=[g.gb.B[k]])

        def ev_glu(m, bk, bB, n=n):
            c.op("act", lambda e, o=g.gt[:, 0, 0:n], i=bk[:, 0:n], b_=g.bglu[:, m:m + 1]: e.activation(out=o, in_=i, func=AF.Sigmoid, bias=b_),
                 reads=[bB, g.cstB], writes=[g.gtB[0]])
            c.op("dve", lambda e, o=g.mixb.t[:, 12 + m, 0:n], a_=g.gf.t[:, m, 0:n], b_=g.gt[:, 0, 0:n]: e.tensor_tensor(out=o, in0=a_, in1=b_, op=ALU.mult),
                 reads=[g.gtB[0], g.gf.B[m]], writes=[g.mixb.B[12 + m]])
        linear(g, "w_glu", 1, 4, 512, lambda k: g.gb.t[:, k, 0:n], g.gb.B, n, ev_glu)

        def ev_mix(m, bk, bB, n=n):
            c.op("dve", lambda e, o=g.xf.t[:, m, 0:n], a_=bk[:, 0:n]: e.scalar_tensor_tensor(out=o, in0=a_, scalar=1.0 / ALPHA, in1=o, op0=ALU.mult, op1=ALU.add),
                 reads=[bB], writes=[g.xf.B[m]])
        linear(g, "w_out_e", 4, KT, 512, lambda k: g.mixb.t[:, k, 0:n], g.mixb.B, n, ev_mix)
        layernorm(g, g.xf, g.xb, g.hb, g.st, n, 1)
        ffn(g, 0, 1, g.xf, g.xb, g.hb, g.gt, n)
        layernorm(g, g.xf, g.xb, g.hb, g.st, n, 2)
        ffn(g, 1, 0, g.xf, g.xb, g.hb, g.gt, n)
        layernorm(g, g.xf, g.xb, g.hb, g.st, n, 3)
        store_fm(g, g.xf, "x4T", c0, n)


def phase2b(g):
    c, I, S, O = g.c, g.I, g.S, g.O
    TP, NB = g.TP, g.NB
    g.xf = ActT(g, "xf", KT, NB, F32)
    g.xb = ActT(g, "xb", KT, NB, BF16)
    g.st = c.sbuf("st", [128, 6, NB], F32)
    g.stB = [Buf(f"st{i}") for i in range(6)]
    cs = c.sbuf("cst3", [128, 8], F32)
    g.gq = cs[:, 0:4]
    g.gkv = cs[:, 4:6]
    c.dma("sp", g.gq, I["g_q"], writes=[g.cstB], sembuf=g.cstB)
    c.dma("sp", g.gkv, I["g_kv"], writes=[g.cstB], sembuf=g.cstB)
    g.sgg = c.sbuf("sgg", [128, 1024], F32)
    g.sgb = c.sbuf("sgb", [128, 1024], F32)
    c.dma("sp", g.sgg[:], I["sg_gv"], writes=[g.cstB], sembuf=g.cstB)
    c.dma("sp", g.sgb[:], I["sg_bv"], writes=[g.cstB], sembuf=g.cstB)
    g.sbs = c.sbuf("sbs", [128, 8, 128], F32)
    g.sbs64 = c.sbuf("sbs64", [128, 8, 128], F32)
    c.dma("sp", g.sbs[:], I["sg_bs"].rearrange("p (a b) -> p a b", a=8), writes=[g.cstB], sembuf=g.cstB)
    c.dma("sp", g.sbs64[:], I["sg_bs64"].rearrange("p (a b) -> p a b", a=8), writes=[g.cstB], sembuf=g.cstB)
    g.swT = c.sbuf("swT", [128, 8, 128], BF16)
    g.swT64 = c.sbuf("swT64", [128, 8, 64], BF16)
    trl = c.sbuf("tril", [128, 128], F32)
    stgt = c.sbuf("stg", [128, 1536], F32)
    stg = stgt[:, 0:1536]
    c.dma("sp", trl[:], I["tril"], writes=[g.cstB], sembuf=g.cstB)
    c.dma("sp", stg[:, 0:1024], I["sg_wT"], writes=[g.cstB], sembuf=g.cstB)
    c.dma("sp", stg[:, 1024:1536], I["sg_wT64"], writes=[g.cstB], sembuf=g.cstB)
    c.op("dve", lambda e: e.tensor_tensor(out=g.swT[:], in0=stg[:, 0:1024].rearrange("p (a b) -> p a b", a=8),
                                          in1=trl[:].unsqueeze(1).broadcast_to([128, 8, 128]), op=ALU.mult), reads=[g.cstB], writes=[g.cstB])
    for hf in range(2):
        c.op("dve", lambda e, hf=hf: e.tensor_tensor(out=g.swT64[hf * 64:(hf + 1) * 64], in0=stg[hf * 64:(hf + 1) * 64, 1024:1536].rearrange("p (a b) -> p a b", a=8),
                                                     in1=trl[hf * 64:(hf + 1) * 64, hf * 64:(hf + 1) * 64].unsqueeze(1).broadcast_to([64, 8, 64]), op=ALU.mult),
             reads=[g.cstB], writes=[g.cstB])
    g.cqf = ActT(g, "cqf", 4, NB, F32)
    g.cqn = ActT(g, "cqn", 4, NB, BF16)
    g.sqs = ActT(g, "sqs", 4, NB, BF16)
    g.ckf = ActT(g, "ckf", 2, NB, F32)
    g.cko = ActT(g, "cko", 2, NB, F32)
    g.ckb = ActT(g, "ckb", 2, NB, BF16)
    g.kpo = c.sbuf("kpo", [128, 4, NB], F32)
    g.kpoB = Buf("kpo")
    g.kpb = c.sbuf("kpb", [128, NB], BF16)
    g.kpbB = Buf("kpb")
    g.rope = c.sbuf("rope", [128, 2, NB], F32)
    g.ropeB = Buf("rope")
    g.qt = [c.sbuf(f"qt{i}", [128, NB], BF16) for i in range(2)]
    g.qtB = [Buf(f"qt{i}") for i in range(2)]
    g.qi = 0
    g.uf8 = ActT(g, "uf8", 8, NB, F32)
    g.sgo = ActT(g, "sgo", 8, NB, BF16)
    g.vtok = c.sbuf("vtok", [128, 1024], F32)
    g.vtB = Buf("vtok")
    g.vbt = c.sbuf("vbt", [128, 1024], BF16)
    g.vbB = Buf("vbt")
    g.bst = c.sbuf("bst", [128, 2, 6], F32)
    g.bag = c.sbuf("bag", [128, 4], F32)
    g.bstB = Buf("bst")
    g.sgt = c.sbuf("sgt", [128, 4, 128], F32)
    g.sgtB = Buf("sgt")
    for bi, (c0, n, kind) in enumerate(g.blocks):
        if "noodd" in os.environ.get("KDBG", ""):
            break
        load_fm(g, g.xf, S["x4T"], c0, n)
        cast_x(g, n)
        odd_proj(g, c0, n, kind)
    for ci in range(g.NCH):
        c.collective(S["g2_in"][ci], S["g2_out"][ci], [[0, 1, 2, 3], [4, 5, 6, 7]], reads=[g.DB["g2_in"]],
                     writes=[g.g2B[ci]], sembuf=g.g2B[ci])


def attend(g, A, qap, nq, tiles, out_ap):
    c = g.c
    ob, oB = g.banks[7], g.bankB[7]
    nt = len(tiles)
    for idx, (kc, kn, bias, mask) in enumerate(tiles):
        sb, sB_ = bank(g)
        c.op("pe", lambda e, o=sb[0:kn, 0:nq], l=A.KT[0:96, kc:kc + kn], r=qap: e.matmul(o, l, r, start=True, stop=True),
             reads=[A.KTB, A.QB], writes=[sB_])
        pi = A.pi
        A.pi = (A.pi + 1) % 2
        pt, pB = A.PT[pi], A.PTB[pi]
        if bias is None:
            c.op("act", lambda e, o=pt[0:kn, 0:nq], i=sb[0:kn, 0:nq]: e.activation(out=o, in_=i, func=AF.Exp, scale=ATTN_SCALE),
                 reads=[sB_], writes=[pB])
        else:
            c.op("act", lambda e, o=pt[0:kn, 0:nq], i=sb[0:kn, 0:nq], b_=bias: e.activation(out=o, in_=i, func=AF.Exp, scale=ATTN_SCALE, bias=b_),
                 reads=[sB_, g.cstB], writes=[pB])
        if mask is not None:
            c.op("dve", lambda e, o=pt[0:kn, 0:nq], m_=mask: e.tensor_tensor(out=o, in0=o, in1=m_, op=ALU.mult), reads=[A.dmB], writes=[pB])
        c.op("pe", lambda e, o=ob[0:65, 0:nq], l=A.VT[0:kn, kc // 128, 0:65], r=pt[0:kn, 0:nq], f=(idx == 0), s_=(idx == nt - 1): e.matmul(o, l, r, start=f, stop=s_),
             reads=[A.VTB, pB], writes=[oB] if idx == 0 else [], wnodep=[] if idx == 0 else [oB], inc=True)
    c.op("dve", lambda e, o=A.rd[64:65, 0:nq], i=ob[64:65, 0:nq]: e.reciprocal(out=o, in_=i), reads=[oB], writes=[A.rdB])
    b2, b2B = g.banks[6], g.bankB[6]
    c.op("pe", lambda e, o=b2[0:64, 0:nq], l=g.onesf[64:65, 0:64], r=A.rd[64:65, 0:nq]: e.matmul(o, l, r, start=True, stop=True),
         reads=[A.rdB, g.onesB], writes=[b2B])
    c.op("act", lambda e, o=A.to[0:64, 0:nq], i=ob[0:64, 0:nq]: e.activation(out=o, in_=i, func=AF.Copy), reads=[oB], writes=[A.toB])
    ai = A.ai
    A.ai = (A.ai + 1) % 2
    at, aB = A.att[ai], A.attB[ai]
    c.op("dve", lambda e, o=at[0:64, 0:nq], a_=A.to[0:64, 0:nq], b_=b2[0:64, 0:nq]: e.tensor_tensor(out=o, in0=a_, in1=b_, op=ALU.mult),
         reads=[A.toB, b2B], writes=[aB])
    c.dma("sp", out_ap, at[0:64, 0:nq], reads=[aB], writes=[g.DB["attT"]], sembuf=aB)


def kv_produce(g, A, h, nk):
    c = g.c
    col = 0
    flip = 0
    while col < nk:
        w = min(512, nk - col)
        bk, bB = bank(g)
        for kt in range(2):
            mm(g, bk[0:64, 0:w], A.wuk[:, kt, h * 64:(h + 1) * 64], A.CK[:, kt, col:col + w], kt == 0, kt == 1, [A.wB, A.CKB], bB)
        if flip:
            c.op("act", lambda e, o=A.KT[0:64, col:col + w], i=bk[0:64, 0:w]: e.activation(out=o, in_=i, func=AF.Copy), reads=[bB], writes=[A.KTB])
        else:
            c.op("dve", lambda e, o=A.KT[0:64, col:col + w], i=bk[0:64, 0:w]: e.tensor_copy(out=o, in_=i), reads=[bB], writes=[A.KTB])
        flip ^= 1
        col += w
    ntile = (nk + 127) // 128
    t = 0
    while t < ntile:
        gsz = min(8, ntile - t)
        bk, bB = bank(g)
        bv = bk[:, 0:512].rearrange("p (a b) -> p a b", a=8)
        full = 0
        for q in range(gsz):
            kn = min(128, nk - (t + q) * 128)
            if kn == 128:
                full += 1
            for kt in range(2):
                first = (q == 0 and kt == 0)
                c.op("pe", lambda e, o=bv[0:kn, q, :], l=A.CK[:, kt, (t + q) * 128:(t + q) * 128 + kn], r=A.wuv[:, kt, h * 64:(h + 1) * 64], f=(kt == 0), s_=(kt == 1): e.matmul(o, l, r, start=f, stop=s_),
                     reads=[A.wB, A.CKB], writes=[bB] if first else [], wnodep=[] if first else [bB], inc=(q == gsz - 1 and kt == 1))
        if full:
            c.op("dve", lambda e, o=A.VT[:, t:t + full, 0:64], i=bv[:, 0:full, :]: e.tensor_copy(out=o, in_=i), reads=[bB], writes=[A.VTB])
        if full < gsz:
            kn = nk - (t + full) * 128
            c.op("dve", lambda e, o=A.VT[0:kn, t + full, 0:64], i=bv[0:kn, full, :]: e.tensor_copy(out=o, in_=i), reads=[bB], writes=[A.VTB])
        t += gsz


def phase3a(g):
    c, I, S = g.c, g.I, g.S
    TP, NB, PAST, NT = g.TP, g.NB, g.PAST, g.NT
    A = K()
    g.nrot = 6
    g.bi = 0
    NKP = 4 * TP
    NKS = PAST + SL
    KW = max(NKP, NKS)
    A.CK = c.sbuf("aCK", [128, 2, KW], BF16); A.CKB = Buf("aCK")
    A.KT = c.sbuf("aKT", [128, KW], BF16); A.KTB = Buf("aKT")
    A.VT = c.sbuf("aVT", [128, (KW + 127) // 128, 65], BF16); A.VTB = Buf("aVT")
    A.Q = c.sbuf("aQ", [128, max(NT, 16 * SL)], BF16); A.QB = Buf("aQ")
    A.PT = [c.sbuf(f"aPT{i}", [128, NB], BF16) for i in range(2)]; A.PTB = [Buf(f"aPT{i}") for i in range(2)]; A.pi = 0
    A.att = [c.sbuf(f"aat{i}", [128, NB], BF16) for i in range(2)]; A.attB = [Buf(f"aat{i}") for i in range(2)]; A.ai = 0
    A.rd = c.sbuf("ard", [128, NB], F32); A.rdB = Buf("ard")
    A.to = c.sbuf("ato", [128, NB], F32); A.toB = Buf("ato")
    A.wuk = c.sbuf("awuk", [128, 2, 1024], BF16)
    A.wuv = c.sbuf("awuv", [128, 2, 1024], BF16)
    A.wB = Buf("awu")
    ND = NB // 128
    A.dm = c.sbuf("adm", [128, ND, NB], BF16); A.dmB = Buf("adm")
    c.dma("pool", A.wuk[:].rearrange("p a b -> p (a b)"), I["w_uk"][0], writes=[A.wB], sembuf=A.wB)
    c.dma("pool", A.wuv[:].rearrange("p a b -> p (a b)"), I["w_uv"][0], writes=[A.wB], sembuf=A.wB)
    c.dma("sp", A.dm[:].rearrange("p a b -> p (a b)"), I["dmask"], writes=[A.dmB], sembuf=A.dmB)
    c.op("dve", lambda e: e.memset(A.VT[:, :, 64:65], 1.0), writes=[A.VTB])
    for j in range(4):
        for ci in range(g.NCH):
            src = S["g2_out"][ci, j * 288:(j + 1) * 288, :] if j < 3 else S["g2_in"][ci]
            rd_ = [g.g2B[ci]] if j < 3 else [g.DB["g2_in"]]
            cs_ = j * TP + ci * 256
            c.dma("sp", A.CK[:, :, cs_:cs_ + 256], src[0:256, :].rearrange("(k p) t -> p k t", p=128), reads=rd_, writes=[A.CKB], sembuf=A.CKB)
            c.dma("sp", A.KT[64:96, cs_:cs_ + 256], src[256:288, :], reads=rd_, writes=[A.KTB], sembuf=A.KTB)
    for h in range(16):
        kv_produce(g, A, h, NKP)
        c.dma("sp", A.Q[0:96, 0:NT], S["qT"][h], reads=[g.DB["qT"]], writes=[A.QB], sembuf=A.QB)
        for qb in range(TP // NB):
            c0 = qb * NB
            tiles = []
            for j in range(3):
                for t in range(TP // 128):
                    tiles.append((j * TP + t * 128, 128, g.cst[:, 192 + 16 + j:192 + 17 + j], None))
            for t in range((c0 + NB) // 128):
                if t * 128 < c0:
                    tiles.append((3 * TP + t * 128, 128, None, None))
                else:
                    tiles.append((3 * TP + t * 128, 128, None, A.dm[:, (t * 128 - c0) // 128, :]))
            attend(g, A, A.Q[0:96, c0:c0 + NB], NB, tiles, S["attT"][h, :, c0:c0 + NB])
    for s_ in range(NSEQ):
        c.dma("pool", A.CK[:, :, 0:PAST], I["cache_ckvT"][s_].rearrange("(k p) t -> p k t", p=128), writes=[A.CKB], sembuf=A.CKB)
        c.dma("sp", A.CK[:, :, PAST:PAST + SL], S["skv"][0:256, s_ * SL:(s_ + 1) * SL].rearrange("(k p) t -> p k t", p=128),
              reads=[g.DB["skv"]], writes=[A.CKB], sembuf=A.CKB)
        c.dma("pool", A.KT[64:96, 0:PAST], I["cache_kpeT"][s_], writes=[A.KTB], sembuf=A.KTB)
        c.dma("sp", A.KT[64:96, PAST:PAST + SL], S["skv"][256:288, s_ * SL:(s_ + 1) * SL], reads=[g.DB["skv"]], writes=[A.KTB], sembuf=A.KTB)
        qv = A.Q[0:96, 0:16 * SL].rearrange("p (h t) -> p h t", h=16)
        c.dma("sp", qv, S["qT"][:, :, TP + s_ * SL:TP + (s_ + 1) * SL].rearrange("h d t -> d h t"), reads=[g.DB["qT"]], writes=[A.QB], sembuf=A.QB)
        for h in range(16):
            kv_produce(g, A, h, NKS)
            tiles = [(t * 128, min(128, NKS - t * 128), None, None) for t in range((NKS + 127) // 128)]
            attend(g, A, qv[:, h, :], SL, tiles, S["attT"][h, :, TP + s_ * SL:TP + (s_ + 1) * SL])


def phase3b(g):
    c, I, S = g.c, g.I, g.S
    TP, NB = g.TP, g.NB
    g.nrot = 8
    alloc_main(g)
    atb = c.sbuf("atb", [128, 16, NB], BF16)
    atB = Buf("atb")
    sgb = ActT(g, "sgb3", 8, NB, BF16)
    for bi, (c0, n, kind) in enumerate(g.blocks):
        load_fm(g, g.xf, S["x4T"], c0, n)
        c.dma("sp", atb[0:64, :, 0:n], S["attT"][:, :, c0:c0 + n].rearrange("h d t -> d h t"), reads=[g.DB["attT"]], writes=[atB], sembuf=atB)
        load_fm(g, sgb, S["sgoT"], c0, n)
        for ch in range(4):
            wa, waB = loadw(g, I["w_out_o_a"][ch], rows=64)
            ws, wsB = loadw(g, I["w_out_o_s"][ch])
            wav = wa[:, 0:16 * 512].rearrange("p (k m) -> p k m", k=16)
            wsv = ws[:, 0:8 * 512].rearrange("p (k m) -> p k m", k=8)
            for mi in range(4):
                bk, bB = bank(g)
                for h in range(16):
                    mm(g, bk[:, 0:n], wav[0:64, h, mi * 128:(mi + 1) * 128], atb[0:64, h, 0:n], h == 0, False, [waB, atB], bB)
                for k in range(8):
                    mm(g, bk[:, 0:n], wsv[:, k, mi * 128:(mi + 1) * 128], sgb.t[:, k, 0:n], False, k == 7, [wsB, sgb.B[k]], bB)
                m = ch * 4 + mi
                c.op("dve", lambda e, o=g.xf.t[:, m, 0:n], a_=bk[:, 0:n]: e.scalar_tensor_tensor(out=o, in0=a_, scalar=1.0 / ALPHA, in1=o, op0=ALU.mult, op1=ALU.add),
                     reads=[bB], writes=[g.xf.B[m]])
        layernorm(g, g.xf, g.xb, g.hb, g.st, n, 4)
        ffn(g, 1, 1, g.xf, g.xb, g.hb, g.gt, n)
        layernorm(g, g.xf, g.xb, g.hb, g.st, n, 5)
        store_fm(g, g.xf, "yT", c0, n)


def _chunked(W, kt, mc):
    Kd, Md = W.shape
    nch = Md // mc
    return np.ascontiguousarray(W.reshape(kt, 128, nch, mc).transpose(2, 1, 0, 3).reshape(nch, 128, kt * mc))


def _sm(v):
    return np.ascontiguousarray(np.asarray(v, np.float32).reshape(16, 128).T)


def prep_shared(inp, TP, PAST, NB):
    f = lambda a: np.asarray(a, np.float32)
    P = {}
    for l in range(2):
        for fi, nm in enumerate(["ffn1", "ffn2"]):
            w1 = f(inp[nm + "_w1"][l]).reshape(D, 22, 256)
            w3 = f(inp[nm + "_w3"][l]).reshape(D, 22, 256)
            P[f"w13_{l}{fi}"] = _chunked(np.concatenate([w1, w3], axis=2).reshape(D, 22 * 512), KT, 512)
            P[f"w2_{l}{fi}"] = _chunked(f(inp[nm + "_w2"][l]), FT, 128)
    P["lng"] = np.ascontiguousarray(f(inp["ln_g"]).reshape(6, KT, 128).transpose(2, 0, 1).reshape(128, 96))
    P["lnb"] = np.ascontiguousarray(f(inp["ln_b"]).reshape(6, KT, 128).transpose(2, 0, 1).reshape(128, 96))
    P["w_in_e"] = _chunked(f(inp["w_in_e"][0]), KT, 512)
    P["w_out_e"] = _chunked(f(inp["w_out_e"][0]), KT, 512)
    pw = f(inp["pool_w"][0])
    P["pool_w"] = np.ascontiguousarray(pw.reshape(4, 3, 128, 384).transpose(2, 0, 1, 3).reshape(1, 128, 12 * 384))
    P["pool_scale"] = np.ascontiguousarray(f(inp["pool_scale"][0]).reshape(12, 128).T)
    bre, bim = f(inp["ssm_b_re"][0]), f(inp["ssm_b_im"][0])
    cre, cim = f(inp["ssm_c_re"][0]), f(inp["ssm_c_im"][0])
    Bre = np.zeros((128, 16, 128), np.float32); Bim = np.zeros_like(Bre)
    Cre = np.zeros((128, 16, 128), np.float32); Cim = np.zeros_like(Cre)
    for gg in range(32):
        i = gg // 2
        r0 = (gg % 8) * 16
        c0 = (gg % 2) * 64
        Bre[r0:r0 + 16, i, c0:c0 + 64] = bre[gg].T
        Bim[r0:r0 + 16, i, c0:c0 + 64] = bim[gg].T
        Cre[c0:c0 + 64, i, r0:r0 + 16] = cre[gg].T
        Cim[c0:c0 + 64, i, r0:r0 + 16] = cim[gg].T
    P["ssm_Bre"], P["ssm_Bim"] = Bre.reshape(128, -1), Bim.reshape(128, -1)
    P["ssm_Cre"], P["ssm_Cim"] = Cre.reshape(128, -1), Cim.reshape(128, -1)
    P["ssm_lre"] = _sm(f(inp["ssm_lam_re"][0]).reshape(-1))
    P["ssm_lim"] = _sm(f(inp["ssm_lam_im"][0]).reshape(-1))
    P["ssm_ldt"] = _sm(np.repeat(f(inp["ssm_log_dt"][0]), 64))
    P["ssm_d"] = np.ascontiguousarray(f(inp["ssm_d"][0]).reshape(4, 128).T)
    P["w_glu"] = _chunked(f(inp["ssm_w_glu"][0]), 4, 512)
    P["b_glu"] = np.ascontiguousarray(f(inp["ssm_b_glu"][0]).reshape(4, 128).T)
    wo = f(inp["w_in_o"][0])
    P["w_in_o_cq"] = _chunked(wo[:, 0:512], KT, 512)
    kpe = wo[:, 768:800]
    sw = kpe[:, (np.arange(32) + 16) % 32]
    P["w_in_o_kv"] = _chunked(np.concatenate([wo[:, 512:768], kpe, sw], axis=1), KT, 320)
    P["w_in_o_u"] = _chunked(wo[:, 800:1824], KT, 512)
    P["w_in_o_v"] = _chunked(wo[:, 1824:2848], KT, 512)
    P["g_q"] = np.ascontiguousarray(f(inp["mla_g_q"][0]).reshape(4, 128).T)
    P["g_kv"] = np.ascontiguousarray(f(inp["mla_g_kv"][0]).reshape(2, 128).T)
    uq = f(inp["mla_w_uq"][0])
    uqs = uq[:, :, 64 + (np.arange(32) + 16) % 32]
    P["w_uq"] = _chunked(np.concatenate([uq, uqs], axis=2).reshape(512, 16 * 128), 4, 1024)
    P["w_uk"] = _chunked(f(inp["mla_w_uk"][0]).reshape(256, 1024), 2, 1024)
    P["w_uv"] = _chunked(f(inp["mla_w_uv"][0]).reshape(256, 1024), 2, 1024)
    P["sg_gv"] = np.ascontiguousarray(np.broadcast_to(f(inp["sg_g_v"][0])[None, :], (128, 1024)))
    P["sg_bv"] = np.ascontiguousarray(np.broadcast_to(f(inp["sg_b_v"][0])[None, :], (128, 1024)))
    ws = f(inp["sg_w_s"][0])
    P["sg_wT"] = np.ascontiguousarray(ws.transpose(2, 0, 1).reshape(128, 8 * 128))
    w64 = ws[:, 0:64, 0:64].transpose(2, 0, 1)
    P["sg_wT64"] = np.ascontiguousarray(np.concatenate([w64, w64], axis=0).reshape(128, 8 * 64))
    bs = f(inp["sg_b_s"][0])
    P["sg_bs"] = np.ascontiguousarray(np.broadcast_to(bs.reshape(1, 8 * 128), (128, 8 * 128)))
    bs64 = np.concatenate([bs[:, 0:64], bs[:, 0:64]], axis=1)
    P["sg_bs64"] = np.ascontiguousarray(np.broadcast_to(bs64.reshape(1, 8 * 128), (128, 8 * 128)))
    ss, tt = np.meshgrid(np.arange(128), np.arange(128), indexing="ij")
    P["tril"] = (ss <= tt).astype(np.float32)
    nd = NB // 128
    t_ = np.arange(128)[:, None, None]
    d_ = np.arange(nd)[None, :, None]
    q_ = np.arange(NB)[None, None, :]
    P["dmask"] = (((d_ * 128 + t_) // 64) <= (q_ // 64)).astype(np.float32).reshape(128, nd * NB).astype(ml_dtypes.bfloat16)
    woo = f(inp["w_out_o"][0])
    P["w_out_o_a"] = np.ascontiguousarray(woo[0:1024].reshape(16, 64, 4, 512).transpose(2, 1, 0, 3).reshape(4, 64, 16 * 512))
    P["w_out_o_s"] = _chunked(woo[1024:2048], 8, 512)
    return P


def prep_core(inp, P, cid, TP, PAST, NB):
    f = lambda a: np.asarray(a, np.float32)
    b, r = cid // 4, cid % 4
    NT = TP + NSEQ * SL
    m = dict(P)
    xp = f(inp["x_prompt"][b, r * TP:(r + 1) * TP])
    xs = f(inp["x_sample"][NSEQ * cid:NSEQ * (cid + 1)]).reshape(NSEQ * SL, D)
    m["xT"] = np.ascontiguousarray(np.concatenate([xp, xs], axis=0).T)
    corr = np.ones((128, 12, 16), np.float32)
    if r == 0:
        for t12 in range(12):
            w = 2 ** (t12 // 3 + 1)
            corr[:, t12, :] = w / np.minimum(np.arange(16) + 1, w)
    m["pool_corr"] = corr.reshape(128, -1)
    m["cache_poolT"] = np.ascontiguousarray(f(inp["cache_pool"][0, NSEQ * cid:NSEQ * (cid + 1)]).transpose(2, 0, 1))
    hre = f(inp["state_ssm_re"][0, NSEQ * cid:NSEQ * (cid + 1)]).reshape(NSEQ, 16, 128)
    him = f(inp["state_ssm_im"][0, NSEQ * cid:NSEQ * (cid + 1)]).reshape(NSEQ, 16, 128)
    m["ssm_h0re"] = np.ascontiguousarray(hre.transpose(2, 1, 0).reshape(128, 16 * NSEQ))
    m["ssm_h0im"] = np.ascontiguousarray(him.transpose(2, 1, 0).reshape(128, 16 * NSEQ))
    meta = np.zeros((128, 32), np.float32)
    for j in range(4):
        meta[:, j] = 1.0 if j == r - 1 else 0.0
        for d in range(1, 4):
            meta[:, 4 + (d - 1) * 4 + j] = 1.0 if j == r - d else 0.0
        meta[:, 16 + j] = 0.0 if j < r else NEGB
    m["meta"] = meta
    pos = np.concatenate([r * TP + np.arange(TP), np.tile(PAST + np.arange(SL), NSEQ)]).astype(np.float32)
    inv = (10000.0 ** (-np.arange(16, dtype=np.float32) / 16)).astype(np.float32)
    ang = pos[None, :] * np.concatenate([inv, inv])[:, None]
    rope = np.zeros((128, 2, NT), np.float32)
    rope[64:96, 0] = np.cos(ang)
    sn = np.sin(ang)
    sn[0:16] *= -1.0
    rope[64:96, 1] = sn
    m["rope"] = rope
    m["cache_ckvT"] = np.ascontiguousarray(f(inp["cache_ckv"][0, NSEQ * cid:NSEQ * (cid + 1)]).transpose(0, 2, 1))
    m["cache_kpeT"] = np.ascontiguousarray(f(inp["cache_kpe"][0, NSEQ * cid:NSEQ * (cid + 1)]).transpose(0, 2, 1))
    return m


def run(inp, TP, PAST, NB, phases=(1, 2, 3, 4), noffn=False):
    nc = build(TP, PAST, NB, phases, noffn)
    P = prep_shared(inp, TP, PAST, NB)
    if noffn:
        P = {k: v for k, v in P.items() if not (k.startswith("w13_") or k.startswith("w2_"))}
    maps = [prep_core(inp, P, cid, TP, PAST, NB) for cid in range(8)]
    res = run_bass_kernel_spmd(nc, maps, core_ids=list(range(8)))
    return res.results


def assemble(R, TP, PAST):
    NT = TP + NSEQ * SL
    S = 4 * TP
    yp = np.zeros((2, S, D), np.float32); ys = np.zeros((32, SL, D), np.float32)
    ckp = np.zeros((1, 2, S, 256), np.float32); kpp = np.zeros((1, 2, S, 32), np.float32)
    cks = np.zeros((1, 32, SL, 256), np.float32); kps = np.zeros((1, 32, SL, 32), np.float32)
    pp = np.zeros((1, 2, 15, 1536), np.float32); ps = np.zeros((1, 32, 15, 1536), np.float32)
    srp = np.zeros((1, 2, 32, 64), np.float32); sip = np.zeros_like(srp)
    srs = np.zeros((1, 32, 32, 64), np.float32); sis = np.zeros_like(srs)
    sgv = np.zeros((1, 32, SL, 1024), np.float32)
    for cid in range(8):
        b, r = cid // 4, cid % 4
        o = R[cid]
        yT = o["yT"]
        yp[b, r * TP:(r + 1) * TP] = yT[:, :TP].T
        ys[NSEQ * cid:NSEQ * (cid + 1)] = yT[:, TP:].T.reshape(NSEQ, SL, D)
        ckp[0, b, r * TP:(r + 1) * TP] = o["ckvT"][:, :TP].T
        kpp[0, b, r * TP:(r + 1) * TP] = o["kpeT"][:, :TP].T
        cks[0, NSEQ * cid:NSEQ * (cid + 1)] = o["ckvT"][:, TP:].T.reshape(NSEQ, SL, 256)
        kps[0, NSEQ * cid:NSEQ * (cid + 1)] = o["kpeT"][:, TP:].T.reshape(NSEQ, SL, 32)
        ps[0, NSEQ * cid:NSEQ * (cid + 1)] = o["poolT_s"].transpose(1, 2, 0)
        ss = o["ssm_s"].reshape(128, 2, 16, NSEQ)
        srs[0, NSEQ * cid:NSEQ * (cid + 1)] = ss[:, 0].transpose(2, 1, 0).reshape(NSEQ, 32, 64)
        sis[0, NSEQ * cid:NSEQ * (cid + 1)] = ss[:, 1].transpose(2, 1, 0).reshape(NSEQ, 32, 64)
        sgv[0, NSEQ * cid:NSEQ * (cid + 1)] = o["sgv"].reshape(NSEQ, SL, 1024)
        if r == 3:
            pp[0, b] = o["poolT_p"].T
            sp_ = o["ssm_p"]
            srp[0, b] = sp_[:, 0].T.reshape(32, 64)
            sip[0, b] = sp_[:, 1].T.reshape(32, 64)
    return (yp, ys, pp, srp, sip, ckp, kpp, ps, srs, sis, cks, kps, sgv)


def kernel(**inputs):
    R = run(inputs, 4096, 4096, 256)
    return assemble(R, 4096, 4096)
```
